# Optimizing a Trainium2 kernel written in Bass

```python
import math
import jax
import jax.numpy as jnp
from jax import lax
import numpy as np

D_MODEL = 1024
BATCH = 2
SEQ = 8192
DEPTH = 2

HEAD_DIM = 64
Q_BLOCK = 128
EPS = 1e-6
NEG = -1e30

NUM_BUCKETS = 32
MAX_DISTANCE = 128
BIAS_HEADS = 16

NSA_HEADS = 8
NSA_KV_HEADS = 2
NSA_GROUP = NSA_HEADS // NSA_KV_HEADS
CMP_BLOCK = 32
CMP_STRIDE = 16
CMP_HIDDEN = 128
SEL_BLOCK = 64
SEL_TOPN = 16
WINDOW = 512
FORCE_BONUS = 1e3

DIFF_HEADS = 4
DIFF_VDIM = 2 * HEAD_DIM

MOBA_HEADS = 16
MOBA_BLOCK = 256
MOBA_TOPK = 3
MOBA_QCHUNK = 32

D_FF = 4 * D_MODEL

NSA_Q = NSA_HEADS * HEAD_DIM
NSA_KV = NSA_KV_HEADS * HEAD_DIM
NSA_GATES = 3 * NSA_HEADS
DIFF_QK = DIFF_HEADS * 2 * HEAD_DIM
DIFF_V = DIFF_HEADS * DIFF_VDIM
D_MIX = NSA_Q + DIFF_V
EVEN_SIZES = (NSA_Q,) + (NSA_KV,) * 6 + (NSA_GATES, DIFF_QK, DIFF_QK, DIFF_V)
EVEN_PROJ = sum(EVEN_SIZES)
ODD_PROJ = 3 * MOBA_HEADS * HEAD_DIM
ODD_MIX = MOBA_HEADS * HEAD_DIM

kernel_name = 'hybrid_nsa_diff_moba_trunk'


def rms_norm(x, g):
    xf = x.astype(jnp.float32)
    y = xf * lax.rsqrt(jnp.mean(xf * xf, axis=-1, keepdims=True) + EPS)
    return (y * g.astype(jnp.float32)).astype(x.dtype)


def t5_bucket(rel):
    n = jnp.maximum(rel, 0)
    max_exact = NUM_BUCKETS // 2
    large = max_exact + (jnp.log(jnp.maximum(n, 1).astype(jnp.float32) / max_exact)
                         / math.log(MAX_DISTANCE / max_exact)
                         * (NUM_BUCKETS - max_exact)).astype(jnp.int32)
    large = jnp.minimum(large, NUM_BUCKETS - 1)
    return jnp.where(n < max_exact, n, large)


def masked_softmax(logits, mask):
    l = jnp.where(mask, logits.astype(jnp.float32), NEG)
    m = jnp.max(l, axis=-1, keepdims=True)
    p = jnp.where(mask, jnp.exp(l - m), 0.0)
    return p / jnp.maximum(jnp.sum(p, axis=-1, keepdims=True), 1e-30)


def to_heads(t, n):
    b, s, _ = t.shape
    return t.reshape(b, s, n, -1).transpose(0, 2, 1, 3)


def compress_kv(kv, pos, w1, w2):
    b, g, s, d = kv.shape
    nc = (s - CMP_BLOCK) // CMP_STRIDE + 1
    tok = jnp.arange(nc)[:, None] * CMP_STRIDE + jnp.arange(CMP_BLOCK)[None, :]
    blk = kv[:, :, tok] + pos
    hid = jax.nn.gelu(blk.reshape(b, g, nc, CMP_BLOCK * d) @ w1)
    return hid @ w2


def nsa_attention(q, kc, vc, ks, vs, kw, vw, gates, tbl_h):
    b, g, r, s, d = q.shape
    nc = kc.shape[2]
    ns = s // SEL_BLOCK
    n_sel = min(SEL_TOPN, ns)
    scale = d ** -0.5
    tbl_gr = tbl_h.reshape(g, r, NUM_BUCKETS)
    cmp_end = jnp.arange(nc) * CMP_STRIDE + CMP_BLOCK - 1
    tok = jnp.arange(nc)[:, None] * CMP_STRIDE + jnp.arange(CMP_BLOCK)[None, :]
    overlap = jax.nn.one_hot(tok // SEL_BLOCK, ns, dtype=jnp.float32).sum(axis=1) / CMP_BLOCK
    ks_blk = ks.reshape(b, g, ns, SEL_BLOCK, d)
    vs_blk = vs.reshape(b, g, ns, SEL_BLOCK, d)
    kw_pad = jnp.pad(kw, ((0, 0), (0, 0), (WINDOW, 0), (0, 0)))
    vw_pad = jnp.pad(vw, ((0, 0), (0, 0), (WINDOW, 0), (0, 0)))
    span = WINDOW + Q_BLOCK
    bi = jnp.arange(b)[:, None, None, None]
    gi = jnp.arange(g)[None, :, None, None]
    gi6 = jnp.arange(g)[None, :, None, None, None, None]
    ri6 = jnp.arange(r)[None, None, :, None, None, None]
    blk_ids = jnp.arange(ns)

    def block(j):
        q0 = j * Q_BLOCK
        t = q0 + jnp.arange(Q_BLOCK)
        qb = lax.dynamic_slice_in_dim(q, q0, Q_BLOCK, axis=3)
        gb = lax.dynamic_slice_in_dim(gates, q0, Q_BLOCK, axis=4)
        rel_c = t[:, None] - cmp_end[None, :]
        lc = jnp.einsum('bgrqd,bgcd->bgrqc', qb, kc) * scale + tbl_gr[:, :, t5_bucket(rel_c)]
        pc = masked_softmax(lc, rel_c >= 0)
        oc = jnp.einsum('bgrqc,bgcd->bgrqd', pc.astype(vc.dtype), vc)
        imp = jnp.einsum('bgrqc,cn->bgqn', pc, overlap)
        cur = t // SEL_BLOCK
        eligible = blk_ids[None, :] <= cur[:, None]
        forced = ((blk_ids[None, :] == 0) | (blk_ids[None, :] == cur[:, None])
                  | (blk_ids[None, :] == cur[:, None] - 1))
        score = jnp.where(eligible, imp + jnp.where(forced, FORCE_BONUS, 0.0), -1.0)
        _, sel = lax.top_k(score, n_sel)
        k_sel = ks_blk[bi, gi, sel]
        v_sel = vs_blk[bi, gi, sel]
        pos_s = sel[..., None] * SEL_BLOCK + jnp.arange(SEL_BLOCK)
        rel_s = t[None, None, :, None, None] - pos_s
        bias_s = tbl_gr[gi6, ri6, t5_bucket(rel_s)[:, :, None]]
        ls = jnp.einsum('bgrqd,bgqnkd->bgrqnk', qb, k_sel) * scale + bias_s
        ms = jnp.broadcast_to((rel_s >= 0)[:, :, None], ls.shape)
        ps = masked_softmax(ls.reshape(b, g, r, Q_BLOCK, -1),
                            ms.reshape(b, g, r, Q_BLOCK, -1)).reshape(ls.shape)
        o_s = jnp.einsum('bgrqnk,bgqnkd->bgrqd', ps.astype(v_sel.dtype), v_sel)
        kwb = lax.dynamic_slice_in_dim(kw_pad, q0, span, axis=2)
        vwb = lax.dynamic_slice_in_dim(vw_pad, q0, span, axis=2)
        pos_w = q0 - WINDOW + jnp.arange(span)
        rel_w = t[:, None] - pos_w[None, :]
        mask_w = (rel_w >= 0) & (rel_w < WINDOW) & (pos_w[None, :] >= 0)
        lw = jnp.einsum('bgrqd,bgkd->bgrqk', qb, kwb) * scale + tbl_gr[:, :, t5_bucket(rel_w)]
        pw = masked_softmax(lw, mask_w)
        ow = jnp.einsum('bgrqk,bgkd->bgrqd', pw.astype(vwb.dtype), vwb)
        return (gb[:, 0][..., None] * oc + gb[:, 1][..., None] * o_s
                + gb[:, 2][..., None] * ow)

    out = lax.map(block, jnp.arange(s // Q_BLOCK))
    return out.transpose(1, 0, 4, 2, 3, 5).reshape(b, s, g * r * d)


def diff_attention(q, k, v, lam, lam_init, subln_g, tbl_h):
    b, s = q.shape[:2]
    q = q.transpose(0, 3, 2, 1, 4)
    k = k.transpose(0, 3, 2, 1, 4)
    v = v.transpose(0, 2, 1, 3)
    kpos = jnp.arange(s)
    scale = HEAD_DIM ** -0.5

    def block(j):
        q0 = j * Q_BLOCK
        t = q0 + jnp.arange(Q_BLOCK)
        qb = lax.dynamic_slice_in_dim(q, q0, Q_BLOCK, axis=3)
        rel = t[:, None] - kpos[None, :]
        logits = jnp.einsum('bmhqd,bmhkd->bmhqk', qb, k) * scale + tbl_h[:, t5_bucket(rel)]
        p = masked_softmax(logits, rel >= 0)
        a = p[:, 0] - lam * p[:, 1]
        return jnp.einsum('bhqk,bhkv->bhqv', a.astype(v.dtype), v)

    out = lax.map(block, jnp.arange(s // Q_BLOCK))
    out = out.transpose(1, 0, 3, 2, 4).reshape(b, s, DIFF_HEADS, DIFF_VDIM)
    out = rms_norm(out, subln_g) * (1.0 - lam_init)
    return out.reshape(b, s, DIFF_HEADS * DIFF_VDIM)


def moba_attention(q, k, v, tbl_h):
    b, h, s, d = q.shape
    nb = -(-s // MOBA_BLOCK)
    pad = nb * MOBA_BLOCK - s
    k_pad = jnp.pad(k, ((0, 0), (0, 0), (0, pad), (0, 0)))
    v_pad = jnp.pad(v, ((0, 0), (0, 0), (0, pad), (0, 0)))
    kb = k_pad.reshape(b, h, nb, MOBA_BLOCK, d)
    vb = v_pad.reshape(b, h, nb, MOBA_BLOCK, d)
    k_mean = jnp.mean(kb.astype(jnp.float32), axis=3).astype(k.dtype)
    n_sel = min(MOBA_TOPK, nb)
    scale = d ** -0.5
    bi = jnp.arange(b)[:, None, None, None]
    hi = jnp.arange(h)[None, :, None, None]
    hi5 = jnp.arange(h)[None, :, None, None, None]
    blk_ids = jnp.arange(nb)

    def block(j):
        q0 = j * MOBA_QCHUNK
        t = q0 + jnp.arange(MOBA_QCHUNK)
        c = q0 // MOBA_BLOCK
        qb = lax.dynamic_slice_in_dim(q, q0, MOBA_QCHUNK, axis=2)
        gate = jnp.einsum('bhqd,bhnd->bhqn', qb, k_mean).astype(jnp.float32)
        past = blk_ids < c
        _, sel = lax.top_k(jnp.where(past, gate, NEG), n_sel)
        valid = sel < c
        k_sel = kb[bi, hi, sel]
        v_sel = vb[bi, hi, sel]
        pos_s = sel[..., None] * MOBA_BLOCK + jnp.arange(MOBA_BLOCK)
        bias_s = tbl_h[hi5, t5_bucket(t[None, None, :, None, None] - pos_s)]
        ls = jnp.einsum('bhqd,bhqnkd->bhqnk', qb, k_sel) * scale + bias_s
        ms = jnp.broadcast_to(valid[..., None], ls.shape)
        k_own = lax.dynamic_slice_in_dim(k_pad, c * MOBA_BLOCK, MOBA_BLOCK, axis=2)
        v_own = lax.dynamic_slice_in_dim(v_pad, c * MOBA_BLOCK, MOBA_BLOCK, axis=2)
        rel_o = t[:, None] - (c * MOBA_BLOCK + jnp.arange(MOBA_BLOCK))[None, :]
        lo = jnp.einsum('bhqd,bhkd->bhqk', qb, k_own) * scale + tbl_h[:, t5_bucket(rel_o)]
        mo = jnp.broadcast_to(rel_o >= 0, lo.shape)
        n_g = n_sel * MOBA_BLOCK
        logits = jnp.concatenate([ls.reshape(b, h, MOBA_QCHUNK, n_g), lo], axis=-1)
        mask = jnp.concatenate([ms.reshape(b, h, MOBA_QCHUNK, n_g), mo], axis=-1)
        p = masked_softmax(logits, mask)
        ps = p[..., :n_g].reshape(ls.shape).astype(v.dtype)
        po = p[..., n_g:].astype(v.dtype)
        return (jnp.einsum('bhqnk,bhqnkd->bhqd', ps, v_sel)
                + jnp.einsum('bhqk,bhkd->bhqd', po, v_own))

    out = lax.map(block, jnp.arange(s // MOBA_QCHUNK))
    return out.transpose(1, 0, 3, 2, 4).reshape(b, s, h * d)


def even_mixer(hn, w_in, w_out, pos_k, w1_k, w2_k, pos_v, w1_v, w2_v,
               lam_q1, lam_k1, lam_q2, lam_k2, subln_g, bias_table, lam_init):
    b, s, _ = hn.shape
    parts = jnp.split(hn @ w_in, np.cumsum(EVEN_SIZES)[:-1].tolist(), axis=-1)
    q, kc, vc, ks, vs, kw, vw, g, dq, dk, dv = parts
    qn = to_heads(q, NSA_HEADS).reshape(b, NSA_KV_HEADS, NSA_GROUP, s, HEAD_DIM)
    kc = compress_kv(to_heads(kc, NSA_KV_HEADS), pos_k, w1_k, w2_k)
    vc = compress_kv(to_heads(vc, NSA_KV_HEADS), pos_v, w1_v, w2_v)
    gates = jax.nn.sigmoid(g.astype(jnp.float32)).astype(hn.dtype)
    gates = gates.reshape(b, s, 3, NSA_KV_HEADS, NSA_GROUP).transpose(0, 2, 3, 4, 1)
    o_a = nsa_attention(qn, kc, vc, to_heads(ks, NSA_KV_HEADS), to_heads(vs, NSA_KV_HEADS),
                        to_heads(kw, NSA_KV_HEADS), to_heads(vw, NSA_KV_HEADS), gates,
                        bias_table[:, :NSA_HEADS].T)
    lam = (jnp.exp(jnp.sum((lam_q1 * lam_k1).astype(jnp.float32)))
           - jnp.exp(jnp.sum((lam_q2 * lam_k2).astype(jnp.float32))) + lam_init)
    o_b = diff_attention(dq.reshape(b, s, DIFF_HEADS, 2, HEAD_DIM),
                         dk.reshape(b, s, DIFF_HEADS, 2, HEAD_DIM),
                         dv.reshape(b, s, DIFF_HEADS, DIFF_VDIM), lam, lam_init, subln_g,
                         bias_table[:, NSA_HEADS:NSA_HEADS + DIFF_HEADS].T)
    return jnp.concatenate([o_a, o_b], axis=-1) @ w_out


def odd_mixer(hn, w_in, w_out, bias_table):
    q, k, v = jnp.split(hn @ w_in, 3, axis=-1)
    o = moba_attention(to_heads(q, MOBA_HEADS), to_heads(k, MOBA_HEADS),
                       to_heads(v, MOBA_HEADS), bias_table[:, :MOBA_HEADS].T)
    return o @ w_out


def sqrelu_mlp(hn, w1, w2):
    return jnp.square(jax.nn.relu(hn @ w1)) @ w2


def setup_inputs(seed: int = 0) -> dict:
    key = jax.random.key(seed)
    k = jax.random.split(key, 22)
    ne = (DEPTH + 1) // 2
    no = DEPTH // 2
    f32 = jnp.float32

    def nrm(i, shape, scale):
        return jax.random.normal(k[i], shape, f32) * scale

    return {
        'x': nrm(0, (BATCH, SEQ, D_MODEL), 1.0),
        'bias_table': nrm(1, (NUM_BUCKETS, BIAS_HEADS), 0.5),
        'norm_mix': 1.0 + nrm(2, (DEPTH, D_MODEL), 0.1),
        'norm_mlp': 1.0 + nrm(3, (DEPTH, D_MODEL), 0.1),
        'norm_final': 1.0 + nrm(4, (D_MODEL,), 0.1),
        'mlp_w1': nrm(5, (DEPTH, D_MODEL, D_FF), D_MODEL ** -0.5),
        'mlp_w2': nrm(6, (DEPTH, D_FF, D_MODEL), D_FF ** -0.5),
        'ev_w_in': nrm(7, (ne, D_MODEL, EVEN_PROJ), D_MODEL ** -0.5),
        'ev_w_out': nrm(8, (ne, D_MIX, D_MODEL), D_MIX ** -0.5),
        'ev_cmp_pos_k': nrm(9, (ne, CMP_BLOCK, HEAD_DIM), 0.1),
        'ev_cmp_w1_k': nrm(10, (ne, CMP_BLOCK * HEAD_DIM, CMP_HIDDEN), (CMP_BLOCK * HEAD_DIM) ** -0.5),
        'ev_cmp_w2_k': nrm(11, (ne, CMP_HIDDEN, HEAD_DIM), CMP_HIDDEN ** -0.5),
        'ev_cmp_pos_v': nrm(12, (ne, CMP_BLOCK, HEAD_DIM), 0.1),
        'ev_cmp_w1_v': nrm(13, (ne, CMP_BLOCK * HEAD_DIM, CMP_HIDDEN), (CMP_BLOCK * HEAD_DIM) ** -0.5),
        'ev_cmp_w2_v': nrm(14, (ne, CMP_HIDDEN, HEAD_DIM), CMP_HIDDEN ** -0.5),
        'ev_lam_q1': nrm(15, (ne, HEAD_DIM), 0.1),
        'ev_lam_k1': nrm(16, (ne, HEAD_DIM), 0.1),
        'ev_lam_q2': nrm(17, (ne, HEAD_DIM), 0.1),
        'ev_lam_k2': nrm(18, (ne, HEAD_DIM), 0.1),
        'ev_subln': 1.0 + nrm(19, (ne, DIFF_VDIM), 0.1),
        'od_w_in': nrm(20, (no, D_MODEL, ODD_PROJ), D_MODEL ** -0.5),
        'od_w_out': nrm(21, (no, ODD_MIX, D_MODEL), ODD_MIX ** -0.5),
    }


def reference(x, bias_table, norm_mix, norm_mlp, norm_final, mlp_w1, mlp_w2,
              ev_w_in, ev_w_out, ev_cmp_pos_k, ev_cmp_w1_k, ev_cmp_w2_k,
              ev_cmp_pos_v, ev_cmp_w1_v, ev_cmp_w2_v, ev_lam_q1, ev_lam_k1,
              ev_lam_q2, ev_lam_k2, ev_subln, od_w_in, od_w_out):
    for i in range(DEPTH):
        hn = rms_norm(x, norm_mix[i])
        if i % 2 == 0:
            e = i // 2
            lam_init = 0.8 - 0.6 * math.exp(-0.3 * i)
            mix = even_mixer(hn, ev_w_in[e], ev_w_out[e], ev_cmp_pos_k[e], ev_cmp_w1_k[e],
                             ev_cmp_w2_k[e], ev_cmp_pos_v[e], ev_cmp_w1_v[e], ev_cmp_w2_v[e],
                             ev_lam_q1[e], ev_lam_k1[e], ev_lam_q2[e], ev_lam_k2[e],
                             ev_subln[e], bias_table, lam_init)
        else:
            o = i // 2
            mix = odd_mixer(hn, od_w_in[o], od_w_out[o], bias_table)
        x = x + mix
        x = x + sqrelu_mlp(rms_norm(x, norm_mlp[i]), mlp_w1[i], mlp_w2[i])
    return rms_norm(x, norm_final)
```

```python
import numpy as np
import ml_dtypes
from contextlib import ExitStack
import concourse.bass as bass
import concourse.mybir as mybir
from concourse.bass_utils import run_bass_kernel_spmd

F32 = mybir.dt.float32
BF16 = mybir.dt.bfloat16
AF = mybir.ActivationFunctionType
ALU = mybir.AluOpType
AX = mybir.AxisListType
NPBF = ml_dtypes.bfloat16


class Buf:
    __slots__ = ("name", "t", "lw", "rd", "excl")

    def __init__(self, name, t, excl=False):
        self.name = name
        self.t = t
        self.excl = excl
        self.lw = None
        self.rd = {}

    def __getitem__(self, idx):
        return self.t[idx]


class Sched:
    ENGS = ("pe", "act", "dve", "pool", "sp")
    EPOCH = 12000
    NDMA = 24

    def __init__(self, nc, es):
        self.nc = nc
        self.es = es
        self.ops = {e: [] for e in self.ENGS}
        self.n = {e: 0 for e in self.ENGS}
        self.waited = {e: {} for e in self.ENGS}
        self.signal = {e: set() for e in self.ENGS}
        self.dma_uses = [0] * self.NDMA
        self.dma_i = 0
        self.nbuf = 0

    def sbuf(self, name, shape, dtype):
        t = self.es.enter_context(self.nc.sbuf_tensor("sb_" + name, list(shape), dtype))
        return t

    def psum(self, name, shape, dtype):
        t = self.es.enter_context(self.nc.psum_tensor("ps_" + name, list(shape), dtype))
        return t

    def buf(self, t, name=None, excl=False):
        self.nbuf += 1
        return Buf(name or f"b{self.nbuf}", t, excl)

    def pbuf(self, name, shape, dtype):
        return self.buf(self.psum(name, shape, dtype), name, excl=True)

    def _need(self, eng, tok, same_raw=False):
        if tok is None:
            return
        stream, v = tok
        if stream == eng:
            if not same_raw:
                return
            if self.n[eng] - v > 3:
                return
        if self.waited[eng].get(stream, -1) >= v:
            return
        self.waited[eng][stream] = v
        if isinstance(stream, str):
            self.signal[stream].add(v)
        self.ops[eng].append(("w", tok))

    def _deps(self, eng, reads, writes):
        for b in reads:
            self._need(eng, b.lw, same_raw=True)
            if b.excl:
                for tok in b.rd.values():
                    self._need(eng, tok, same_raw=False)
        for b in writes:
            self._need(eng, b.lw, same_raw=False)
            for tok in b.rd.values():
                self._need(eng, tok, same_raw=False)

    def _commit(self, tok, reads, writes):
        stream = tok[0]
        for b in reads:
            b.rd[stream] = tok
        for b in writes:
            b.lw = tok
            b.rd = {}

    def op(self, eng, fn, reads=(), writes=()):
        self._deps(eng, reads, writes)
        idx = self.n[eng]
        self.n[eng] += 1
        self.ops[eng].append(("i", fn, idx))
        self._commit((eng, idx), reads, writes)

    def dma(self, eng, out, in_, reads=(), writes=(), **kw):
        k = self.dma_i % self.NDMA
        self.dma_i += 1
        stream = ("dma", k)
        prev = self.dma_uses[k]
        self._deps(eng, reads, writes)
        if prev > 0:
            self._need(eng, (stream, prev * 16))
        self.dma_uses[k] = prev + 1
        val = (prev + 1) * 16
        self.ops[eng].append(("d", (out, in_, kw), k, val))
        self._commit((stream, val), reads, writes)

    def finish_outputs(self, bufs, eng="sp"):
        for b in bufs:
            self._need(eng, b.lw)

    def emit(self):
        nc = self.nc
        engobj = {"pe": nc.tensor, "act": nc.scalar, "dve": nc.vector,
                  "pool": nc.gpsimd, "sp": nc.sync}
        sigval = {}
        sems = {}
        for e in self.ENGS:
            cnt = 0
            for idx in sorted(self.signal[e]):
                ep = cnt // self.EPOCH
                sigval[(e, idx)] = ((e, ep), cnt % self.EPOCH + 1)
                cnt += 1
                if (e, ep) not in sems:
                    sems[(e, ep)] = self.es.enter_context(nc.semaphore(f"s_{e}_{ep}"))
        for k in range(self.NDMA):
            if self.dma_uses[k]:
                sems[("dma", k)] = self.es.enter_context(nc.semaphore(f"s_dma_{k}"))
        self.nsems = len(sems)
        block = self.es.enter_context(nc.Block())
        deco = {"pe": block.tensor, "act": block.scalar, "dve": block.vector,
                "pool": block.gpsimd, "sp": block.sync}

        def make(e):
            ops = self.ops[e]

            def body(eng):
                for o in ops:
                    if o[0] == "w":
                        stream, v = o[1]
                        if isinstance(stream, str):
                            sk, sv = sigval[(stream, v)]
                            eng.wait_ge(sems[sk], sv)
                        else:
                            eng.wait_ge(sems[stream], v)
                    elif o[0] == "i":
                        ins = o[1](eng)
                        sv = sigval.get((e, o[2]))
                        if sv is not None:
                            ins.then_inc(sems[sv[0]], 1)
                    else:
                        out, in_, kw = o[1]
                        eng.dma_start(out=out, in_=in_, **kw).then_inc(sems[("dma", o[2])], 16)
            return body

        for e in self.ENGS:
            if self.ops[e]:
                deco[e](make(e))


STAGE = 99

NT = 2048
D = 1024
EPS = 1e-6


class Ring:
    def __init__(self, items):
        self.items = list(items)
        self.i = 0

    def next(self):
        b = self.items[self.i % len(self.items)]
        self.i += 1
        return b


def build_token_phase(attn_in, mlp, proj_cols, final, x_out, gate_rows=None):
    nc = bass.Bass("TRN2", target_bir_lowering=False)
    dr = {}
    dr["x"] = nc.dram_tensor("x", [NT, D], F32, kind="ExternalInput").ap()
    dr["ident"] = nc.dram_tensor("ident", [128, 128], BF16, kind="ExternalInput").ap()
    if attn_in:
        dr["oT"] = nc.dram_tensor("oT", [D, NT], BF16, kind="ExternalInput").ap()
        dr["w_out"] = nc.dram_tensor("w_out", [D, D], F32, kind="ExternalInput").ap()
    if mlp:
        dr["g_mlp"] = nc.dram_tensor("g_mlp", [1, D], F32, kind="ExternalInput").ap()
        dr["w1"] = nc.dram_tensor("w1", [D, 4 * D], F32, kind="ExternalInput").ap()
        dr["w2"] = nc.dram_tensor("w2", [4 * D, D], F32, kind="ExternalInput").ap()
    if proj_cols:
        dr["g_mix"] = nc.dram_tensor("g_mix", [1, D], F32, kind="ExternalInput").ap()
        dr["w_in"] = nc.dram_tensor("w_in", [D, proj_cols], F32, kind="ExternalInput").ap()
        dr["projT"] = nc.dram_tensor("projT", [proj_cols, NT], BF16, kind="ExternalOutput").ap()
        if gate_rows:
            dr["gT"] = nc.dram_tensor("gT", [gate_rows[1] - gate_rows[0], NT], F32, kind="ExternalOutput").ap()
    if final:
        dr["g_fin"] = nc.dram_tensor("g_fin", [1, D], F32, kind="ExternalInput").ap()
        dr["out"] = nc.dram_tensor("out", [NT, D], F32, kind="ExternalOutput").ap()
    if x_out:
        dr["x_o"] = nc.dram_tensor("x_o", [NT, D], F32, kind="ExternalOutput").ap()

    es = ExitStack()
    with es:
        s = Sched(nc, es)
        token_phase_body(nc, s, dr, attn_in, mlp, proj_cols, final, x_out, gate_rows)
        s.emit()
    return nc


def token_phase_body(nc, s, dr, attn_in, mlp, proj_cols, final, x_out, gate_rows):
    NTT = NT // 128
    xs = s.sbuf("xs", [128, NTT, D], F32)
    xb = [s.buf(xs[:, t, :], f"x{t}") for t in range(NTT)]
    hnT_t = s.sbuf("hnT", [128, 8, NT], BF16)
    hnT = [s.buf(hnT_t[:, :, t * 128:(t + 1) * 128], f"hnT{t}") for t in range(NTT)]
    ident = s.buf(s.sbuf("ident", [128, 128], BF16), "ident")
    gbuf = s.buf(s.sbuf("gbuf", [128, D], F32), "gbuf")
    ss = s.buf(s.sbuf("ss", [128, NTT], F32), "ss")
    rstd = s.buf(s.sbuf("rstd", [128, NTT], F32), "rstd")
    junk = Ring([s.buf(s.sbuf(f"junk{i}", [128, D], BF16), f"junk{i}") for i in range(2)])
    hnb = Ring([s.buf(s.sbuf(f"hnb{i}", [128, D], BF16), f"hnb{i}") for i in range(2)])
    pT = Ring([s.pbuf(f"pT{i}", [128, 8, 128], BF16) for i in range(2)])
    pm = Ring([s.pbuf(f"pm{i}", [128, 512], F32) for i in range(5)])
    wa_t = [s.sbuf(f"wa{i}", [128, 8, 512], BF16) for i in range(2)]
    wa = Ring([s.buf(t, f"wa{i}") for i, t in enumerate(wa_t)])
    evac = Ring(["act", "dve"])

    s.dma("sp", ident[:], dr["ident"], writes=[ident])
    xv = dr["x"].rearrange("(t p) d -> t p d", p=128)
    for t in range(NTT):
        s.dma("sp", xb[t][:], xv[t], writes=[xb[t]])

    def rmsnorm_to_hnT(g_ap):
        s.dma("sp", gbuf[:], g_ap.to_broadcast([128, D]), writes=[gbuf])
        for t in range(NTT):
            j = junk.next()
            s.op("act", lambda e, t=t, j=j: e.activation(out=j[:], in_=xb[t][:], func=AF.Square,
                                                         accum_out=ss[:, t:t + 1]), [xb[t]], [j, ss])
        s.op("dve", lambda e: e.tensor_scalar(out=rstd[:], in0=ss[:], scalar1=1.0 / D, scalar2=EPS,
                                              op0=ALU.mult, op1=ALU.add), [ss], [rstd])
        s.op("act", lambda e: e.activation(out=rstd[:], in_=rstd[:], func=AF.Sqrt), [rstd], [rstd])
        s.op("dve", lambda e: e.reciprocal(out=rstd[:], in_=rstd[:]), [rstd], [rstd])

    def hn_transposes():
        if STAGE < 2: return
        for t in range(NTT):
            h = hnb.next()
            s.op("dve", lambda e, t=t, h=h: e.scalar_tensor_tensor(out=h[:], in0=xb[t][:], scalar=rstd[:, t:t + 1],
                                                                   in1=gbuf[:], op0=ALU.mult, op1=ALU.mult),
                 [xb[t], rstd, gbuf], [h])
            if STAGE < 3: continue
            p = pT.next()
            for k in range(8):
                s.op("pe", lambda e, k=k, h=h, p=p: e.transpose(out=p[:, k, :], in_=h[:, k * 128:(k + 1) * 128],
                                                                identity=ident[:]), [h, ident], [p])
            s.op("act", lambda e, t=t, p=p: e.copy(out=hnT[t][:], in_=p[:]), [p], [hnT[t]])

    if attn_in:
        oT = hnT
        ov = dr["oT"].rearrange("(k p) t -> p k t", p=128)
        for t in range(NTT):
            s.dma("sp", oT[t][:], ov[:, :, t * 128:(t + 1) * 128], writes=[oT[t]])
        wo_t = s.sbuf("wo", [128, 8, D], BF16)
        wo = s.buf(wo_t, "wo")
        s.dma("pool", wo[:], dr["w_out"].rearrange("(k p) n -> p k n", p=128), writes=[wo])
        for t in range(NTT):
            for half in range(2):
                p = pm.next()
                for k in range(8):
                    s.op("pe", lambda e, t=t, half=half, k=k, p=p: e.matmul(
                        p[:], lhsT=oT[t][:, k, :], rhs=wo[:, k, half * 512:(half + 1) * 512],
                        start=(k == 0), stop=(k == 7)), [oT[t], wo], [p])
                s.op("dve", lambda e, t=t, half=half, p=p: e.tensor_tensor(
                    out=xb[t][:, half * 512:(half + 1) * 512], in0=xb[t][:, half * 512:(half + 1) * 512],
                    in1=p[:], op=ALU.add), [p, xb[t]], [xb[t]])

    if mlp:
        rmsnorm_to_hnT(dr["g_mlp"])
        hn_transposes()
        wb_t = [s.sbuf(f"wb{i}", [128, 4, D], BF16) for i in range(2)]
        wb = Ring([s.buf(t, f"wb{i}") for i, t in enumerate(wb_t)])
        hT_t = s.sbuf("hT", [128, 4, NT], BF16)
        hT = [[s.buf(hT_t[:, f, c * 512:(c + 1) * 512], f"hT{f}_{c}") for c in range(4)] for f in range(4)]
        rl = Ring([s.buf(s.sbuf(f"rl{i}", [128, 512], F32), f"rl{i}") for i in range(3)])
        w1v = dr["w1"].rearrange("(k p) f -> p k f", p=128)
        w2v = dr["w2"].rearrange("(g f p) n -> g p f n", p=128, f=4)
        NG = 8
        for g in range(NG):
            a = wa.next()
            b = wb.next()
            s.dma("pool", a[:], w1v[:, :, g * 512:(g + 1) * 512], writes=[a])
            s.dma("pool", b[:], w2v[g], writes=[b])
            for c in range(4):
                for f in range(4):
                    p = pm.next()
                    for k in range(8):
                        s.op("pe", lambda e, k=k, f=f, c=c, p=p, a=a: e.matmul(
                            p[:], lhsT=a[:, k, f * 128:(f + 1) * 128], rhs=hnT_t[:, k, c * 512:(c + 1) * 512],
                            start=(k == 0), stop=(k == 7)), [a] + hnT[c * 4:(c + 1) * 4], [p])
                    r = rl.next()
                    s.op("act", lambda e, p=p, r=r: e.activation(out=r[:], in_=p[:], func=AF.Relu), [p], [r])
                    s.op("dve", lambda e, r=r, f=f, c=c: e.tensor_tensor(out=hT[f][c][:], in0=r[:], in1=r[:],
                                                                         op=ALU.mult), [r], [hT[f][c]])
            for t in range(NTT):
                c = t // 4
                for half in range(2):
                    p = pm.next()
                    for f in range(4):
                        s.op("pe", lambda e, t=t, f=f, half=half, p=p, b=b: e.matmul(
                            p[:], lhsT=hT_t[:, f, t * 128:(t + 1) * 128], rhs=b[:, f, half * 512:(half + 1) * 512],
                            start=(f == 0), stop=(f == 3)), [hT[f][c], b], [p])
                    s.op("dve", lambda e, t=t, half=half, p=p: e.tensor_tensor(
                        out=xb[t][:, half * 512:(half + 1) * 512], in0=xb[t][:, half * 512:(half + 1) * 512],
                        in1=p[:], op=ALU.add), [p, xb[t]], [xb[t]])

    if x_out:
        xo = s.buf(dr["x_o"], "x_o")
        xov = dr["x_o"].rearrange("(t p) d -> t p d", p=128)
        for t in range(NTT):
            s.dma("sp", xov[t], xb[t][:], reads=[xb[t]], writes=[xo])
        s.finish_outputs([xo])

    if proj_cols:
        rmsnorm_to_hnT(dr["g_mix"])
        hn_transposes()
        pj = s.buf(dr["projT"], "projT")
        outs = [pj]
        stage = Ring([s.buf(s.sbuf(f"stg{i}", [128, NT], BF16), f"stg{i}") for i in range(2)])
        if gate_rows:
            gt = s.buf(dr["gT"], "gT")
            outs.append(gt)
            gst = s.buf(s.sbuf("gst", [128, NT], F32), "gst")
        ncg = (proj_cols + 511) // 512
        if STAGE < 4: ncg = 0
        if STAGE == 4: ncg = 1
        if STAGE == 5: ncg = 5
        if STAGE == 6: ncg = 2
        if STAGE == 7: ncg = 3
        for cg in range(ncg):
            c0 = cg * 512
            cw = min(512, proj_cols - c0)
            a = wa.next()
            s.dma("pool", a[:, :, 0:cw], dr["w_in"].rearrange("(k p) n -> p k n", p=128)[:, :, c0:c0 + cw], writes=[a])
            for ct in range((cw + 127) // 128):
                m0 = ct * 128
                mw = min(128, cw - m0)
                st = stage.next()
                for c in range(4):
                    p = pm.next()
                    for k in range(8):
                        s.op("pe", lambda e, k=k, c=c, p=p, a=a, m0=m0, mw=mw: e.matmul(
                            p[0:mw, :], lhsT=a[:, k, m0:m0 + mw], rhs=hnT_t[:, k, c * 512:(c + 1) * 512],
                            start=(k == 0), stop=(k == 7)), [a] + hnT[c * 4:(c + 1) * 4], [p])
                    ev = evac.next()
                    if gate_rows and c0 + m0 == gate_rows[0]:
                        ev = "act"
                    if ev == "act":
                        s.op("act", lambda e, p=p, st=st, c=c, mw=mw: e.copy(out=st[0:mw, c * 512:(c + 1) * 512],
                                                                            in_=p[0:mw, :]), [p], [st])
                    else:
                        s.op("dve", lambda e, p=p, st=st, c=c, mw=mw: e.tensor_copy(out=st[0:mw, c * 512:(c + 1) * 512],
                                                                                   in_=p[0:mw, :]), [p], [st])
                    if gate_rows and c0 + m0 == gate_rows[0]:
                        ng = gate_rows[1] - gate_rows[0]
                        s.op("act", lambda e, p=p, c=c, ng=ng: e.copy(out=gst[0:ng, c * 512:(c + 1) * 512],
                                                                     in_=p[0:ng, :]), [p], [gst])
                s.dma("sp", dr["projT"][c0 + m0:c0 + m0 + mw, :], st[0:mw, :], reads=[st], writes=[pj])
                if gate_rows and c0 + m0 == gate_rows[0]:
                    ng = gate_rows[1] - gate_rows[0]
                    s.dma("sp", dr["gT"], gst[0:ng, :], reads=[gst], writes=[gt])
        s.finish_outputs(outs)

    if final:
        rmsnorm_to_hnT(dr["g_fin"])
        ob = s.buf(dr["out"], "out")
        ofin = Ring([s.buf(s.sbuf(f"ofin{i}", [128, D], F32), f"ofin{i}") for i in range(2)])
        outv = dr["out"].rearrange("(t p) d -> t p d", p=128)
        for t in range(NTT):
            o = ofin.next()
            s.op("dve", lambda e, t=t, o=o: e.scalar_tensor_tensor(out=o[:], in0=xb[t][:], scalar=rstd[:, t:t + 1],
                                                                   in1=gbuf[:], op0=ALU.mult, op1=ALU.mult),
                 [xb[t], rstd, gbuf], [o])
            s.dma("sp", outv[t], o[:], reads=[o], writes=[ob])
        s.finish_outputs([ob])


S = 8192
NEGM = -30000.0
NCH = 16


def t5_bucket_np(rel):
    n = np.maximum(rel, 0).astype(np.int32)
    nf = np.maximum(n, 1).astype(np.float32)
    large = 16 + (np.log(nf / np.float32(16)) / np.float32(np.log(128 / 16)) * np.float32(16)).astype(np.int32)
    large = np.minimum(large, 31)
    return np.where(n < 16, n, large)


def causal_strip(bias_table, h):
    p = np.arange(128)[:, None]
    y = np.arange(1024)[None, :]
    rel = y - p - 384
    v = bias_table[t5_bucket_np(rel), h].astype(np.float32)
    return np.where(rel >= 0, v, np.float32(NEGM)).astype(np.float32)


def moba_host_inputs(projT_b, bias_table, heads):
    qT = np.stack([projT_b[h * 64:(h + 1) * 64] for h in heads])
    ind = (np.arange(S)[None, :] // 256 == np.arange(32)[:, None]).astype(NPBF)
    kA = np.stack([np.concatenate([projT_b[1024 + h * 64:1024 + (h + 1) * 64], ind], 0) for h in heads])
    ones = np.ones((S, 1), NPBF)
    vA = np.stack([np.concatenate([np.ascontiguousarray(projT_b[2048 + h * 64:2048 + (h + 1) * 64].T), ones], 1)
                   for h in heads])
    strip = np.stack([causal_strip(bias_table, h) for h in heads])
    cb = np.broadcast_to(bias_table[31, heads][None, :], (128, 4)).astype(np.float32).copy()
    c = np.arange(32)[:, None]
    n = np.arange(32)[None, :]
    past01 = (n < c).astype(np.float32)
    own01 = (n == c).astype(np.float32)
    pastneg = np.where(n < c, 0.0, NEGM).astype(np.float32)
    cm = np.stack([pastneg, past01, own01])
    cm = np.broadcast_to(cm[None], (128, 3, 32, 32)).copy()
    return {"qT": qT, "kA": kA, "vA": vA, "strip": strip, "cb": cb, "cm": cm,
            "identb": np.eye(128, dtype=np.float32).astype(NPBF), "identf": np.eye(128, dtype=np.float32)}


def build_moba(nch=NCH, nheads=4):
    nc = bass.Bass("TRN2", target_bir_lowering=False)
    dr = {}
    dr["qT"] = nc.dram_tensor("qT", [4, 64, S], BF16, kind="ExternalInput").ap()
    dr["kA"] = nc.dram_tensor("kA", [4, 96, S], BF16, kind="ExternalInput").ap()
    dr["vA"] = nc.dram_tensor("vA", [4, S, 65], BF16, kind="ExternalInput").ap()
    dr["strip"] = nc.dram_tensor("strip", [4, 128, 1024], F32, kind="ExternalInput").ap()
    dr["cb"] = nc.dram_tensor("cb", [128, 4], F32, kind="ExternalInput").ap()
    dr["cm"] = nc.dram_tensor("cm", [128, 3, 32, 32], F32, kind="ExternalInput").ap()
    dr["identb"] = nc.dram_tensor("identb", [128, 128], BF16, kind="ExternalInput").ap()
    dr["identf"] = nc.dram_tensor("identf", [128, 128], F32, kind="ExternalInput").ap()
    dr["o"] = nc.dram_tensor("o", [S, 256], BF16, kind="ExternalOutput").ap()
    es = ExitStack()
    with es:
        s = Sched(nc, es)
        moba_body(nc, s, dr, nch, nheads)
        s.emit()
    return nc


def moba_body(nc, s, dr, nch, nheads):
    H = nheads
    kA = [s.buf(s.sbuf(f"kA{h}", [96, S], BF16), f"kA{h}") for h in range(H)]
    vA = [s.buf(s.sbuf(f"vA{h}", [128, 64, 65], BF16), f"vA{h}") for h in range(H)]
    strip = [s.buf(s.sbuf(f"strip{h}", [128, 1024], F32), f"strip{h}") for h in range(H)]
    cb = s.buf(s.sbuf("cb", [128, 4], F32), "cb")
    cm = s.buf(s.sbuf("cm", [128, 3, 32, 32], F32), "cm")
    identb = s.buf(s.sbuf("identb", [128, 128], BF16), "identb")
    identf = s.buf(s.sbuf("identf", [128, 128], F32), "identf")
    kmean = [s.buf(s.sbuf(f"kmean{h}", [64, 32], F32), f"kmean{h}") for h in range(H)]
    QA = Ring([s.buf(s.sbuf(f"QA{i}", [96, 512], BF16), f"QA{i}") for i in range(3)])
    qf = Ring([s.buf(s.sbuf(f"qf{i}", [64, 512], F32), f"qf{i}") for i in range(2)])
    gm = Ring([s.buf(s.sbuf(f"gm{i}", [128, 4, 32], F32), f"gm{i}") for i in range(2)])
    mx = Ring([s.buf(s.sbuf(f"mx{i}", [128, 4, 8], F32), f"mx{i}") for i in range(2)])
    sm = Ring([s.buf(s.sbuf(f"sm{i}", [128, 4, 32], F32), f"sm{i}") for i in range(2)])
    NM = Ring([s.buf(s.sbuf(f"NM{i}", [128, 4, 128], BF16), f"NM{i}") for i in range(2)])
    P = Ring([s.buf(s.sbuf(f"P{i}", [128, 512], BF16), f"P{i}") for i in range(4)])
    tmp = Ring([s.buf(s.sbuf(f"tmp{i}", [128, 512], F32), f"tmp{i}") for i in range(2)])
    Osb = Ring([s.buf(s.sbuf(f"Osb{i}", [65, 512], F32), f"Osb{i}") for i in range(2)])
    rs = Ring([s.buf(s.sbuf(f"rs{i}", [128, 4, 1], F32), f"rs{i}") for i in range(2)])
    ost = Ring([s.buf(s.sbuf(f"ost{i}", [128, 4, 256], BF16), f"ost{i}") for i in range(2)])
    pS = Ring([s.pbuf(f"pS{i}", [128, 512], F32) for i in range(3)])
    pO = Ring([s.pbuf(f"pO{i}", [128, 512], F32) for i in range(2)])
    pA = s.pbuf("pA", [128, 512], F32)
    pB = s.pbuf("pB", [128, 512], F32)
    ob = s.buf(dr["o"], "o")

    s.dma("sp", identb[:], dr["identb"], writes=[identb])
    s.dma("sp", identf[:], dr["identf"], writes=[identf])
    s.dma("sp", cb[:], dr["cb"], writes=[cb])
    s.dma("sp", cm[:], dr["cm"], writes=[cm])
    for h in range(H):
        s.dma("sp", kA[h][:], dr["kA"][h], writes=[kA[h]])
        s.dma("sp", vA[h][:], dr["vA"][h].rearrange("(t p) c -> p t c", p=128), writes=[vA[h]])
        s.dma("sp", strip[h][:], dr["strip"][h], writes=[strip[h]])
    for nm in NM.items:
        s.op("pool", lambda e, nm=nm: e.memset(nm[:], 0.0), [], [nm])
    for h in range(H):
        s.op("dve", lambda e, h=h: e.tensor_reduce(out=kmean[h][:], in_=kA[h][0:64, :].rearrange("p (n k) -> p n k", k=256),
                                                   axis=AX.X, op=ALU.add), [kA[h]], [kmean[h]])
        s.op("dve", lambda e, h=h: e.tensor_scalar(out=kmean[h][:], in0=kmean[h][:], scalar1=1.0 / 256, scalar2=None,
                                                   op0=ALU.mult), [kmean[h]], [kmean[h]])

    ov = dr["o"].rearrange("(c s p) d -> c p s d", p=128, s=4)
    for i in range(nch):
        osb_out = ost.next()
        for h in range(H):
            qa = QA.next()
            s.dma("sp", qa[0:64, :], dr["qT"][h][:, i * 512:(i + 1) * 512], writes=[qa])
            q32 = qf.next()
            s.op("act", lambda e, qa=qa, q32=q32: e.copy(out=q32[:], in_=qa[0:64, :]), [qa], [q32])
            for sub in range(4):
                s.op("pe", lambda e, sub=sub, q32=q32, h=h: e.matmul(
                    pA[:, sub * 32:(sub + 1) * 32], lhsT=q32[:, sub * 128:(sub + 1) * 128], rhs=kmean[h][:],
                    start=True, stop=True), [q32, kmean[h]], [pA])
            g = gm.next()
            m8 = mx.next()
            sel = sm.next()
            nm = NM.next()
            for sub in range(4):
                c = (i * 4 + sub) // 2
                s.op("dve", lambda e, sub=sub, c=c, g=g: e.tensor_tensor(
                    out=g[:, sub, :], in0=pA[:, sub * 32:(sub + 1) * 32], in1=cm[:, 0, c, :], op=ALU.add),
                    [pA, cm], [g])
            for sub in range(4):
                s.op("dve", lambda e, sub=sub, g=g, m8=m8: e.max(out=m8[:, sub, :], in_=g[:, sub, :]), [g], [m8])
            for sub in range(4):
                c = (i * 4 + sub) // 2
                s.op("dve", lambda e, sub=sub, g=g, m8=m8, sel=sel: e.tensor_scalar(
                    out=sel[:, sub, :], in0=g[:, sub, :], scalar1=m8[:, sub, 2:3], scalar2=None, op0=ALU.is_ge),
                    [g, m8], [sel])
                s.op("dve", lambda e, sub=sub, c=c, sel=sel: e.tensor_tensor(
                    out=sel[:, sub, :], in0=sel[:, sub, :], in1=cm[:, 1, c, :], op=ALU.mult), [sel, cm], [sel])
                s.op("dve", lambda e, sub=sub, c=c, sel=sel: e.tensor_tensor(
                    out=sel[:, sub, :], in0=sel[:, sub, :], in1=cm[:, 2, c, :], op=ALU.add), [sel, cm], [sel])
            s.op("dve", lambda e, sel=sel, nm=nm: e.tensor_scalar(
                out=nm[:, :, 64:96], in0=sel[:], scalar1=1.0, scalar2=-NEGM, op0=ALU.subtract, op1=ALU.mult),
                [sel], [nm])
            for sub in range(4):
                s.op("pe", lambda e, sub=sub, nm=nm: e.matmul(
                    pB[:, sub * 128:(sub + 1) * 128], lhsT=nm[:, sub, :], rhs=identb[:], start=True, stop=True),
                    [nm, identb], [pB])
            s.op("act", lambda e, qa=qa: e.copy(out=qa[64:96, :], in_=pB[64:96, :]), [pB], [qa])
            po = pO.next()
            nkt = 4 * i + 4
            for kt in range(nkt):
                j = kt - 4 * i
                ps = pS.next()
                s.op("pe", lambda e, kt=kt, ps=ps, qa=qa, h=h: e.matmul(
                    ps[:], lhsT=kA[h][:, kt * 128:(kt + 1) * 128], rhs=qa[:], start=True, stop=True),
                    [kA[h], qa], [ps])
                p = P.next()
                if j >= -1:
                    y0 = 384 - 128 * j
                    t = tmp.next()
                    s.op("dve", lambda e, ps=ps, t=t, h=h, y0=y0: e.scalar_tensor_tensor(
                        out=t[:], in0=ps[:], scalar=0.125, in1=strip[h][:, y0:y0 + 512], op0=ALU.mult, op1=ALU.add),
                        [ps, strip[h]], [t])
                    s.op("act", lambda e, t=t, p=p: e.activation(out=p[:], in_=t[:], func=AF.Exp), [t], [p])
                else:
                    s.op("act", lambda e, ps=ps, p=p, h=h: e.activation(out=p[:], in_=ps[:], func=AF.Exp,
                                                                        bias=cb[:, h:h + 1], scale=0.125),
                         [ps, cb], [p])
                s.op("pe", lambda e, kt=kt, p=p, po=po, h=h, nkt=nkt: e.matmul(
                    po[0:65, :], lhsT=vA[h][:, kt, :], rhs=p[:], start=(kt == 0), stop=(kt == nkt - 1)),
                    [vA[h], p], [po])
            osb = Osb.next()
            s.op("dve", lambda e, osb=osb, po=po: e.tensor_copy(out=osb[:], in_=po[0:65, :]), [po], [osb])
            for sub in range(4):
                s.op("pe", lambda e, sub=sub, osb=osb: e.transpose(
                    out=pA[:, 128 + sub * 65:128 + (sub + 1) * 65], in_=osb[:, sub * 128:(sub + 1) * 128],
                    identity=identf[0:65, 0:65]), [osb, identf], [pA])
            r = rs.next()
            pav = pA[:, 128:128 + 260].rearrange("p (s c) -> p s c", c=65)
            s.op("dve", lambda e, r=r, pav=pav: e.reciprocal(out=r[:], in_=pav[:, :, 64:65]), [pA], [r])
            s.op("dve", lambda e, r=r, pav=pav, h=h, osb_out=osb_out: e.tensor_tensor(
                out=osb_out[:, :, h * 64:(h + 1) * 64], in0=pav[:, :, 0:64], in1=r[:].to_broadcast([128, 4, 64]),
                op=ALU.mult), [pA, r], [osb_out])
        s.dma("sp", ov[i], osb_out[:], reads=[osb_out], writes=[ob])
    s.finish_outputs([ob])


LAM_INIT0 = 0.8 - 0.6 * 1.0
SUB_EPS = 1e-6
GELU_C = 0.7978845608028654


def window_strip(bias_table, h):
    p = np.arange(128)[:, None]
    y = np.arange(1408)[None, :]
    rel = y - p - 384
    v = bias_table[t5_bucket_np(rel), h].astype(np.float32)
    return np.where((rel >= 0) & (rel < 512), v, np.float32(NEGM)).astype(np.float32)


def cmp_bias(bias_table, h):
    out = []
    p = np.arange(128)[:, None]
    x = np.arange(512)[None, :]
    for o in range(0, 2560, 512):
        rel = o + x - 16 * p - 31
        v = bias_table[t5_bucket_np(rel), h].astype(np.float32)
        out.append(np.where(rel >= 0, v, np.float32(NEGM)))
    return np.stack(out).astype(np.float32)


def l2_host_inputs(projT_b, gT_b, inputs, g, half, hd):
    bt = inputs["bias_table"]
    own = [2 * half, 2 * half + 1]
    oth = [r for r in range(4) if r not in own]
    order = own + oth
    gh = [g * 4 + r for r in order]
    qT4 = np.stack([projT_b[h * 64:(h + 1) * 64] for h in gh])
    kcvT = np.concatenate([projT_b[512 + g * 64:512 + (g + 1) * 64], projT_b[640 + g * 64:640 + (g + 1) * 64]], 0)
    ind = ((np.arange(S)[None, :] // 64) % 64 == np.arange(64)[:, None]).astype(NPBF)
    ksA = np.concatenate([projT_b[768 + g * 64:768 + (g + 1) * 64], ind], 0)
    ones = np.ones((S, 1), NPBF)
    vsA = np.concatenate([np.ascontiguousarray(projT_b[896 + g * 64:896 + (g + 1) * 64].T), ones], 1)
    kwT = np.ascontiguousarray(projT_b[1024 + g * 64:1024 + (g + 1) * 64])
    vwA = np.concatenate([np.ascontiguousarray(projT_b[1152 + g * 64:1152 + (g + 1) * 64].T), ones], 1)
    gsel = np.stack([np.stack([gT_b[br * 8 + gh[k]] for br in range(3)], -1) for k in range(2)], 1).astype(np.float32)
    dqT = np.ascontiguousarray(projT_b[1304 + hd * 128:1304 + (hd + 1) * 128])
    dkT = np.ascontiguousarray(projT_b[1816 + hd * 128:1816 + (hd + 1) * 128])
    dvA = np.ascontiguousarray(projT_b[2328 + hd * 128:2328 + (hd + 1) * 128].T)
    cstrip = np.stack([causal_strip(bt, gh[0]), causal_strip(bt, gh[1]), causal_strip(bt, 8 + hd)])
    wstrip = np.stack([window_strip(bt, gh[0]), window_strip(bt, gh[1])])
    cbias = np.stack([cmp_bias(bt, h) for h in gh])
    cb = np.broadcast_to(bt[31, gh + [8 + hd]][None, :], (128, 5)).astype(np.float32).copy()
    w1kv = np.stack([inputs["ev_cmp_w1_k"][0], inputs["ev_cmp_w1_v"][0]])
    w2kv = np.stack([inputs["ev_cmp_w2_k"][0], inputs["ev_cmp_w2_v"][0]])
    posT = np.concatenate([inputs["ev_cmp_pos_k"][0].T, inputs["ev_cmp_pos_v"][0].T], 0)
    c = np.arange(512)
    ovl = np.zeros((512, 129), np.float32)
    for cc in range(511):
        for tk in range(16 * cc, 16 * cc + 32):
            ovl[cc, tk // 64] += 1.0 / 32
        ovl[cc, 128] = 1.0
    cur = np.arange(128)[:, None]
    n = np.arange(128)[None, :]
    elig = n <= cur
    forced = (n == 0) | (n == cur) | (n == cur - 1)
    A = elig.astype(np.float32)
    B = np.where(elig, np.where(forced, 1000.0, 0.0), -1.0).astype(np.float32)
    ABtab = np.stack([A, B], 1)
    lam = np.stack([inputs["ev_lam_q1"][0], inputs["ev_lam_k1"][0], inputs["ev_lam_q2"][0], inputs["ev_lam_k2"][0]])
    return {"qT4": qT4, "kcvT": kcvT, "ksA": ksA, "vsA": vsA, "kwT": kwT, "vwA": vwA, "gsel": gsel,
            "dqT": dqT, "dkT": dkT, "dvA": dvA, "cstrip": cstrip, "wstrip": wstrip, "cbias": cbias, "cb": cb,
            "w1kv": w1kv, "w2kv": w2kv, "posT": posT.astype(np.float32), "ovl": ovl, "ABtab": ABtab,
            "lam": lam.reshape(1, 256).astype(np.float32), "subln": inputs["ev_subln"][0].reshape(1, 128),
            "identb": np.eye(128, dtype=np.float32).astype(NPBF), "identf": np.eye(128, dtype=np.float32),
            "ones2": np.stack([np.concatenate([np.ones((128, 1)), np.zeros((128, 1))], 1),
                               np.concatenate([np.zeros((128, 1)), np.ones((128, 1))], 1)]).astype(NPBF)}


L2_SPECS = {
    "qT4": ([4, 64, S], BF16), "kcvT": ([128, S], BF16), "ksA": ([128, S], BF16), "vsA": ([S, 65], BF16),
    "kwT": ([64, S], BF16), "vwA": ([S, 65], BF16), "gsel": ([S, 2, 3], F32), "dqT": ([128, S], BF16),
    "dkT": ([128, S], BF16), "dvA": ([S, 128], BF16), "cstrip": ([3, 128, 1024], F32),
    "wstrip": ([2, 128, 1408], F32), "cbias": ([4, 5, 128, 512], F32), "cb": ([128, 5], F32),
    "w1kv": ([2, 2048, 128], F32), "w2kv": ([2, 128, 64], F32), "posT": ([128, 32], F32),
    "ovl": ([512, 129], F32), "ABtab": ([128, 2, 128], F32), "lam": ([1, 256], F32), "subln": ([1, 128], F32),
    "identb": ([128, 128], BF16), "identf": ([128, 128], F32), "ones2": ([2, 128, 2], BF16),
}


def build_l2(nch=16, do_nsa=True, do_diff=True):
    nc = bass.Bass("TRN2", target_bir_lowering=False)
    dr = {k: nc.dram_tensor(k, sh, dt, kind="ExternalInput").ap() for k, (sh, dt) in L2_SPECS.items()}
    dr["o"] = nc.dram_tensor("o", [S, 256], BF16, kind="ExternalOutput").ap()
    es = ExitStack()
    with es:
        s = Sched(nc, es)
        l2_body(nc, s, dr, nch, do_nsa, do_diff)
        s.emit()
    return nc


def l2_body(nc, s, dr, nch, do_nsa, do_diff):
    def sb(name, shape, dt):
        return s.buf(s.sbuf(name, shape, dt), name)

    def load(name, shape, dt, src=None, eng="sp"):
        b = sb(name, shape, dt)
        s.dma(eng, b[:], dr[name] if src is None else src, writes=[b])
        return b

    identb = load("identb", [128, 128], BF16)
    identf = load("identf", [128, 128], F32)
    cb = load("cb", [128, 5], F32)
    cstrip = [load(f"cstrip{k}", [128, 1024], F32, dr["cstrip"][k]) for k in range(3)]
    pS = Ring([s.pbuf(f"pS{i}", [128, 512], F32) for i in range(3)])
    pOa = s.pbuf("pOa", [128, 512], F32)
    pOb = s.pbuf("pOb", [128, 512], F32)
    pU0 = s.pbuf("pU0", [128, 512], F32)
    pU1 = s.pbuf("pU1", [128, 512], F32)
    pL = s.pbuf("pL", [128, 512], F32)
    P = Ring([sb(f"P{i}", [128, 512], BF16) for i in range(4)])
    tmp = Ring([sb(f"tmp{i}", [128, 512], F32) for i in range(3)])
    ost = Ring([sb(f"ost{i}", [128, 4, 256], BF16) for i in range(2)])
    ob = s.buf(dr["o"], "o")
    ov = dr["o"].rearrange("(c s p) d -> c p s d", p=128, s=4)

    def exp_tile(ps, j, strip_ap_fn, cb_ap):
        p = P.next()
        if strip_ap_fn is not None:
            t = tmp.next()
            sbuf_, ap = strip_ap_fn
            s.op("dve", lambda e, ps=ps, t=t, ap=ap: e.scalar_tensor_tensor(
                out=t[:], in0=ps[:], scalar=0.125, in1=ap, op0=ALU.mult, op1=ALU.add), [ps, sbuf_], [t])
            s.op("act", lambda e, t=t, p=p: e.activation(out=p[:], in_=t[:], func=AF.Exp), [t], [p])
        else:
            s.op("act", lambda e, ps=ps, p=p: e.activation(out=p[:], in_=ps[:], func=AF.Exp, bias=cb_ap, scale=0.125),
                 [ps, cb], [p])
        return p

    ksd = sb("ksd", [128, S], BF16)
    kcd = sb("kcd", [128, S], BF16)
    scr = sb("scr", [128, 8192], BF16)
    ob3 = Ring([sb(f"ob3_{i}", [128, 512], F32) for i in range(3)])
    if do_nsa:
        ksA = ksd
        s.dma("sp", ksd[:], dr["ksA"], writes=[ksd])
        vsA = load("vsA", [128, 64, 65], BF16, dr["vsA"].rearrange("(t p) c -> p t c", p=128))
        kwT = load("kwT", [64, S], BF16)
        vwA = load("vwA", [128, 64, 65], BF16, dr["vwA"].rearrange("(t p) c -> p t c", p=128))
        wstrip = [load(f"wstrip{k}", [128, 1408], F32, dr["wstrip"][k]) for k in range(2)]
        cbias = [load(f"cbias{k}", [128, 5, 512], BF16, dr["cbias"][k].rearrange("o p x -> p o x"), eng="pool") for k in range(4)]
        gates = load("gates", [128, 64, 6], F32, dr["gsel"].rearrange("(t p) h b -> p t (h b)", p=128))
        s.op("act", lambda e: e.activation(out=gates[:], in_=gates[:], func=AF.Sigmoid), [gates], [gates])
        kcvT = kcd
        s.dma("sp", kcd[:], dr["kcvT"], writes=[kcd])
        w1kv = s.buf(scr.t[:, 0:4096].rearrange("p (j m) -> p j m", m=128), "w1kv_view")
        for m in range(2):
            s.dma("pool", w1kv[64 * m:64 * m + 64, :, :], dr["w1kv"][m].rearrange("(j d) m -> d j m", d=64), writes=[scr])
        w2kv = sb("w2kv", [128, 2, 64], BF16)
        s.dma("pool", w2kv[:], dr["w2kv"].rearrange("t k d -> k t d"), writes=[w2kv])
        posT = sb("posT", [128, 32], BF16)
        s.dma("pool", posT[:], dr["posT"], writes=[posT])
        R = sb("R", [128, 4, 193], BF16)
        s.dma("pool", R[:, :, 0:129], dr["ovl"].rearrange("(j p) n -> p j n", p=128), writes=[R])
        KcT = sb("KcT", [64, 512], BF16)
        gl = [P.items[0], P.items[1]]
        xs = tmp.items[0]
        x2 = tmp.items[1]
        for m in range(2):
            ps = pS.next()
            lo = 64 * m
            for j in range(32):
                s.op("pe", lambda e, j=j, lo=lo, ps=ps: e.matmul(
                    ps[:, 0:511], lhsT=w1kv[lo:lo + 64, j, :], rhs=kcvT[lo:lo + 64, j:j + 8161:16],
                    start=(j == 0), stop=False), [scr, kcvT], [ps])
            for j in range(32):
                s.op("pe", lambda e, j=j, lo=lo, ps=ps: e.matmul(
                    ps[:, 0:511], lhsT=w1kv[lo:lo + 64, j, :], rhs=posT[lo:lo + 64, j:j + 1].to_broadcast([64, 511]),
                    start=False, stop=(j == 31)), [scr, posT], [ps])
            s.op("pool", lambda e, m=m: e.memset(gl[m][:], 0.0), [], [gl[m]])
            s.op("act", lambda e, ps=ps: e.copy(out=xs[:, 0:511], in_=ps[:, 0:511]), [ps], [xs])
            s.op("dve", lambda e: e.tensor_tensor(out=x2[:, 0:511], in0=xs[:, 0:511], in1=xs[:, 0:511], op=ALU.mult), [xs], [x2])
            s.op("dve", lambda e: e.tensor_scalar(out=x2[:, 0:511], in0=x2[:, 0:511], scalar1=0.044715, scalar2=1.0,
                                                  op0=ALU.mult, op1=ALU.add), [x2], [x2])
            s.op("dve", lambda e: e.tensor_tensor(out=x2[:, 0:511], in0=x2[:, 0:511], in1=xs[:, 0:511], op=ALU.mult), [x2, xs], [x2])
            s.op("act", lambda e: e.activation(out=x2[:, 0:511], in_=x2[:, 0:511], func=AF.Sigmoid, scale=2.0 * GELU_C), [x2], [x2])
            s.op("dve", lambda e, m=m: e.tensor_tensor(out=gl[m][:, 0:511], in0=x2[:, 0:511], in1=xs[:, 0:511], op=ALU.mult),
                 [x2, xs], [gl[m]])
        ps = pS.next()
        s.op("pe", lambda e, ps=ps: e.matmul(ps[0:64, :], lhsT=w2kv[:, 0, :], rhs=gl[0][:], start=True, stop=True),
             [w2kv, gl[0]], [ps])
        s.op("act", lambda e, ps=ps: e.copy(out=KcT[:], in_=ps[0:64, :]), [ps], [KcT])
        ps = pS.next()
        for jt in range(4):
            s.op("pe", lambda e, jt=jt, ps=ps: e.matmul(ps[:, jt * 64:(jt + 1) * 64], lhsT=gl[1][:, jt * 128:(jt + 1) * 128],
                                                        rhs=w2kv[:, 1, :], start=True, stop=True), [gl[1], w2kv], [ps])
        s.op("act", lambda e, ps=ps: e.copy(out=R[:, :, 129:193], in_=ps[:, 0:256].rearrange("p (j d) -> p j d", d=64)),
             [ps], [R])
        QA = [[sb(f"QA{k}_{hf}_{i}", [128, 512], BF16) for i in range(2)] for k in range(2) for hf in range(2)]
        Qc = [Ring([sb(f"Qc{k}_{i}", [64, 512], BF16) for i in range(2)]) for k in range(2)]
        ABt = Ring([sb(f"ABt{i}", [128, 2, 128], F32) for i in range(3)])
        rsu = Ring([sb(f"rsu{i}", [128, 4], F32) for i in range(2)])
        imp = Ring([sb(f"imp{i}", [128, 128], F32) for i in range(2)])
        sc2 = Ring([sb(f"sc2{i}", [128, 128], F32) for i in range(2)])
        mx = Ring([sb(f"mx{i}", [128, 16], F32) for i in range(2)])
        NM = Ring([sb(f"NM{i}", [128, 192], BF16) for i in range(2)])
        for nm in NM.items:
            s.op("pool", lambda e, nm=nm: e.memset(nm[:], 0.0), [], [nm])
        ocmp = Ring([sb(f"ocmp{i}", [128, 4, 2, 64], F32) for i in range(2)])
        Osb = ob3
        rs2 = Ring([sb(f"rs2{i}", [128, 4, 1], F32) for i in range(3)])
        acc = Ring([sb(f"acc{i}", [128, 4, 64], F32) for i in range(2)])
        t2b = Ring([sb(f"t2b{i}", [128, 4, 64], F32) for i in range(2)])
        wg = Ring([sb(f"wg{i}", [128, 4, 1], F32) for i in range(4)])
        Ecm = [[scr.t[:, (r * 4 + jt) * 512:(r * 4 + jt + 1) * 512] for jt in range(4)] for r in range(4)]

    if do_diff:
        ones2 = load("ones2", [128, 2, 2], BF16, dr["ones2"].rearrange("m p c -> p m c"))
        lamv = load("lamv", [128, 256], F32, dr["lam"].to_broadcast([128, 256]))
        gsub = load("gsub", [128, 128], F32, dr["subln"].to_broadcast([128, 128]))
        s.op("dve", lambda e: e.tensor_scalar(out=gsub[:], in0=gsub[:], scalar1=1.0 - LAM_INIT0, scalar2=None, op0=ALU.mult),
             [gsub], [gsub])
        lt = sb("lt", [128, 128], F32)
        l2s = sb("l2s", [128, 2], F32)
        nlam = sb("nlam", [128, 1], F32)
        lv = lamv[:].rearrange("p (a d) -> p a d", d=64)
        s.op("dve", lambda e: e.tensor_tensor(out=lt[:].rearrange("p (a d) -> p a d", d=64), in0=lv[:, 0:4:2, :],
                                              in1=lv[:, 1:4:2, :], op=ALU.mult), [lamv], [lt])
        s.op("dve", lambda e: e.tensor_reduce(out=l2s[:], in_=lt[:].rearrange("p (a d) -> p a d", d=64), axis=AX.X,
                                              op=ALU.add), [lt], [l2s])
        s.op("act", lambda e: e.activation(out=l2s[:], in_=l2s[:], func=AF.Exp), [l2s], [l2s])
        s.op("dve", lambda e: e.tensor_tensor(out=nlam[:], in0=l2s[:, 1:2], in1=l2s[:, 0:1], op=ALU.subtract), [l2s], [nlam])
        s.op("dve", lambda e: e.tensor_scalar(out=nlam[:], in0=nlam[:], scalar1=-LAM_INIT0, scalar2=None, op0=ALU.add),
             [nlam], [nlam])
        dq = Ring([sb(f"dq{i}", [128, 512], BF16) for i in range(2)])
        Od = ob3
        sd = sb("sd", [2, 512], F32)
        rd = Ring([sb(f"rd{i}", [128, 4, 2], F32) for i in range(2)])
        o0 = Ring([sb(f"o0{i}", [128, 4, 128], F32) for i in range(1)])
        av = Ring([sb(f"av{i}", [128, 4, 128], F32) for i in range(1)])
        sq = sb("sq", [128, 4, 128], F32)
        ssd = Ring([sb(f"ssd{i}", [128, 4], F32) for i in range(2)])

    ovn = dr["o"][:, 0:128].rearrange("(c s p) d -> c p s d", p=128, s=4)
    ovd = dr["o"][:, 128:256].rearrange("(c s p) d -> c p s d", p=128, s=4)
    for i in range(nch if do_nsa else 0):
        oo = ost.next()
        nkt = 4 * i + 4
        if do_nsa:
            qa = [[QA[k * 2 + hf][i % 2] for hf in range(2)] for k in range(2)]
            nhf = 2 if i >= 8 else 1
            qsrc = []
            for k in range(2):
                for hf in range(nhf):
                    s.dma("sp", qa[k][hf][0:64, :], dr["qT4"][k][:, i * 512:(i + 1) * 512], writes=[qa[k][hf]])
                qsrc.append((qa[k][0], qa[k][0][0:64, :]))
            for k in range(2):
                q = Qc[k].next()
                s.dma("sp", q[:], dr["qT4"][2 + k][:, i * 512:(i + 1) * 512], writes=[q])
                qsrc.append((q, q[:]))
            ncj = min(4, (512 * i + 511 - 31) // 16 // 128 + 1)
            for r in range(4):
                qb, qap = qsrc[r]
                for jt in range(ncj):
                    ps = pS.next()
                    s.op("pe", lambda e, ps=ps, jt=jt, qap=qap: e.matmul(
                        ps[:], lhsT=KcT[:, jt * 128:(jt + 1) * 128], rhs=qap, start=True, stop=True), [KcT, qb], [ps])
                    o_ = 512 * i - 2048 * jt
                    ec = Ecm[r][jt]
                    if o_ <= 2048:
                        t = tmp.next()
                        s.op("dve", lambda e, ps=ps, t=t, r=r, o_=o_: e.scalar_tensor_tensor(
                            out=t[:], in0=ps[:], scalar=0.125, in1=cbias[r][:, o_ // 512, :], op0=ALU.mult, op1=ALU.add),
                            [ps, cbias[r]], [t])
                        s.op("act", lambda e, t=t, ec=ec: e.activation(out=ec, in_=t[:], func=AF.Exp), [t], [scr])
                    else:
                        s.op("act", lambda e, ps=ps, ec=ec, r=r: e.activation(out=ec, in_=ps[:], func=AF.Exp,
                                                                            bias=cb[:, r:r + 1], scale=0.125), [ps, cb], [scr])
            oc = ocmp.next()
            for sub in range(4):
                tt = 4 * i + sub
                ab = ABt.next()
                for hh in range(2):
                    s.dma("sp", ab[64 * hh:64 * hh + 64, :, :], dr["ABtab"][2 * tt + hh:2 * tt + hh + 1].to_broadcast([64, 2, 128]),
                          writes=[ab])
                pU = [pU0, pU1]
                for r in range(4):
                    pu = pU[r // 2]
                    c0 = (r % 2) * 193
                    for jt in range(ncj):
                        s.op("pe", lambda e, pu=pu, c0=c0, r=r, jt=jt, sub=sub, ncj=ncj: e.matmul(
                            pu[:, c0:c0 + 193], lhsT=Ecm[r][jt][:, sub * 128:(sub + 1) * 128], rhs=R[:, jt, :],
                            start=(jt == 0), stop=(jt == ncj - 1)), [scr, R], [pu])
                ru = rsu.next()
                for b2 in range(2):
                    s.op("dve", lambda e, b2=b2, ru=ru, pU=pU: e.tensor_scalar(
                        out=ru[:, 2 * b2:2 * b2 + 2], in0=pU[b2][:, 0:386].rearrange("p (h c) -> p h c", c=193)[:, :, 128],
                        scalar1=1e-30, scalar2=None, op0=ALU.max), [pU[b2]], [ru])
                s.op("dve", lambda e, ru=ru: e.reciprocal(out=ru[:], in_=ru[:]), [ru], [ru])
                im = imp.next()
                s.op("dve", lambda e, im=im, ru=ru: e.tensor_scalar(out=im[:], in0=pU0[:, 0:128], scalar1=ru[:, 0:1], scalar2=None,
                                                                     op0=ALU.mult), [pU0, ru], [im])
                for r in range(1, 4):
                    pu = pU[r // 2]
                    c0 = (r % 2) * 193
                    s.op("dve", lambda e, im=im, ru=ru, pu=pu, c0=c0, r=r: e.scalar_tensor_tensor(
                        out=im[:], in0=pu[:, c0:c0 + 128], scalar=ru[:, r:r + 1], in1=im[:], op0=ALU.mult, op1=ALU.add),
                        [pu, ru, im], [im])
                for k in range(2):
                    c0 = k * 193
                    s.op("dve", lambda e, k=k, c0=c0, oc=oc, ru=ru, sub=sub: e.tensor_scalar(
                        out=oc[:, sub, k, :], in0=pU0[:, c0 + 129:c0 + 193], scalar1=ru[:, k:k + 1], scalar2=None, op0=ALU.mult),
                        [pU0, ru], [oc])
                s.op("dve", lambda e, im=im, ab=ab: e.tensor_tensor(out=im[:], in0=im[:], in1=ab[:, 0, :], op=ALU.mult), [im, ab], [im])
                s.op("dve", lambda e, im=im, ab=ab: e.tensor_tensor(out=im[:], in0=im[:], in1=ab[:, 1, :], op=ALU.add), [im, ab], [im])
                m8 = mx.next()
                s2 = sc2.next()
                s.op("dve", lambda e, im=im, m8=m8: e.max(out=m8[:, 0:8], in_=im[:]), [im], [m8])
                s.op("dve", lambda e, im=im, m8=m8, s2=s2: e.match_replace(out=s2[:], in_to_replace=m8[:, 0:8], in_values=im[:],
                                                                           imm_value=-2.0), [im, m8], [s2])
                s.op("dve", lambda e, m8=m8, s2=s2: e.max(out=m8[:, 8:16], in_=s2[:]), [s2, m8], [m8])
                s.op("dve", lambda e, im=im, m8=m8, s2=s2: e.tensor_scalar(out=s2[:], in0=im[:], scalar1=m8[:, 15:16], scalar2=None,
                                                                           op0=ALU.is_ge), [im, m8], [s2])
                nm = NM.next()
                s.op("dve", lambda e, s2=s2, nm=nm: e.tensor_scalar(out=nm[:, 64:192], in0=s2[:], scalar1=1.0, scalar2=-NEGM,
                                                                    op0=ALU.subtract, op1=ALU.mult), [s2], [nm])
                for hf in range(nhf):
                    s.op("pe", lambda e, nm=nm, hf=hf: e.matmul(pL[:, hf * 128:(hf + 1) * 128], lhsT=nm[:, hf * 64:hf * 64 + 128],
                                                                 rhs=identb[:], start=True, stop=True), [nm, identb], [pL])
                for hf in range(nhf):
                    for k in range(2):
                        q = qa[k][hf]
                        s.op("act", lambda e, q=q, hf=hf, sub=sub: e.copy(out=q[64:128, sub * 128:(sub + 1) * 128],
                                                                          in_=pL[64:128, hf * 128:(hf + 1) * 128]), [pL], [q])
            for k in range(2):
                for kt in range(nkt):
                    j = kt - 4 * i
                    hf = 0 if kt < 32 else 1
                    q = qa[k][hf]
                    ps = pS.next()
                    s.op("pe", lambda e, ps=ps, kt=kt, q=q: e.matmul(ps[:], lhsT=ksA[:, kt * 128:(kt + 1) * 128], rhs=q[:],
                                                                      start=True, stop=True), [ksA, q], [ps])
                    if j >= -1:
                        y0 = 384 - 128 * j
                        p = exp_tile(ps, j, (cstrip[k], cstrip[k][:, y0:y0 + 512]), None)
                    else:
                        p = exp_tile(ps, j, None, cb[:, k:k + 1])
                    s.op("pe", lambda e, p=p, kt=kt, nkt=nkt: e.matmul(pOa[0:65, :], lhsT=vsA[:, kt, :], rhs=p[:],
                                                                        start=(kt == 0), stop=(kt == nkt - 1)), [vsA, p], [pOa])
                kw0 = max(0, 4 * i - 4)
                q = qa[k][0]
                for kt in range(kw0, nkt):
                    j = kt - 4 * i
                    ps = pS.next()
                    s.op("pe", lambda e, ps=ps, kt=kt, q=q: e.matmul(ps[:], lhsT=kwT[:, kt * 128:(kt + 1) * 128], rhs=q[0:64, :],
                                                                      start=True, stop=True), [kwT, q], [ps])
                    y0 = 384 - 128 * j
                    p = exp_tile(ps, j, (wstrip[k], wstrip[k][:, y0:y0 + 512]), None)
                    s.op("pe", lambda e, p=p, kt=kt, kw0=kw0, nkt=nkt: e.matmul(pOb[0:65, :], lhsT=vwA[:, kt, :], rhs=p[:],
                                                                                 start=(kt == kw0), stop=(kt == nkt - 1)), [vwA, p], [pOb])
                a = acc.next()
                for bi, (po, pu, gi) in enumerate(((pOa, pU0, 1), (pOb, pU1, 2))):
                    osb = Osb.next()
                    s.op("dve" if bi == 0 else "act",
                         (lambda e, osb=osb, po=po: e.tensor_copy(out=osb[0:65, :], in_=po[0:65, :])) if bi == 0 else
                         (lambda e, osb=osb, po=po: e.copy(out=osb[0:65, :], in_=po[0:65, :])), [po], [osb])
                    for sub in range(4):
                        s.op("pe", lambda e, sub=sub, osb=osb, pu=pu: e.transpose(
                            out=pu[:, sub * 65:(sub + 1) * 65], in_=osb[0:65, sub * 128:(sub + 1) * 128], identity=identf[0:65, 0:65]),
                            [osb, identf], [pu])
                    puv = pu[:, 0:260].rearrange("p (s c) -> p s c", c=65)
                    r2 = rs2.next()
                    w = wg.next()
                    s.op("dve", lambda e, r2=r2, puv=puv: e.reciprocal(out=r2[:], in_=puv[:, :, 64:65]), [pu], [r2])
                    s.op("dve", lambda e, r2=r2, w=w, k=k, gi=gi, i=i: e.tensor_tensor(
                        out=w[:], in0=r2[:], in1=gates[:, 4 * i:4 * i + 4, k * 3 + gi:k * 3 + gi + 1], op=ALU.mult), [r2, gates], [w])
                    if bi == 0:
                        s.op("dve", lambda e, a=a, puv=puv, w=w: e.tensor_tensor(out=a[:], in0=puv[:, :, 0:64],
                                                                                 in1=w[:].to_broadcast([128, 4, 64]), op=ALU.mult), [pu, w], [a])
                    else:
                        t2 = t2b.next()
                        s.op("dve", lambda e, t2=t2, puv=puv, w=w: e.tensor_tensor(out=t2[:], in0=puv[:, :, 0:64],
                                                                                   in1=w[:].to_broadcast([128, 4, 64]), op=ALU.mult), [pu, w], [t2])
                        s.op("pool", lambda e, a=a, t2=t2: e.tensor_tensor(out=a[:], in0=a[:], in1=t2[:], op=ALU.add), [a, t2], [a])
                t3 = t2b.next()
                s.op("pool", lambda e, t3=t3, oc=oc, k=k, i=i: e.tensor_tensor(
                    out=t3[:], in0=oc[:, :, k, :], in1=gates[:, 4 * i:4 * i + 4, k * 3:k * 3 + 1].to_broadcast([128, 4, 64]), op=ALU.mult),
                    [oc, gates], [t3])
                s.op("pool", lambda e, t3=t3, a=a, oo=oo, k=k: e.tensor_tensor(out=oo[:, :, k * 64:(k + 1) * 64], in0=a[:], in1=t3[:],
                                                                              op=ALU.add), [a, t3], [oo])
        s.dma("sp", ovn[i], oo[:, :, 0:128], reads=[oo], writes=[ob])
    if do_diff:
        dkT = kcd
        s.dma("sp", kcd[:], dr["dkT"], writes=[kcd])
        dvA = s.buf(ksd.t[:, :].rearrange("p (t c) -> p t c", c=128), "dvA_view")
        s.dma("sp", dvA[:], dr["dvA"].rearrange("(t p) c -> p t c", p=128), writes=[ksd])
    for i in range(nch if do_diff else 0):
        oo = ost.next()
        nkt = 4 * i + 4
        if do_diff:
            dqc = dq.next()
            s.dma("sp", dqc[:], dr["dqT"][:, i * 512:(i + 1) * 512], writes=[dqc])
            pOm = [pOa, pOb]
            for m in range(2):
                lo = 64 * m
                for kt in range(nkt):
                    j = kt - 4 * i
                    ps = pS.next()
                    s.op("pe", lambda e, ps=ps, kt=kt, lo=lo, dqc=dqc: e.matmul(
                        ps[:], lhsT=dkT[lo:lo + 64, kt * 128:(kt + 1) * 128], rhs=dqc[lo:lo + 64, :], start=True, stop=True),
                        [dkT, dqc], [ps])
                    if j >= -1:
                        y0 = 384 - 128 * j
                        p = exp_tile(ps, j, (cstrip[2], cstrip[2][:, y0:y0 + 512]), None)
                    else:
                        p = exp_tile(ps, j, None, cb[:, 4:5])
                    s.op("pe", lambda e, p=p, kt=kt, m=m, nkt=nkt: e.matmul(pOm[m][:], lhsT=dvA[:, kt, :], rhs=p[:],
                                                                             start=(kt == 0), stop=(kt == nkt - 1)), [ksd, p], [pOm[m]])
                    s.op("pe", lambda e, p=p, kt=kt, m=m, nkt=nkt: e.matmul(
                        pL[0:2, :], lhsT=ones2[:, m, :], rhs=p[:], start=(m == 0 and kt == 0), stop=(m == 1 and kt == nkt - 1)),
                        [ones2, p], [pL])
            od = [Od.next(), Od.next()]
            s.op("dve", lambda e, od=od: e.tensor_copy(out=od[0][:], in_=pOa[:]), [pOa], [od[0]])
            s.op("act", lambda e, od=od: e.copy(out=od[1][:], in_=pOb[:]), [pOb], [od[1]])
            s.op("act", lambda e: e.copy(out=sd[:], in_=pL[0:2, :]), [pL], [sd])
            pUm = [pU0, pU1]
            for m in range(2):
                for sub in range(4):
                    s.op("pe", lambda e, m=m, sub=sub, od=od: e.transpose(
                        out=pUm[m][:, sub * 128:(sub + 1) * 128], in_=od[m][:, sub * 128:(sub + 1) * 128], identity=identf[:]),
                        [od[m], identf], [pUm[m]])
            for sub in range(4):
                s.op("pe", lambda e, sub=sub: e.transpose(out=pL[:, sub * 2:(sub + 1) * 2], in_=sd[0:2, sub * 128:(sub + 1) * 128],
                                                          identity=identf[0:2, 0:2]), [sd, identf], [pL])
            r = rd.next()
            s.op("dve", lambda e, r=r: e.reciprocal(out=r[:], in_=pL[:, 0:8].rearrange("p (s m) -> p s m", m=2)), [pL], [r])
            s.op("dve", lambda e, r=r: e.tensor_scalar(out=r[:, :, 1:2], in0=r[:, :, 1:2], scalar1=nlam[:, 0:1], scalar2=None,
                                                        op0=ALU.mult), [r, nlam], [r])
            o0b = o0.next()
            a = av.next()
            pv0 = pU0[:].rearrange("p (s c) -> p s c", c=128)
            pv1 = pU1[:].rearrange("p (s c) -> p s c", c=128)
            s.op("dve", lambda e, o0b=o0b, r=r, pv0=pv0: e.tensor_tensor(out=o0b[:], in0=pv0, in1=r[:, :, 0:1].to_broadcast([128, 4, 128]),
                                                                         op=ALU.mult), [pU0, r], [o0b])
            s.op("dve", lambda e, a=a, r=r, pv1=pv1: e.tensor_tensor(out=a[:], in0=pv1, in1=r[:, :, 1:2].to_broadcast([128, 4, 128]),
                                                                     op=ALU.mult), [pU1, r], [a])
            s.op("pool", lambda e, a=a, o0b=o0b: e.tensor_tensor(out=a[:], in0=a[:], in1=o0b[:], op=ALU.add), [a, o0b], [a])
            s.op("pool", lambda e, a=a: e.tensor_tensor(out=sq[:], in0=a[:], in1=a[:], op=ALU.mult), [a], [sq])
            sv = ssd.next()
            s.op("dve", lambda e, sv=sv: e.tensor_reduce(out=sv[:], in_=sq[:], axis=AX.X, op=ALU.add), [sq], [sv])
            s.op("dve", lambda e, sv=sv: e.tensor_scalar(out=sv[:], in0=sv[:], scalar1=1.0 / 128, scalar2=SUB_EPS, op0=ALU.mult,
                                                         op1=ALU.add), [sv], [sv])
            s.op("act", lambda e, sv=sv: e.activation(out=sv[:], in_=sv[:], func=AF.Sqrt), [sv], [sv])
            s.op("dve", lambda e, sv=sv: e.reciprocal(out=sv[:], in_=sv[:]), [sv], [sv])
            s.op("dve", lambda e, a=a, sv=sv: e.tensor_tensor(out=a[:], in0=a[:], in1=sv[:].unsqueeze(2).to_broadcast([128, 4, 128]),
                                                              op=ALU.mult), [a, sv], [a])
            s.op("dve", lambda e, a=a, oo=oo: e.tensor_tensor(out=oo[:, :, 128:256], in0=a[:],
                                                              in1=gsub[:].unsqueeze(1).to_broadcast([128, 4, 128]), op=ALU.mult),
                 [a, gsub], [oo])
        s.dma("sp", ovd[i], oo[:, :, 128:256], reads=[oo], writes=[ob])
    s.finish_outputs([ob])


_CACHE = {}


def _prog(key, fn):
    if key not in _CACHE:
        _CACHE[key] = fn()
    return _CACHE[key]


def kernel(**inputs):
    inputs = {k: np.asarray(v) for k, v in inputs.items()}
    x = np.ascontiguousarray(inputs["x"], dtype=np.float32).reshape(8, NT, D)
    bt = inputs["bias_table"]
    identb = np.eye(128, dtype=np.float32).astype(NPBF)
    cores = list(range(8))
    nc1 = _prog("l1", lambda: build_token_phase(False, False, 2840, False, False, gate_rows=(1280, 1304)))
    r1 = run_bass_kernel_spmd(nc1, [{"x": x[c], "ident": identb, "g_mix": inputs["norm_mix"][0:1],
                                     "w_in": inputs["ev_w_in"][0]} for c in cores], core_ids=cores).results
    projT = [np.concatenate([r1[b * 4 + q]["projT"] for q in range(4)], axis=1) for b in range(2)]
    gT = [np.concatenate([r1[b * 4 + q]["gT"] for q in range(4)], axis=1) for b in range(2)]
    nc2 = _prog("l2", lambda: build_l2(16, True, True))
    in2 = []
    for c in cores:
        b, g, half, hd = c // 4, (c % 4) // 2, c % 2, c % 4
        in2.append(l2_host_inputs(projT[b], gT[b], inputs, g, half, hd))
    r2 = run_bass_kernel_spmd(nc2, in2, core_ids=cores).results
    o0 = np.zeros((2, S, D), dtype=NPBF)
    for c in cores:
        b, g, half, hd = c // 4, (c % 4) // 2, c % 2, c % 4
        oc = r2[c]["o"]
        for k in range(2):
            h = g * 4 + 2 * half + k
            o0[b, :, h * 64:(h + 1) * 64] = oc[:, k * 64:(k + 1) * 64]
        o0[b, :, 512 + hd * 128:512 + (hd + 1) * 128] = oc[:, 128:256]
    o0 = o0.reshape(8, NT, D)
    nc3 = _prog("l3", lambda: build_token_phase(True, True, 3072, False, True))
    r3 = run_bass_kernel_spmd(nc3, [{"x": x[c], "ident": identb, "oT": np.ascontiguousarray(o0[c].T),
                                     "w_out": inputs["ev_w_out"][0], "g_mlp": inputs["norm_mlp"][0:1],
                                     "w1": inputs["mlp_w1"][0], "w2": inputs["mlp_w2"][0],
                                     "g_mix": inputs["norm_mix"][1:2], "w_in": inputs["od_w_in"][0]}
                                    for c in cores], core_ids=cores).results
    x1 = [r3[c]["x_o"] for c in cores]
    proj1T = [np.concatenate([r3[b * 4 + q]["projT"] for q in range(4)], axis=1) for b in range(2)]
    nc4 = _prog("l4", lambda: build_moba(16, 4))
    in4 = [moba_host_inputs(proj1T[c // 4], bt, [4 * (c % 4) + k for k in range(4)]) for c in cores]
    r4 = run_bass_kernel_spmd(nc4, in4, core_ids=cores).results
    o1 = np.zeros((2, S, D), dtype=NPBF)
    for c in cores:
        o1[c // 4, :, (c % 4) * 256:(c % 4 + 1) * 256] = r4[c]["o"]
    o1 = o1.reshape(8, NT, D)
    nc5 = _prog("l5", lambda: build_token_phase(True, True, 0, True, False))
    r5 = run_bass_kernel_spmd(nc5, [{"x": x1[c], "ident": identb, "oT": np.ascontiguousarray(o1[c].T),
                                     "w_out": inputs["od_w_out"][0], "g_mlp": inputs["norm_mlp"][1:2],
                                     "w1": inputs["mlp_w1"][1], "w2": inputs["mlp_w2"][1],
                                     "g_fin": inputs["norm_final"].reshape(1, D)} for c in cores], core_ids=cores).results
    out = np.stack([r5[c]["out"] for c in cores]).reshape(2, S, D).astype(np.float32)
    return out
```

```python
import numpy as np
import ml_dtypes
from contextlib import ExitStack
import concourse.bass as bass
import concourse.mybir as mybir
from concourse.bass_utils import run_bass_kernel_spmd

F32 = mybir.dt.float32
BF16 = mybir.dt.bfloat16
AF = mybir.ActivationFunctionType
ALU = mybir.AluOpType
AX = mybir.AxisListType
NPBF = ml_dtypes.bfloat16


class Buf:
    __slots__ = ("name", "t", "lw", "rd", "excl")

    def __init__(self, name, t, excl=False):
        self.name = name
        self.t = t
        self.excl = excl
        self.lw = None
        self.rd = {}

    def __getitem__(self, idx):
        return self.t[idx]


class Sched:
    ENGS = ("pe", "act", "dve", "pool", "sp")
    EPOCH = 12000
    NDMA = 24

    def __init__(self, nc, es):
        self.nc = nc
        self.es = es
        self.ops = {e: [] for e in self.ENGS}
        self.n = {e: 0 for e in self.ENGS}
        self.waited = {e: {} for e in self.ENGS}
        self.signal = {e: set() for e in self.ENGS}
        self.dma_uses = [0] * self.NDMA
        self.dma_i = 0
        self.nbuf = 0

    def sbuf(self, name, shape, dtype):
        t = self.es.enter_context(self.nc.sbuf_tensor("sb_" + name, list(shape), dtype))
        return t

    def psum(self, name, shape, dtype):
        t = self.es.enter_context(self.nc.psum_tensor("ps_" + name, list(shape), dtype))
        return t

    def buf(self, t, name=None, excl=False):
        self.nbuf += 1
        return Buf(name or f"b{self.nbuf}", t, excl)

    def pbuf(self, name, shape, dtype):
        return self.buf(self.psum(name, shape, dtype), name, excl=True)

    def _need(self, eng, tok, same_raw=False):
        if tok is None:
            return
        stream, v = tok
        if stream == eng:
            if not same_raw:
                return
            if self.n[eng] - v > 3:
                return
        if self.waited[eng].get(stream, -1) >= v:
            return
        self.waited[eng][stream] = v
        if isinstance(stream, str):
            self.signal[stream].add(v)
        self.ops[eng].append(("w", tok))

    def _deps(self, eng, reads, writes):
        for b in reads:
            self._need(eng, b.lw, same_raw=True)
            if b.excl:
                for tok in b.rd.values():
                    self._need(eng, tok, same_raw=False)
        for b in writes:
            self._need(eng, b.lw, same_raw=False)
            for tok in b.rd.values():
                self._need(eng, tok, same_raw=False)

    def _commit(self, tok, reads, writes):
        stream = tok[0]
        for b in reads:
            b.rd[stream] = tok
        for b in writes:
            b.lw = tok
            b.rd = {}

    def op(self, eng, fn, reads=(), writes=()):
        self._deps(eng, reads, writes)
        idx = self.n[eng]
        self.n[eng] += 1
        self.ops[eng].append(("i", fn, idx))
        self._commit((eng, idx), reads, writes)

    def dma(self, eng, out, in_, reads=(), writes=(), **kw):
        k = self.dma_i % self.NDMA
        self.dma_i += 1
        stream = ("dma", k)
        prev = self.dma_uses[k]
        self._deps(eng, reads, writes)
        if prev > 0:
            self._need(eng, (stream, prev * 16))
        self.dma_uses[k] = prev + 1
        val = (prev + 1) * 16
        self.ops[eng].append(("d", (out, in_, kw), k, val))
        self._commit((stream, val), reads, writes)

    def finish_outputs(self, bufs, eng="sp"):
        for b in bufs:
            self._need(eng, b.lw)

    def emit(self):
        nc = self.nc
        engobj = {"pe": nc.tensor, "act": nc.scalar, "dve": nc.vector,
                  "pool": nc.gpsimd, "sp": nc.sync}
        sigval = {}
        sems = {}
        for e in self.ENGS:
            cnt = 0
            for idx in sorted(self.signal[e]):
                ep = cnt // self.EPOCH
                sigval[(e, idx)] = ((e, ep), cnt % self.EPOCH + 1)
                cnt += 1
                if (e, ep) not in sems:
                    sems[(e, ep)] = self.es.enter_context(nc.semaphore(f"s_{e}_{ep}"))
        for k in range(self.NDMA):
            if self.dma_uses[k]:
                sems[("dma", k)] = self.es.enter_context(nc.semaphore(f"s_dma_{k}"))
        self.nsems = len(sems)
        block = self.es.enter_context(nc.Block())
        deco = {"pe": block.tensor, "act": block.scalar, "dve": block.vector,
                "pool": block.gpsimd, "sp": block.sync}

        def make(e):
            ops = self.ops[e]

            def body(eng):
                for o in ops:
                    if o[0] == "w":
                        stream, v = o[1]
                        if isinstance(stream, str):
                            sk, sv = sigval[(stream, v)]
                            eng.wait_ge(sems[sk], sv)
                        else:
                            eng.wait_ge(sems[stream], v)
                    elif o[0] == "i":
                        ins = o[1](eng)
                        sv = sigval.get((e, o[2]))
                        if sv is not None:
                            ins.then_inc(sems[sv[0]], 1)
                    else:
                        out, in_, kw = o[1]
                        eng.dma_start(out=out, in_=in_, **kw).then_inc(sems[("dma", o[2])], 16)
            return body

        for e in self.ENGS:
            if self.ops[e]:
                deco[e](make(e))


STAGE = 99

NT = 2048
D = 1024
EPS = 1e-6


class Ring:
    def __init__(self, items):
        self.items = list(items)
        self.i = 0

    def next(self):
        b = self.items[self.i % len(self.items)]
        self.i += 1
        return b


def build_token_phase(attn_in, mlp, proj_cols, final, x_out, gate_rows=None):
    nc = bass.Bass("TRN2", target_bir_lowering=False)
    dr = {}
    dr["x"] = nc.dram_tensor("x", [NT, D], F32, kind="ExternalInput").ap()
    dr["ident"] = nc.dram_tensor("ident", [128, 128], BF16, kind="ExternalInput").ap()
    if attn_in:
        dr["oT"] = nc.dram_tensor("oT", [D, NT], BF16, kind="ExternalInput").ap()
        dr["w_out"] = nc.dram_tensor("w_out", [D, D], F32, kind="ExternalInput").ap()
    if mlp:
        dr["g_mlp"] = nc.dram_tensor("g_mlp", [1, D], F32, kind="ExternalInput").ap()
        dr["w1"] = nc.dram_tensor("w1", [D, 4 * D], F32, kind="ExternalInput").ap()
        dr["w2"] = nc.dram_tensor("w2", [4 * D, D], F32, kind="ExternalInput").ap()
    if proj_cols:
        dr["g_mix"] = nc.dram_tensor("g_mix", [1, D], F32, kind="ExternalInput").ap()
        dr["w_in"] = nc.dram_tensor("w_in", [D, proj_cols], F32, kind="ExternalInput").ap()
        dr["projT"] = nc.dram_tensor("projT", [proj_cols, NT], BF16, kind="ExternalOutput").ap()
        if gate_rows:
            dr["gT"] = nc.dram_tensor("gT", [gate_rows[1] - gate_rows[0], NT], F32, kind="ExternalOutput").ap()
    if final:
        dr["g_fin"] = nc.dram_tensor("g_fin", [1, D], F32, kind="ExternalInput").ap()
        dr["out"] = nc.dram_tensor("out", [NT, D], F32, kind="ExternalOutput").ap()
    if x_out:
        dr["x_o"] = nc.dram_tensor("x_o", [NT, D], F32, kind="ExternalOutput").ap()

    es = ExitStack()
    with es:
        s = Sched(nc, es)
        token_phase_body(nc, s, dr, attn_in, mlp, proj_cols, final, x_out, gate_rows)
        s.emit()
    return nc


def token_phase_body(nc, s, dr, attn_in, mlp, proj_cols, final, x_out, gate_rows):
    NTT = NT // 128
    xs = s.sbuf("xs", [128, NTT, D], F32)
    xb = [s.buf(xs[:, t, :], f"x{t}") for t in range(NTT)]
    hnT_t = s.sbuf("hnT", [128, 8, NT], BF16)
    hnT = [s.buf(hnT_t[:, :, t * 128:(t + 1) * 128], f"hnT{t}") for t in range(NTT)]
    ident = s.buf(s.sbuf("ident", [128, 128], BF16), "ident")
    gbuf = s.buf(s.sbuf("gbuf", [128, D], F32), "gbuf")
    ss = s.buf(s.sbuf("ss", [128, NTT], F32), "ss")
    rstd = s.buf(s.sbuf("rstd", [128, NTT], F32), "rstd")
    junk = Ring([s.buf(s.sbuf(f"junk{i}", [128, D], BF16), f"junk{i}") for i in range(2)])
    hnb = Ring([s.buf(s.sbuf(f"hnb{i}", [128, D], BF16), f"hnb{i}") for i in range(2)])
    pT = Ring([s.pbuf(f"pT{i}", [128, 8, 128], BF16) for i in range(2)])
    pm = Ring([s.pbuf(f"pm{i}", [128, 512], F32) for i in range(5)])
    wa_t = [s.sbuf(f"wa{i}", [128, 8, 512], BF16) for i in range(2)]
    wa = Ring([s.buf(t, f"wa{i}") for i, t in enumerate(wa_t)])
    evac = Ring(["act", "dve"])

    s.dma("sp", ident[:], dr["ident"], writes=[ident])
    xv = dr["x"].rearrange("(t p) d -> t p d", p=128)
    for t in range(NTT):
        s.dma("sp", xb[t][:], xv[t], writes=[xb[t]])

    def rmsnorm_to_hnT(g_ap):
        s.dma("sp", gbuf[:], g_ap.to_broadcast([128, D]), writes=[gbuf])
        for t in range(NTT):
            j = junk.next()
            s.op("act", lambda e, t=t, j=j: e.activation(out=j[:], in_=xb[t][:], func=AF.Square,
                                                         accum_out=ss[:, t:t + 1]), [xb[t]], [j, ss])
        s.op("dve", lambda e: e.tensor_scalar(out=rstd[:], in0=ss[:], scalar1=1.0 / D, scalar2=EPS,
                                              op0=ALU.mult, op1=ALU.add), [ss], [rstd])
        s.op("act", lambda e: e.activation(out=rstd[:], in_=rstd[:], func=AF.Sqrt), [rstd], [rstd])
        s.op("dve", lambda e: e.reciprocal(out=rstd[:], in_=rstd[:]), [rstd], [rstd])

    def hn_transposes():
        if STAGE < 2: return
        for t in range(NTT):
            h = hnb.next()
            s.op("dve", lambda e, t=t, h=h: e.scalar_tensor_tensor(out=h[:], in0=xb[t][:], scalar=rstd[:, t:t + 1],
                                                                   in1=gbuf[:], op0=ALU.mult, op1=ALU.mult),
                 [xb[t], rstd, gbuf], [h])
            if STAGE < 3: continue
            p = pT.next()
            for k in range(8):
                s.op("pe", lambda e, k=k, h=h, p=p: e.transpose(out=p[:, k, :], in_=h[:, k * 128:(k + 1) * 128],
                                                                identity=ident[:]), [h, ident], [p])
            s.op("act", lambda e, t=t, p=p: e.copy(out=hnT[t][:], in_=p[:]), [p], [hnT[t]])

    if attn_in:
        oT = hnT
        ov = dr["oT"].rearrange("(k p) t -> p k t", p=128)
        for t in range(NTT):
            s.dma("sp", oT[t][:], ov[:, :, t * 128:(t + 1) * 128], writes=[oT[t]])
        wo_t = s.sbuf("wo", [128, 8, D], BF16)
        wo = s.buf(wo_t, "wo")
        s.dma("pool", wo[:], dr["w_out"].rearrange("(k p) n -> p k n", p=128), writes=[wo])
        for t in range(NTT):
            for half in range(2):
                p = pm.next()
                for k in range(8):
                    s.op("pe", lambda e, t=t, half=half, k=k, p=p: e.matmul(
                        p[:], lhsT=oT[t][:, k, :], rhs=wo[:, k, half * 512:(half + 1) * 512],
                        start=(k == 0), stop=(k == 7)), [oT[t], wo], [p])
                s.op("dve", lambda e, t=t, half=half, p=p: e.tensor_tensor(
                    out=xb[t][:, half * 512:(half + 1) * 512], in0=xb[t][:, half * 512:(half + 1) * 512],
                    in1=p[:], op=ALU.add), [p, xb[t]], [xb[t]])

    if mlp:
        rmsnorm_to_hnT(dr["g_mlp"])
        hn_transposes()
        wb_t = [s.sbuf(f"wb{i}", [128, 4, D], BF16) for i in range(2)]
        wb = Ring([s.buf(t, f"wb{i}") for i, t in enumerate(wb_t)])
        hT_t = s.sbuf("hT", [128, 4, NT], BF16)
        hT = [[s.buf(hT_t[:, f, c * 512:(c + 1) * 512], f"hT{f}_{c}") for c in range(4)] for f in range(4)]
        rl = Ring([s.buf(s.sbuf(f"rl{i}", [128, 512], F32), f"rl{i}") for i in range(3)])
        w1v = dr["w1"].rearrange("(k p) f -> p k f", p=128)
        w2v = dr["w2"].rearrange("(g f p) n -> g p f n", p=128, f=4)
        NG = 8
        for g in range(NG):
            a = wa.next()
            b = wb.next()
            s.dma("pool", a[:], w1v[:, :, g * 512:(g + 1) * 512], writes=[a])
            s.dma("pool", b[:], w2v[g], writes=[b])
            for c in range(4):
                for f in range(4):
                    p = pm.next()
                    for k in range(8):
                        s.op("pe", lambda e, k=k, f=f, c=c, p=p, a=a: e.matmul(
                            p[:], lhsT=a[:, k, f * 128:(f + 1) * 128], rhs=hnT_t[:, k, c * 512:(c + 1) * 512],
                            start=(k == 0), stop=(k == 7)), [a] + hnT[c * 4:(c + 1) * 4], [p])
                    r = rl.next()
                    s.op("act", lambda e, p=p, r=r: e.activation(out=r[:], in_=p[:], func=AF.Relu), [p], [r])
                    s.op("dve", lambda e, r=r, f=f, c=c: e.tensor_tensor(out=hT[f][c][:], in0=r[:], in1=r[:],
                                                                         op=ALU.mult), [r], [hT[f][c]])
            for t in range(NTT):
                c = t // 4
                for half in range(2):
                    p = pm.next()
                    for f in range(4):
                        s.op("pe", lambda e, t=t, f=f, half=half, p=p, b=b: e.matmul(
                            p[:], lhsT=hT_t[:, f, t * 128:(t + 1) * 128], rhs=b[:, f, half * 512:(half + 1) * 512],
                            start=(f == 0), stop=(f == 3)), [hT[f][c], b], [p])
                    s.op("dve", lambda e, t=t, half=half, p=p: e.tensor_tensor(
                        out=xb[t][:, half * 512:(half + 1) * 512], in0=xb[t][:, half * 512:(half + 1) * 512],
                        in1=p[:], op=ALU.add), [p, xb[t]], [xb[t]])

    if x_out:
        xo = s.buf(dr["x_o"], "x_o")
        xov = dr["x_o"].rearrange("(t p) d -> t p d", p=128)
        for t in range(NTT):
            s.dma("sp", xov[t], xb[t][:], reads=[xb[t]], writes=[xo])
        s.finish_outputs([xo])

    if proj_cols:
        rmsnorm_to_hnT(dr["g_mix"])
        hn_transposes()
        pj = s.buf(dr["projT"], "projT")
        outs = [pj]
        stage = Ring([s.buf(s.sbuf(f"stg{i}", [128, NT], BF16), f"stg{i}") for i in range(2)])
        if gate_rows:
            gt = s.buf(dr["gT"], "gT")
            outs.append(gt)
            gst = s.buf(s.sbuf("gst", [128, NT], F32), "gst")
        ncg = (proj_cols + 511) // 512
        if STAGE < 4: ncg = 0
        if STAGE == 4: ncg = 1
        if STAGE == 5: ncg = 5
        if STAGE == 6: ncg = 2
        if STAGE == 7: ncg = 3
        for cg in range(ncg):
            c0 = cg * 512
            cw = min(512, proj_cols - c0)
            a = wa.next()
            s.dma("pool", a[:, :, 0:cw], dr["w_in"].rearrange("(k p) n -> p k n", p=128)[:, :, c0:c0 + cw], writes=[a])
            for ct in range((cw + 127) // 128):
                m0 = ct * 128
                mw = min(128, cw - m0)
                st = stage.next()
                for c in range(4):
                    p = pm.next()
                    for k in range(8):
                        s.op("pe", lambda e, k=k, c=c, p=p, a=a, m0=m0, mw=mw: e.matmul(
                            p[0:mw, :], lhsT=a[:, k, m0:m0 + mw], rhs=hnT_t[:, k, c * 512:(c + 1) * 512],
                            start=(k == 0), stop=(k == 7)), [a] + hnT[c * 4:(c + 1) * 4], [p])
                    ev = evac.next()
                    if gate_rows and c0 + m0 == gate_rows[0]:
                        ev = "act"
                    if ev == "act":
                        s.op("act", lambda e, p=p, st=st, c=c, mw=mw: e.copy(out=st[0:mw, c * 512:(c + 1) * 512],
                                                                            in_=p[0:mw, :]), [p], [st])
                    else:
                        s.op("dve", lambda e, p=p, st=st, c=c, mw=mw: e.tensor_copy(out=st[0:mw, c * 512:(c + 1) * 512],
                                                                                   in_=p[0:mw, :]), [p], [st])
                    if gate_rows and c0 + m0 == gate_rows[0]:
                        ng = gate_rows[1] - gate_rows[0]
                        s.op("act", lambda e, p=p, c=c, ng=ng: e.copy(out=gst[0:ng, c * 512:(c + 1) * 512],
                                                                     in_=p[0:ng, :]), [p], [gst])
                s.dma("sp", dr["projT"][c0 + m0:c0 + m0 + mw, :], st[0:mw, :], reads=[st], writes=[pj])
                if gate_rows and c0 + m0 == gate_rows[0]:
                    ng = gate_rows[1] - gate_rows[0]
                    s.dma("sp", dr["gT"], gst[0:ng, :], reads=[gst], writes=[gt])
        s.finish_outputs(outs)

    if final:
        rmsnorm_to_hnT(dr["g_fin"])
        ob = s.buf(dr["out"], "out")
        ofin = Ring([s.buf(s.sbuf(f"ofin{i}", [128, D], F32), f"ofin{i}") for i in range(2)])
        outv = dr["out"].rearrange("(t p) d -> t p d", p=128)
        for t in range(NTT):
            o = ofin.next()
            s.op("dve", lambda e, t=t, o=o: e.scalar_tensor_tensor(out=o[:], in0=xb[t][:], scalar=rstd[:, t:t + 1],
                                                                   in1=gbuf[:], op0=ALU.mult, op1=ALU.mult),
                 [xb[t], rstd, gbuf], [o])
            s.dma("sp", outv[t], o[:], reads=[o], writes=[ob])
        s.finish_outputs([ob])


S = 8192
NEGM = -30000.0
NCH = 16


class TileStream:
    def __init__(self, s, la=2):
        self.s = s
        self.la = la
        self.pend = []
        self.n = 0

    def _drain(self):
        i = 0
        while i < len(self.pend):
            due, fn, arg = self.pend[i]
            if due <= self.n:
                self.pend.pop(i)
                fn(arg)
            else:
                i += 1

    def push(self, qk, pv):
        p = qk()
        self.pend.append((self.n + self.la + 1, pv, p))
        self.n += 1
        self._drain()

    def push_fin(self, fin_a, fin_b):
        due_a = self.n + self.la

        def run_a(_):
            x = fin_a()
            self.pend.append((due_a + 2, fin_b, x))
        self.pend.append((due_a, run_a, None))

    def flush(self):
        while self.pend:
            self.n += 1
            self._drain()


def t5_bucket_np(rel):
    n = np.maximum(rel, 0).astype(np.int32)
    nf = np.maximum(n, 1).astype(np.float32)
    large = 16 + (np.log(nf / np.float32(16)) / np.float32(np.log(128 / 16)) * np.float32(16)).astype(np.int32)
    large = np.minimum(large, 31)
    return np.where(n < 16, n, large)


def causal_strip(bias_table, h):
    p = np.arange(128)[:, None]
    y = np.arange(1024)[None, :]
    rel = y - p - 384
    v = bias_table[t5_bucket_np(rel), h].astype(np.float32)
    return np.where(rel >= 0, v, np.float32(NEGM)).astype(np.float32)


def moba_host_inputs(projT_b, bias_table, heads):
    qT = np.stack([projT_b[h * 64:(h + 1) * 64] for h in heads])
    ind = (np.arange(S)[None, :] // 256 == np.arange(32)[:, None]).astype(NPBF)
    kA = np.stack([np.concatenate([projT_b[1024 + h * 64:1024 + (h + 1) * 64], ind], 0) for h in heads])
    ones = np.ones((S, 1), NPBF)
    vA = np.stack([np.concatenate([np.ascontiguousarray(projT_b[2048 + h * 64:2048 + (h + 1) * 64].T), ones], 1)
                   for h in heads])
    strip = np.stack([causal_strip(bias_table, h) for h in heads])
    cb = np.broadcast_to(bias_table[31, heads][None, :], (128, 4)).astype(np.float32).copy()
    c = np.arange(32)[:, None]
    n = np.arange(32)[None, :]
    past01 = (n < c).astype(np.float32)
    own01 = (n == c).astype(np.float32)
    pastneg = np.where(n < c, 0.0, NEGM).astype(np.float32)
    cm = np.stack([pastneg, past01, own01])
    cm = np.broadcast_to(cm[None], (128, 3, 32, 32)).copy()
    return {"qT": qT, "kA": kA, "vA": vA, "strip": strip, "cb": cb, "cm": cm,
            "identb": np.eye(128, dtype=np.float32).astype(NPBF), "identf": np.eye(128, dtype=np.float32)}


def build_moba(nch=NCH, nheads=4):
    nc = bass.Bass("TRN2", target_bir_lowering=False)
    dr = {}
    dr["qT"] = nc.dram_tensor("qT", [4, 64, S], BF16, kind="ExternalInput").ap()
    dr["kA"] = nc.dram_tensor("kA", [4, 96, S], BF16, kind="ExternalInput").ap()
    dr["vA"] = nc.dram_tensor("vA", [4, S, 65], BF16, kind="ExternalInput").ap()
    dr["strip"] = nc.dram_tensor("strip", [4, 128, 1024], F32, kind="ExternalInput").ap()
    dr["cb"] = nc.dram_tensor("cb", [128, 4], F32, kind="ExternalInput").ap()
    dr["cm"] = nc.dram_tensor("cm", [128, 3, 32, 32], F32, kind="ExternalInput").ap()
    dr["identb"] = nc.dram_tensor("identb", [128, 128], BF16, kind="ExternalInput").ap()
    dr["identf"] = nc.dram_tensor("identf", [128, 128], F32, kind="ExternalInput").ap()
    dr["o"] = nc.dram_tensor("o", [S, 256], BF16, kind="ExternalOutput").ap()
    es = ExitStack()
    with es:
        s = Sched(nc, es)
        moba_body(nc, s, dr, nch, nheads)
        s.emit()
    return nc


def moba_body(nc, s, dr, nch, nheads):
    H = nheads
    kA = [s.buf(s.sbuf(f"kA{h}", [96, S], BF16), f"kA{h}") for h in range(H)]
    vA = [s.buf(s.sbuf(f"vA{h}", [128, 64, 65], BF16), f"vA{h}") for h in range(H)]
    strip = [s.buf(s.sbuf(f"strip{h}", [128, 1024], F32), f"strip{h}") for h in range(H)]
    cb = s.buf(s.sbuf("cb", [128, 4], F32), "cb")
    cm = s.buf(s.sbuf("cm", [128, 3, 32, 32], F32), "cm")
    identb = s.buf(s.sbuf("identb", [128, 128], BF16), "identb")
    identf = s.buf(s.sbuf("identf", [128, 128], F32), "identf")
    kmean = [s.buf(s.sbuf(f"kmean{h}", [64, 32], F32), f"kmean{h}") for h in range(H)]
    QA = Ring([s.buf(s.sbuf(f"QA{i}", [96, 512], BF16), f"QA{i}") for i in range(3)])
    qf = Ring([s.buf(s.sbuf(f"qf{i}", [64, 512], F32), f"qf{i}") for i in range(2)])
    gm = Ring([s.buf(s.sbuf(f"gm{i}", [128, 4, 32], F32), f"gm{i}") for i in range(2)])
    mx = Ring([s.buf(s.sbuf(f"mx{i}", [128, 4, 8], F32), f"mx{i}") for i in range(2)])
    sm = Ring([s.buf(s.sbuf(f"sm{i}", [128, 4, 32], F32), f"sm{i}") for i in range(2)])
    NM = Ring([s.buf(s.sbuf(f"NM{i}", [128, 4, 128], BF16), f"NM{i}") for i in range(2)])
    P = Ring([s.buf(s.sbuf(f"P{i}", [128, 512], BF16), f"P{i}") for i in range(4)])
    tmp = Ring([s.buf(s.sbuf(f"tmp{i}", [128, 512], F32), f"tmp{i}") for i in range(2)])
    Osb = Ring([s.buf(s.sbuf(f"Osb{i}", [65, 512], F32), f"Osb{i}") for i in range(2)])
    rs = Ring([s.buf(s.sbuf(f"rs{i}", [128, 4, 1], F32), f"rs{i}") for i in range(2)])
    ost = Ring([s.buf(s.sbuf(f"ost{i}", [128, 4, 256], BF16), f"ost{i}") for i in range(2)])
    pS = Ring([s.pbuf(f"pS{i}", [128, 512], F32) for i in range(3)])
    pO = Ring([s.pbuf(f"pO{i}", [128, 512], F32) for i in range(2)])
    pA = s.pbuf("pA", [128, 512], F32)
    pB = s.pbuf("pB", [128, 512], F32)
    ob = s.buf(dr["o"], "o")

    s.dma("sp", identb[:], dr["identb"], writes=[identb])
    s.dma("sp", identf[:], dr["identf"], writes=[identf])
    s.dma("sp", cb[:], dr["cb"], writes=[cb])
    s.dma("sp", cm[:], dr["cm"], writes=[cm])
    for h in range(H):
        s.dma("sp", kA[h][:], dr["kA"][h], writes=[kA[h]])
        s.dma("sp", vA[h][:], dr["vA"][h].rearrange("(t p) c -> p t c", p=128), writes=[vA[h]])
        s.dma("sp", strip[h][:], dr["strip"][h], writes=[strip[h]])
    for nm in NM.items:
        s.op("pool", lambda e, nm=nm: e.memset(nm[:], 0.0), [], [nm])
    for h in range(H):
        s.op("dve", lambda e, h=h: e.tensor_reduce(out=kmean[h][:], in_=kA[h][0:64, :].rearrange("p (n k) -> p n k", k=256),
                                                   axis=AX.X, op=ALU.add), [kA[h]], [kmean[h]])
        s.op("dve", lambda e, h=h: e.tensor_scalar(out=kmean[h][:], in0=kmean[h][:], scalar1=1.0 / 256, scalar2=None,
                                                   op0=ALU.mult), [kmean[h]], [kmean[h]])

    ov = dr["o"].rearrange("(c s p) d -> c p s d", p=128, s=4)
    pF = s.pbuf("pF", [128, 512], F32)
    units = [(i, h) for i in range(nch) for h in range(H)]
    state = {}

    def pre_a(u):
        i, h = units[u]
        qa = QA.next()
        s.dma("sp", qa[0:64, :], dr["qT"][h][:, i * 512:(i + 1) * 512], writes=[qa])
        q32 = qf.next()
        s.op("act", lambda e, qa=qa, q32=q32: e.copy(out=q32[:], in_=qa[0:64, :]), [qa], [q32])
        for sub in range(4):
            s.op("pe", lambda e, sub=sub, q32=q32, h=h: e.matmul(
                pA[:, sub * 32:(sub + 1) * 32], lhsT=q32[:, sub * 128:(sub + 1) * 128], rhs=kmean[h][:],
                start=True, stop=True), [q32, kmean[h]], [pA])
        g = gm.next()
        m8 = mx.next()
        sel = sm.next()
        nm = NM.next()
        for sub in range(4):
            c = (i * 4 + sub) // 2
            s.op("dve", lambda e, sub=sub, c=c, g=g: e.tensor_tensor(
                out=g[:, sub, :], in0=pA[:, sub * 32:(sub + 1) * 32], in1=cm[:, 0, c, :], op=ALU.add),
                [pA, cm], [g])
        for sub in range(4):
            s.op("dve", lambda e, sub=sub, g=g, m8=m8: e.max(out=m8[:, sub, :], in_=g[:, sub, :]), [g], [m8])
        for sub in range(4):
            c = (i * 4 + sub) // 2
            s.op("dve", lambda e, sub=sub, g=g, m8=m8, sel=sel: e.tensor_scalar(
                out=sel[:, sub, :], in0=g[:, sub, :], scalar1=m8[:, sub, 2:3], scalar2=None, op0=ALU.is_ge),
                [g, m8], [sel])
            s.op("dve", lambda e, sub=sub, c=c, sel=sel: e.tensor_tensor(
                out=sel[:, sub, :], in0=sel[:, sub, :], in1=cm[:, 1, c, :], op=ALU.mult), [sel, cm], [sel])
            s.op("dve", lambda e, sub=sub, c=c, sel=sel: e.tensor_tensor(
                out=sel[:, sub, :], in0=sel[:, sub, :], in1=cm[:, 2, c, :], op=ALU.add), [sel, cm], [sel])
        s.op("dve", lambda e, sel=sel, nm=nm: e.tensor_scalar(
            out=nm[:, :, 64:96], in0=sel[:], scalar1=1.0, scalar2=-NEGM, op0=ALU.subtract, op1=ALU.mult),
            [sel], [nm])
        state[u] = (qa, nm)

    def pre_b(u):
        qa, nm = state[u]
        for sub in range(4):
            s.op("pe", lambda e, sub=sub, nm=nm: e.matmul(
                pB[:, sub * 128:(sub + 1) * 128], lhsT=nm[:, sub, :], rhs=identb[:], start=True, stop=True),
                [nm, identb], [pB])
        s.op("act", lambda e, qa=qa: e.copy(out=qa[64:96, :], in_=pB[64:96, :]), [pB], [qa])

    stream = TileStream(s, la=2)
    osb_out = None
    pre_a(0)
    pre_b(0)
    for u, (i, h) in enumerate(units):
        if h == 0:
            osb_out = ost.next()
        qa, _ = state[u]
        po = pO.next()
        nkt = 4 * i + 4
        if u + 1 < len(units):
            pre_a(u + 1)
        for kt in range(nkt):
            j = kt - 4 * i

            def qk(kt=kt, j=j, qa=qa, h=h):
                ps = pS.next()
                s.op("pe", lambda e, kt=kt, ps=ps, qa=qa, h=h: e.matmul(
                    ps[:], lhsT=kA[h][:, kt * 128:(kt + 1) * 128], rhs=qa[:], start=True, stop=True),
                    [kA[h], qa], [ps])
                p = P.next()
                if j >= -1:
                    y0 = 384 - 128 * j
                    t = tmp.next()
                    s.op("dve", lambda e, ps=ps, t=t, h=h, y0=y0: e.scalar_tensor_tensor(
                        out=t[:], in0=ps[:], scalar=0.125, in1=strip[h][:, y0:y0 + 512], op0=ALU.mult, op1=ALU.add),
                        [ps, strip[h]], [t])
                    s.op("act", lambda e, t=t, p=p: e.activation(out=p[:], in_=t[:], func=AF.Exp), [t], [p])
                else:
                    s.op("act", lambda e, ps=ps, p=p, h=h: e.activation(out=p[:], in_=ps[:], func=AF.Exp,
                                                                        bias=cb[:, h:h + 1], scale=0.125),
                         [ps, cb], [p])
                return p

            def pv(p, kt=kt, po=po, h=h, nkt=nkt):
                s.op("pe", lambda e, kt=kt, p=p, po=po, h=h, nkt=nkt: e.matmul(
                    po[0:65, :], lhsT=vA[h][:, kt, :], rhs=p[:], start=(kt == 0), stop=(kt == nkt - 1)),
                    [vA[h], p], [po])
            stream.push(qk, pv)
            if kt == min(7, nkt - 1) and u + 1 < len(units):
                pre_b(u + 1)

        def fin_a(po=po):
            osb = Osb.next()
            s.op("dve", lambda e, osb=osb, po=po: e.tensor_copy(out=osb[:], in_=po[0:65, :]), [po], [osb])
            return osb

        def fin_b(osb, h=h, osb_out=osb_out, i=i):
            for sub in range(4):
                s.op("pe", lambda e, sub=sub, osb=osb: e.transpose(
                    out=pF[:, sub * 65:(sub + 1) * 65], in_=osb[:, sub * 128:(sub + 1) * 128],
                    identity=identf[0:65, 0:65]), [osb, identf], [pF])
            r = rs.next()
            pav = pF[:, 0:260].rearrange("p (s c) -> p s c", c=65)
            s.op("dve", lambda e, r=r, pav=pav: e.reciprocal(out=r[:], in_=pav[:, :, 64:65]), [pF], [r])
            s.op("dve", lambda e, r=r, pav=pav, h=h, osb_out=osb_out: e.tensor_tensor(
                out=osb_out[:, :, h * 64:(h + 1) * 64], in0=pav[:, :, 0:64], in1=r[:].to_broadcast([128, 4, 64]),
                op=ALU.mult), [pF, r], [osb_out])
            if h == H - 1:
                s.dma("sp", ov[i], osb_out[:], reads=[osb_out], writes=[ob])
        stream.push_fin(fin_a, fin_b)
    stream.flush()
    s.finish_outputs([ob])


LAM_INIT0 = 0.8 - 0.6 * 1.0
SUB_EPS = 1e-6
GELU_C = 0.7978845608028654


def window_strip(bias_table, h):
    p = np.arange(128)[:, None]
    y = np.arange(1408)[None, :]
    rel = y - p - 384
    v = bias_table[t5_bucket_np(rel), h].astype(np.float32)
    return np.where((rel >= 0) & (rel < 512), v, np.float32(NEGM)).astype(np.float32)


def cmp_bias(bias_table, h):
    out = []
    p = np.arange(128)[:, None]
    x = np.arange(512)[None, :]
    for o in range(0, 2560, 512):
        rel = o + x - 16 * p - 31
        v = bias_table[t5_bucket_np(rel), h].astype(np.float32)
        out.append(np.where(rel >= 0, v, np.float32(NEGM)))
    return np.stack(out).astype(np.float32)


def l2_host_inputs(projT_b, gT_b, inputs, g, half, hd):
    bt = inputs["bias_table"]
    own = [2 * half, 2 * half + 1]
    oth = [r for r in range(4) if r not in own]
    order = own + oth
    gh = [g * 4 + r for r in order]
    qT4 = np.stack([projT_b[h * 64:(h + 1) * 64] for h in gh])
    kcvT = np.concatenate([projT_b[512 + g * 64:512 + (g + 1) * 64], projT_b[640 + g * 64:640 + (g + 1) * 64]], 0)
    ind = ((np.arange(S)[None, :] // 64) % 64 == np.arange(64)[:, None]).astype(NPBF)
    ksA = np.concatenate([projT_b[768 + g * 64:768 + (g + 1) * 64], ind], 0)
    ones = np.ones((S, 1), NPBF)
    vsA = np.concatenate([np.ascontiguousarray(projT_b[896 + g * 64:896 + (g + 1) * 64].T), ones], 1)
    kwT = np.ascontiguousarray(projT_b[1024 + g * 64:1024 + (g + 1) * 64])
    vwA = np.concatenate([np.ascontiguousarray(projT_b[1152 + g * 64:1152 + (g + 1) * 64].T), ones], 1)
    gsel = np.stack([np.stack([gT_b[br * 8 + gh[k]] for br in range(3)], -1) for k in range(2)], 1).astype(np.float32)
    dqT = np.ascontiguousarray(projT_b[1304 + hd * 128:1304 + (hd + 1) * 128])
    dkT = np.ascontiguousarray(projT_b[1816 + hd * 128:1816 + (hd + 1) * 128])
    dvA = np.ascontiguousarray(projT_b[2328 + hd * 128:2328 + (hd + 1) * 128].T)
    cstrip = np.stack([causal_strip(bt, gh[0]), causal_strip(bt, gh[1]), causal_strip(bt, 8 + hd)])
    wstrip = np.stack([window_strip(bt, gh[0]), window_strip(bt, gh[1])])
    cbias = np.stack([cmp_bias(bt, h) for h in gh])
    cb = np.broadcast_to(bt[31, gh + [8 + hd]][None, :], (128, 5)).astype(np.float32).copy()
    w1kv = np.stack([inputs["ev_cmp_w1_k"][0], inputs["ev_cmp_w1_v"][0]])
    w2kv = np.stack([inputs["ev_cmp_w2_k"][0], inputs["ev_cmp_w2_v"][0]])
    posT = np.concatenate([inputs["ev_cmp_pos_k"][0].T, inputs["ev_cmp_pos_v"][0].T], 0)
    c = np.arange(512)
    ovl = np.zeros((512, 129), np.float32)
    for cc in range(511):
        for tk in range(16 * cc, 16 * cc + 32):
            ovl[cc, tk // 64] += 1.0 / 32
        ovl[cc, 128] = 1.0
    cur = np.arange(128)[:, None]
    n = np.arange(128)[None, :]
    elig = n <= cur
    forced = (n == 0) | (n == cur) | (n == cur - 1)
    A = elig.astype(np.float32)
    B = np.where(elig, np.where(forced, 1000.0, 0.0), -1.0).astype(np.float32)
    ABtab = np.stack([A, B], 1)
    lam = np.stack([inputs["ev_lam_q1"][0], inputs["ev_lam_k1"][0], inputs["ev_lam_q2"][0], inputs["ev_lam_k2"][0]])
    return {"qT4": qT4, "kcvT": kcvT, "ksA": ksA, "vsA": vsA, "kwT": kwT, "vwA": vwA, "gsel": gsel,
            "dqT": dqT, "dkT": dkT, "dvA": dvA, "cstrip": cstrip, "wstrip": wstrip, "cbias": cbias, "cb": cb,
            "w1kv": w1kv, "w2kv": w2kv, "posT": posT.astype(np.float32), "ovl": ovl, "ABtab": ABtab,
            "lam": lam.reshape(1, 256).astype(np.float32), "subln": inputs["ev_subln"][0].reshape(1, 128),
            "identb": np.eye(128, dtype=np.float32).astype(NPBF), "identf": np.eye(128, dtype=np.float32),
            "ones2": np.stack([np.concatenate([np.ones((128, 1)), np.zeros((128, 1))], 1),
                               np.concatenate([np.zeros((128, 1)), np.ones((128, 1))], 1)]).astype(NPBF)}


L2_SPECS = {
    "qT4": ([4, 64, S], BF16), "kcvT": ([128, S], BF16), "ksA": ([128, S], BF16), "vsA": ([S, 65], BF16),
    "kwT": ([64, S], BF16), "vwA": ([S, 65], BF16), "gsel": ([S, 2, 3], F32), "dqT": ([128, S], BF16),
    "dkT": ([128, S], BF16), "dvA": ([S, 128], BF16), "cstrip": ([3, 128, 1024], F32),
    "wstrip": ([2, 128, 1408], F32), "cbias": ([4, 5, 128, 512], F32), "cb": ([128, 5], F32),
    "w1kv": ([2, 2048, 128], F32), "w2kv": ([2, 128, 64], F32), "posT": ([128, 32], F32),
    "ovl": ([512, 129], F32), "ABtab": ([128, 2, 128], F32), "lam": ([1, 256], F32), "subln": ([1, 128], F32),
    "identb": ([128, 128], BF16), "identf": ([128, 128], F32), "ones2": ([2, 128, 2], BF16),
}


def build_l2(nch=16, do_nsa=True, do_diff=True):
    nc = bass.Bass("TRN2", target_bir_lowering=False)
    dr = {k: nc.dram_tensor(k, sh, dt, kind="ExternalInput").ap() for k, (sh, dt) in L2_SPECS.items()}
    dr["o"] = nc.dram_tensor("o", [S, 256], BF16, kind="ExternalOutput").ap()
    es = ExitStack()
    with es:
        s = Sched(nc, es)
        l2_body(nc, s, dr, nch, do_nsa, do_diff)
        s.emit()
    return nc


def l2_body(nc, s, dr, nch, do_nsa, do_diff):
    def sb(name, shape, dt):
        return s.buf(s.sbuf(name, shape, dt), name)

    def load(name, shape, dt, src=None, eng="sp"):
        b = sb(name, shape, dt)
        s.dma(eng, b[:], dr[name] if src is None else src, writes=[b])
        return b

    identb = load("identb", [128, 128], BF16)
    identf = load("identf", [128, 128], F32)
    cb = load("cb", [128, 5], F32)
    cstrip = [load(f"cstrip{k}", [128, 1024], F32, dr["cstrip"][k]) for k in range(3)]
    pS = Ring([s.pbuf(f"pS{i}", [128, 512], F32) for i in range(3)])
    pOa = s.pbuf("pOa", [128, 512], F32)
    pOb = s.pbuf("pOb", [128, 512], F32)
    pU0 = s.pbuf("pU0", [128, 512], F32)
    pU1 = s.pbuf("pU1", [128, 512], F32)
    pL = s.pbuf("pL", [128, 512], F32)
    P = Ring([sb(f"P{i}", [128, 512], BF16) for i in range(4)])
    tmp = Ring([sb(f"tmp{i}", [128, 512], F32) for i in range(3)])
    ost = Ring([sb(f"ost{i}", [128, 4, 256], BF16) for i in range(2)])
    ob = s.buf(dr["o"], "o")
    ov = dr["o"].rearrange("(c s p) d -> c p s d", p=128, s=4)

    def exp_tile(ps, j, strip_ap_fn, cb_ap):
        p = P.next()
        if strip_ap_fn is not None:
            t = tmp.next()
            sbuf_, ap = strip_ap_fn
            s.op("dve", lambda e, ps=ps, t=t, ap=ap: e.scalar_tensor_tensor(
                out=t[:], in0=ps[:], scalar=0.125, in1=ap, op0=ALU.mult, op1=ALU.add), [ps, sbuf_], [t])
            s.op("act", lambda e, t=t, p=p: e.activation(out=p[:], in_=t[:], func=AF.Exp), [t], [p])
        else:
            s.op("act", lambda e, ps=ps, p=p: e.activation(out=p[:], in_=ps[:], func=AF.Exp, bias=cb_ap, scale=0.125),
                 [ps, cb], [p])
        return p

    ksd = sb("ksd", [128, S], BF16)
    kcd = sb("kcd", [128, S], BF16)
    scr = sb("scr", [128, 8192], BF16)
    ob3 = Ring([sb(f"ob3_{i}", [128, 512], F32) for i in range(3)])
    if do_nsa:
        ksA = ksd
        s.dma("sp", ksd[:], dr["ksA"], writes=[ksd])
        vsA = load("vsA", [128, 64, 65], BF16, dr["vsA"].rearrange("(t p) c -> p t c", p=128))
        kwT = load("kwT", [64, S], BF16)
        vwA = load("vwA", [128, 64, 65], BF16, dr["vwA"].rearrange("(t p) c -> p t c", p=128))
        wstrip = [load(f"wstrip{k}", [128, 1408], F32, dr["wstrip"][k]) for k in range(2)]
        cbias = [load(f"cbias{k}", [128, 5, 512], BF16, dr["cbias"][k].rearrange("o p x -> p o x"), eng="pool") for k in range(4)]
        gates = load("gates", [128, 64, 6], F32, dr["gsel"].rearrange("(t p) h b -> p t (h b)", p=128))
        s.op("act", lambda e: e.activation(out=gates[:], in_=gates[:], func=AF.Sigmoid), [gates], [gates])
        kcvT = kcd
        s.dma("sp", kcd[:], dr["kcvT"], writes=[kcd])
        w1kv = s.buf(scr.t[:, 0:4096].rearrange("p (j m) -> p j m", m=128), "w1kv_view")
        for m in range(2):
            s.dma("pool", w1kv[64 * m:64 * m + 64, :, :], dr["w1kv"][m].rearrange("(j d) m -> d j m", d=64), writes=[scr])
        w2kv = sb("w2kv", [128, 2, 64], BF16)
        s.dma("pool", w2kv[:], dr["w2kv"].rearrange("t k d -> k t d"), writes=[w2kv])
        posT = sb("posT", [128, 32], BF16)
        s.dma("pool", posT[:], dr["posT"], writes=[posT])
        R = sb("R", [128, 4, 193], BF16)
        s.dma("pool", R[:, :, 0:129], dr["ovl"].rearrange("(j p) n -> p j n", p=128), writes=[R])
        KcT = sb("KcT", [64, 512], BF16)
        gl = [P.items[0], P.items[1]]
        xs = tmp.items[0]
        x2 = tmp.items[1]
        for m in range(2):
            ps = pS.next()
            lo = 64 * m
            for j in range(32):
                s.op("pe", lambda e, j=j, lo=lo, ps=ps: e.matmul(
                    ps[:, 0:511], lhsT=w1kv[lo:lo + 64, j, :], rhs=kcvT[lo:lo + 64, j:j + 8161:16],
                    start=(j == 0), stop=False), [scr, kcvT], [ps])
            for j in range(32):
                s.op("pe", lambda e, j=j, lo=lo, ps=ps: e.matmul(
                    ps[:, 0:511], lhsT=w1kv[lo:lo + 64, j, :], rhs=posT[lo:lo + 64, j:j + 1].to_broadcast([64, 511]),
                    start=False, stop=(j == 31)), [scr, posT], [ps])
            s.op("pool", lambda e, m=m: e.memset(gl[m][:], 0.0), [], [gl[m]])
            s.op("act", lambda e, ps=ps: e.copy(out=xs[:, 0:511], in_=ps[:, 0:511]), [ps], [xs])
            s.op("dve", lambda e: e.tensor_tensor(out=x2[:, 0:511], in0=xs[:, 0:511], in1=xs[:, 0:511], op=ALU.mult), [xs], [x2])
            s.op("dve", lambda e: e.tensor_scalar(out=x2[:, 0:511], in0=x2[:, 0:511], scalar1=0.044715, scalar2=1.0,
                                                  op0=ALU.mult, op1=ALU.add), [x2], [x2])
            s.op("dve", lambda e: e.tensor_tensor(out=x2[:, 0:511], in0=x2[:, 0:511], in1=xs[:, 0:511], op=ALU.mult), [x2, xs], [x2])
            s.op("act", lambda e: e.activation(out=x2[:, 0:511], in_=x2[:, 0:511], func=AF.Sigmoid, scale=2.0 * GELU_C), [x2], [x2])
            s.op("dve", lambda e, m=m: e.tensor_tensor(out=gl[m][:, 0:511], in0=x2[:, 0:511], in1=xs[:, 0:511], op=ALU.mult),
                 [x2, xs], [gl[m]])
        ps = pS.next()
        s.op("pe", lambda e, ps=ps: e.matmul(ps[0:64, :], lhsT=w2kv[:, 0, :], rhs=gl[0][:], start=True, stop=True),
             [w2kv, gl[0]], [ps])
        s.op("act", lambda e, ps=ps: e.copy(out=KcT[:], in_=ps[0:64, :]), [ps], [KcT])
        ps = pS.next()
        for jt in range(4):
            s.op("pe", lambda e, jt=jt, ps=ps: e.matmul(ps[:, jt * 64:(jt + 1) * 64], lhsT=gl[1][:, jt * 128:(jt + 1) * 128],
                                                        rhs=w2kv[:, 1, :], start=True, stop=True), [gl[1], w2kv], [ps])
        s.op("act", lambda e, ps=ps: e.copy(out=R[:, :, 129:193], in_=ps[:, 0:256].rearrange("p (j d) -> p j d", d=64)),
             [ps], [R])
        QA = [[sb(f"QA{k}_{hf}_{i}", [128, 512], BF16) for i in range(2)] for k in range(2) for hf in range(2)]
        Qc = [Ring([sb(f"Qc{k}_{i}", [64, 512], BF16) for i in range(2)]) for k in range(2)]
        ABt = Ring([sb(f"ABt{i}", [128, 2, 128], F32) for i in range(3)])
        rsu = Ring([sb(f"rsu{i}", [128, 4], F32) for i in range(2)])
        imp = Ring([sb(f"imp{i}", [128, 128], F32) for i in range(2)])
        sc2 = Ring([sb(f"sc2{i}", [128, 128], F32) for i in range(2)])
        mx = Ring([sb(f"mx{i}", [128, 16], F32) for i in range(2)])
        NM = Ring([sb(f"NM{i}", [128, 192], BF16) for i in range(2)])
        for nm in NM.items:
            s.op("pool", lambda e, nm=nm: e.memset(nm[:], 0.0), [], [nm])
        ocmp = Ring([sb(f"ocmp{i}", [128, 4, 2, 64], F32) for i in range(2)])
        Osb = ob3
        rs2 = Ring([sb(f"rs2{i}", [128, 4, 1], F32) for i in range(3)])
        acc = Ring([sb(f"acc{i}", [128, 4, 64], F32) for i in range(2)])
        t2b = Ring([sb(f"t2b{i}", [128, 4, 64], F32) for i in range(2)])
        wg = Ring([sb(f"wg{i}", [128, 4, 1], F32) for i in range(4)])
        Ecm = [[scr.t[:, (r * 4 + jt) * 512:(r * 4 + jt + 1) * 512] for jt in range(4)] for r in range(4)]

    if do_diff:
        ones2 = load("ones2", [128, 2, 2], BF16, dr["ones2"].rearrange("m p c -> p m c"))
        lamv = load("lamv", [128, 256], F32, dr["lam"].to_broadcast([128, 256]))
        gsub = load("gsub", [128, 128], F32, dr["subln"].to_broadcast([128, 128]))
        s.op("dve", lambda e: e.tensor_scalar(out=gsub[:], in0=gsub[:], scalar1=1.0 - LAM_INIT0, scalar2=None, op0=ALU.mult),
             [gsub], [gsub])
        lt = sb("lt", [128, 128], F32)
        l2s = sb("l2s", [128, 2], F32)
        nlam = sb("nlam", [128, 1], F32)
        lv = lamv[:].rearrange("p (a d) -> p a d", d=64)
        s.op("dve", lambda e: e.tensor_tensor(out=lt[:].rearrange("p (a d) -> p a d", d=64), in0=lv[:, 0:4:2, :],
                                              in1=lv[:, 1:4:2, :], op=ALU.mult), [lamv], [lt])
        s.op("dve", lambda e: e.tensor_reduce(out=l2s[:], in_=lt[:].rearrange("p (a d) -> p a d", d=64), axis=AX.X,
                                              op=ALU.add), [lt], [l2s])
        s.op("act", lambda e: e.activation(out=l2s[:], in_=l2s[:], func=AF.Exp), [l2s], [l2s])
        s.op("dve", lambda e: e.tensor_tensor(out=nlam[:], in0=l2s[:, 1:2], in1=l2s[:, 0:1], op=ALU.subtract), [l2s], [nlam])
        s.op("dve", lambda e: e.tensor_scalar(out=nlam[:], in0=nlam[:], scalar1=-LAM_INIT0, scalar2=None, op0=ALU.add),
             [nlam], [nlam])
        dq = Ring([sb(f"dq{i}", [128, 512], BF16) for i in range(2)])
        Od = ob3
        sd = sb("sd", [2, 512], F32)
        rd = Ring([sb(f"rd{i}", [128, 4, 2], F32) for i in range(2)])
        o0 = Ring([sb(f"o0{i}", [128, 4, 128], F32) for i in range(1)])
        av = Ring([sb(f"av{i}", [128, 4, 128], F32) for i in range(1)])
        sq = sb("sq", [128, 4, 128], F32)
        ssd = Ring([sb(f"ssd{i}", [128, 4], F32) for i in range(2)])

    ovn = dr["o"][:, 0:128].rearrange("(c s p) d -> c p s d", p=128, s=4)
    ovd = dr["o"][:, 128:256].rearrange("(c s p) d -> c p s d", p=128, s=4)
    pF = pU1
    st = {}

    def pre_gen(i):
        qa = [[QA[k * 2 + hf][i % 2] for hf in range(2)] for k in range(2)]
        nhf = 2 if i >= 8 else 1
        qsrc = []
        for k in range(2):
            for hf in range(nhf):
                s.dma("sp", qa[k][hf][0:64, :], dr["qT4"][k][:, i * 512:(i + 1) * 512], writes=[qa[k][hf]])
            qsrc.append((qa[k][0], qa[k][0][0:64, :]))
        for k in range(2):
            q = Qc[k].next()
            s.dma("sp", q[:], dr["qT4"][2 + k][:, i * 512:(i + 1) * 512], writes=[q])
            qsrc.append((q, q[:]))
        oc = ocmp.next()
        st[i] = (qa, oc)
        yield
        ncj = min(4, (512 * i + 511 - 31) // 16 // 128 + 1)
        for r in range(4):
            qb, qap = qsrc[r]
            for jt in range(ncj):
                ps = pS.next()
                s.op("pe", lambda e, ps=ps, jt=jt, qap=qap: e.matmul(
                    ps[:], lhsT=KcT[:, jt * 128:(jt + 1) * 128], rhs=qap, start=True, stop=True), [KcT, qb], [ps])
                o_ = 512 * i - 2048 * jt
                ec = Ecm[r][jt]
                if o_ <= 2048:
                    t = tmp.next()
                    s.op("dve", lambda e, ps=ps, t=t, r=r, o_=o_: e.scalar_tensor_tensor(
                        out=t[:], in0=ps[:], scalar=0.125, in1=cbias[r][:, o_ // 512, :], op0=ALU.mult, op1=ALU.add),
                        [ps, cbias[r]], [t])
                    s.op("act", lambda e, t=t, ec=ec: e.activation(out=ec, in_=t[:], func=AF.Exp), [t], [scr])
                else:
                    s.op("act", lambda e, ps=ps, ec=ec, r=r: e.activation(out=ec, in_=ps[:], func=AF.Exp,
                                                                        bias=cb[:, r:r + 1], scale=0.125), [ps, cb], [scr])
                yield
        for sub in range(4):
            tt = 4 * i + sub
            ab = ABt.next()
            for hh in range(2):
                s.dma("sp", ab[64 * hh:64 * hh + 64, :, :], dr["ABtab"][2 * tt + hh:2 * tt + hh + 1].to_broadcast([64, 2, 128]),
                      writes=[ab])
            ru = rsu.next()
            im = imp.next()
            for pair in range(2):
                for r in (2 * pair, 2 * pair + 1):
                    c0 = (r % 2) * 193
                    for jt in range(ncj):
                        s.op("pe", lambda e, c0=c0, r=r, jt=jt, sub=sub, ncj=ncj: e.matmul(
                            pU0[:, c0:c0 + 193], lhsT=Ecm[r][jt][:, sub * 128:(sub + 1) * 128], rhs=R[:, jt, :],
                            start=(jt == 0), stop=(jt == ncj - 1)), [scr, R], [pU0])
                yield
                s.op("dve", lambda e, pair=pair, ru=ru: e.tensor_scalar(
                    out=ru[:, 2 * pair:2 * pair + 2], in0=pU0[:, 0:386].rearrange("p (h c) -> p h c", c=193)[:, :, 128],
                    scalar1=1e-30, scalar2=None, op0=ALU.max), [pU0], [ru])
                s.op("dve", lambda e, pair=pair, ru=ru: e.reciprocal(out=ru[:, 2 * pair:2 * pair + 2],
                                                                     in_=ru[:, 2 * pair:2 * pair + 2]), [ru], [ru])
                for r in (2 * pair, 2 * pair + 1):
                    c0 = (r % 2) * 193
                    if r == 0:
                        s.op("dve", lambda e, im=im, ru=ru: e.tensor_scalar(out=im[:], in0=pU0[:, 0:128], scalar1=ru[:, 0:1],
                                                                             scalar2=None, op0=ALU.mult), [pU0, ru], [im])
                    else:
                        s.op("dve", lambda e, im=im, ru=ru, c0=c0, r=r: e.scalar_tensor_tensor(
                            out=im[:], in0=pU0[:, c0:c0 + 128], scalar=ru[:, r:r + 1], in1=im[:], op0=ALU.mult, op1=ALU.add),
                            [pU0, ru, im], [im])
                if pair == 0:
                    for k in range(2):
                        c0 = k * 193
                        s.op("dve", lambda e, k=k, c0=c0, oc=oc, ru=ru, sub=sub: e.tensor_scalar(
                            out=oc[:, sub, k, :], in0=pU0[:, c0 + 129:c0 + 193], scalar1=ru[:, k:k + 1], scalar2=None,
                            op0=ALU.mult), [pU0, ru], [oc])
                yield
            s.op("dve", lambda e, im=im, ab=ab: e.tensor_tensor(out=im[:], in0=im[:], in1=ab[:, 0, :], op=ALU.mult), [im, ab], [im])
            s.op("dve", lambda e, im=im, ab=ab: e.tensor_tensor(out=im[:], in0=im[:], in1=ab[:, 1, :], op=ALU.add), [im, ab], [im])
            m8 = mx.next()
            s2 = sc2.next()
            s.op("dve", lambda e, im=im, m8=m8: e.max(out=m8[:, 0:8], in_=im[:]), [im], [m8])
            s.op("dve", lambda e, im=im, m8=m8, s2=s2: e.match_replace(out=s2[:], in_to_replace=m8[:, 0:8], in_values=im[:],
                                                                       imm_value=-2.0), [im, m8], [s2])
            yield
            s.op("dve", lambda e, m8=m8, s2=s2: e.max(out=m8[:, 8:16], in_=s2[:]), [s2, m8], [m8])
            s.op("dve", lambda e, im=im, m8=m8, s2=s2: e.tensor_scalar(out=s2[:], in0=im[:], scalar1=m8[:, 15:16], scalar2=None,
                                                                       op0=ALU.is_ge), [im, m8], [s2])
            nm = NM.next()
            s.op("dve", lambda e, s2=s2, nm=nm: e.tensor_scalar(out=nm[:, 64:192], in0=s2[:], scalar1=1.0, scalar2=-NEGM,
                                                                op0=ALU.subtract, op1=ALU.mult), [s2], [nm])
            yield
            for hf in range(nhf):
                s.op("pe", lambda e, nm=nm, hf=hf: e.matmul(pL[:, hf * 128:(hf + 1) * 128], lhsT=nm[:, hf * 64:hf * 64 + 128],
                                                             rhs=identb[:], start=True, stop=True), [nm, identb], [pL])
            for hf in range(nhf):
                for k in range(2):
                    q = qa[k][hf]
                    s.op("act", lambda e, q=q, hf=hf, sub=sub: e.copy(out=q[64:128, sub * 128:(sub + 1) * 128],
                                                                      in_=pL[64:128, hf * 128:(hf + 1) * 128]), [pL], [q])
            yield

    def advance(gen, n):
        if gen is None:
            return None
        try:
            for _ in range(n):
                next(gen)
        except StopIteration:
            return None
        return gen

    stream = TileStream(s, la=2)
    if do_nsa:
        g0 = pre_gen(0)
        while g0 is not None:
            g0 = advance(g0, 1000)
    for i in range(nch if do_nsa else 0):
        oo = ost.next()
        nkt = 4 * i + 4
        qa, oc = st[i]
        gen = pre_gen(i + 1) if i + 1 < nch else None
        kw0 = max(0, 4 * i - 4)
        ntile = 2 * (nkt + (nkt - kw0))
        per = -(-48 // ntile)
        for k in range(2):
            for kt in range(nkt):
                j = kt - 4 * i
                hf = 0 if kt < 32 else 1
                q = qa[k][hf]

                def qk(kt=kt, j=j, q=q, k=k):
                    ps = pS.next()
                    s.op("pe", lambda e, ps=ps, kt=kt, q=q: e.matmul(ps[:], lhsT=ksA[:, kt * 128:(kt + 1) * 128], rhs=q[:],
                                                                      start=True, stop=True), [ksA, q], [ps])
                    if j >= -1:
                        y0 = 384 - 128 * j
                        return exp_tile(ps, j, (cstrip[k], cstrip[k][:, y0:y0 + 512]), None)
                    return exp_tile(ps, j, None, cb[:, k:k + 1])

                def pv(p, kt=kt, nkt=nkt):
                    s.op("pe", lambda e, p=p, kt=kt, nkt=nkt: e.matmul(pOa[0:65, :], lhsT=vsA[:, kt, :], rhs=p[:],
                                                                        start=(kt == 0), stop=(kt == nkt - 1)), [vsA, p], [pOa])
                stream.push(qk, pv)
                gen = advance(gen, per)
            q = qa[k][0]
            for kt in range(kw0, nkt):
                j = kt - 4 * i

                def qk(kt=kt, j=j, q=q, k=k):
                    ps = pS.next()
                    s.op("pe", lambda e, ps=ps, kt=kt, q=q: e.matmul(ps[:], lhsT=kwT[:, kt * 128:(kt + 1) * 128], rhs=q[0:64, :],
                                                                      start=True, stop=True), [kwT, q], [ps])
                    y0 = 384 - 128 * j
                    return exp_tile(ps, j, (wstrip[k], wstrip[k][:, y0:y0 + 512]), None)

                def pv(p, kt=kt, kw0=kw0, nkt=nkt):
                    s.op("pe", lambda e, p=p, kt=kt, kw0=kw0, nkt=nkt: e.matmul(pOb[0:65, :], lhsT=vwA[:, kt, :], rhs=p[:],
                                                                                 start=(kt == kw0), stop=(kt == nkt - 1)), [vwA, p], [pOb])
                stream.push(qk, pv)
                gen = advance(gen, per)

            def fin_a():
                o1 = Osb.next()
                o2 = Osb.next()
                s.op("dve", lambda e, o1=o1: e.tensor_copy(out=o1[0:65, :], in_=pOa[0:65, :]), [pOa], [o1])
                s.op("act", lambda e, o2=o2: e.copy(out=o2[0:65, :], in_=pOb[0:65, :]), [pOb], [o2])
                return (o1, o2)

            def fin_b(os_, k=k, i=i, oc=oc, oo=oo):
                a = acc.next()
                for bi, gi in ((0, 1), (1, 2)):
                    osb = os_[bi]
                    for sub in range(4):
                        s.op("pe", lambda e, sub=sub, osb=osb: e.transpose(
                            out=pF[:, sub * 65:(sub + 1) * 65], in_=osb[0:65, sub * 128:(sub + 1) * 128], identity=identf[0:65, 0:65]),
                            [osb, identf], [pF])
                    puv = pF[:, 0:260].rearrange("p (s c) -> p s c", c=65)
                    r2 = rs2.next()
                    w = wg.next()
                    s.op("dve", lambda e, r2=r2, puv=puv: e.reciprocal(out=r2[:], in_=puv[:, :, 64:65]), [pF], [r2])
                    s.op("dve", lambda e, r2=r2, w=w, k=k, gi=gi, i=i: e.tensor_tensor(
                        out=w[:], in0=r2[:], in1=gates[:, 4 * i:4 * i + 4, k * 3 + gi:k * 3 + gi + 1], op=ALU.mult), [r2, gates], [w])
                    if bi == 0:
                        s.op("dve", lambda e, a=a, puv=puv, w=w: e.tensor_tensor(out=a[:], in0=puv[:, :, 0:64],
                                                                                 in1=w[:].to_broadcast([128, 4, 64]), op=ALU.mult), [pF, w], [a])
                    else:
                        t2 = t2b.next()
                        s.op("dve", lambda e, t2=t2, puv=puv, w=w: e.tensor_tensor(out=t2[:], in0=puv[:, :, 0:64],
                                                                                   in1=w[:].to_broadcast([128, 4, 64]), op=ALU.mult), [pF, w], [t2])
                        s.op("pool", lambda e, a=a, t2=t2: e.tensor_tensor(out=a[:], in0=a[:], in1=t2[:], op=ALU.add), [a, t2], [a])
                t3 = t2b.next()
                s.op("pool", lambda e, t3=t3, oc=oc, k=k, i=i: e.tensor_tensor(
                    out=t3[:], in0=oc[:, :, k, :], in1=gates[:, 4 * i:4 * i + 4, k * 3:k * 3 + 1].to_broadcast([128, 4, 64]), op=ALU.mult),
                    [oc, gates], [t3])
                s.op("pool", lambda e, t3=t3, a=a, oo=oo, k=k: e.tensor_tensor(out=oo[:, :, k * 64:(k + 1) * 64], in0=a[:], in1=t3[:],
                                                                              op=ALU.add), [a, t3], [oo])
                if k == 1:
                    s.dma("sp", ovn[i], oo[:, :, 0:128], reads=[oo], writes=[ob])
            stream.push_fin(fin_a, fin_b)
        while gen is not None:
            gen = advance(gen, 1000)
    stream.flush()

    if do_diff:
        dkT = kcd
        s.dma("sp", kcd[:], dr["dkT"], writes=[kcd])
        dvA = s.buf(ksd.t[:, :].rearrange("p (t c) -> p t c", c=128), "dvA_view")
        s.dma("sp", dvA[:], dr["dvA"].rearrange("(t p) c -> p t c", p=128), writes=[ksd])
        pOm = [pOa, pOb]
        pUm = [pU0, pU1]
        dqs = {}

        def load_dq(i):
            dqc = dq.next()
            s.dma("sp", dqc[:], dr["dqT"][:, i * 512:(i + 1) * 512], writes=[dqc])
            dqs[i] = dqc
        load_dq(0)
    for i in range(nch if do_diff else 0):
        oo = ost.next()
        nkt = 4 * i + 4
        dqc = dqs[i]
        if i + 1 < nch:
            load_dq(i + 1)
        for m in range(2):
            lo = 64 * m
            for kt in range(nkt):
                j = kt - 4 * i

                def qk(kt=kt, j=j, lo=lo, dqc=dqc):
                    ps = pS.next()
                    s.op("pe", lambda e, ps=ps, kt=kt, lo=lo, dqc=dqc: e.matmul(
                        ps[:], lhsT=dkT[lo:lo + 64, kt * 128:(kt + 1) * 128], rhs=dqc[lo:lo + 64, :], start=True, stop=True),
                        [dkT, dqc], [ps])
                    if j >= -1:
                        y0 = 384 - 128 * j
                        return exp_tile(ps, j, (cstrip[2], cstrip[2][:, y0:y0 + 512]), None)
                    return exp_tile(ps, j, None, cb[:, 4:5])

                def pv(p, kt=kt, m=m, nkt=nkt):
                    s.op("pe", lambda e, p=p, kt=kt, m=m, nkt=nkt: e.matmul(pOm[m][:], lhsT=dvA[:, kt, :], rhs=p[:],
                                                                             start=(kt == 0), stop=(kt == nkt - 1)), [ksd, p], [pOm[m]])
                    s.op("pe", lambda e, p=p, kt=kt, m=m, nkt=nkt: e.matmul(
                        pL[0:2, :], lhsT=ones2[:, m, :], rhs=p[:], start=(m == 0 and kt == 0), stop=(m == 1 and kt == nkt - 1)),
                        [ones2, p], [pL])
                stream.push(qk, pv)

        def fin_a():
            od = [Od.next(), Od.next()]
            s.op("dve", lambda e, od=od: e.tensor_copy(out=od[0][:], in_=pOa[:]), [pOa], [od[0]])
            s.op("act", lambda e, od=od: e.copy(out=od[1][:], in_=pOb[:]), [pOb], [od[1]])
            s.op("act", lambda e: e.copy(out=sd[:], in_=pL[0:2, :]), [pL], [sd])
            return od

        def fin_b(od, i=i, oo=oo):
            for m in range(2):
                for sub in range(4):
                    s.op("pe", lambda e, m=m, sub=sub, od=od: e.transpose(
                        out=pUm[m][:, sub * 128:(sub + 1) * 128], in_=od[m][:, sub * 128:(sub + 1) * 128], identity=identf[:]),
                        [od[m], identf], [pUm[m]])
            pq = pS.next()
            for sub in range(4):
                s.op("pe", lambda e, sub=sub, pq=pq: e.transpose(out=pq[:, sub * 2:(sub + 1) * 2], in_=sd[0:2, sub * 128:(sub + 1) * 128],
                                                                 identity=identf[0:2, 0:2]), [sd, identf], [pq])
            r = rd.next()
            s.op("dve", lambda e, r=r, pq=pq: e.reciprocal(out=r[:], in_=pq[:, 0:8].rearrange("p (s m) -> p s m", m=2)), [pq], [r])
            s.op("dve", lambda e, r=r: e.tensor_scalar(out=r[:, :, 1:2], in0=r[:, :, 1:2], scalar1=nlam[:, 0:1], scalar2=None,
                                                        op0=ALU.mult), [r, nlam], [r])
            o0b = o0.next()
            a = av.next()
            pv0 = pU0[:].rearrange("p (s c) -> p s c", c=128)
            pv1 = pU1[:].rearrange("p (s c) -> p s c", c=128)
            s.op("dve", lambda e, o0b=o0b, r=r, pv0=pv0: e.tensor_tensor(out=o0b[:], in0=pv0, in1=r[:, :, 0:1].to_broadcast([128, 4, 128]),
                                                                         op=ALU.mult), [pU0, r], [o0b])
            s.op("dve", lambda e, a=a, r=r, pv1=pv1: e.tensor_tensor(out=a[:], in0=pv1, in1=r[:, :, 1:2].to_broadcast([128, 4, 128]),
                                                                     op=ALU.mult), [pU1, r], [a])
            s.op("pool", lambda e, a=a, o0b=o0b: e.tensor_tensor(out=a[:], in0=a[:], in1=o0b[:], op=ALU.add), [a, o0b], [a])
            s.op("pool", lambda e, a=a: e.tensor_tensor(out=sq[:], in0=a[:], in1=a[:], op=ALU.mult), [a], [sq])
            sv = ssd.next()
            s.op("dve", lambda e, sv=sv: e.tensor_reduce(out=sv[:], in_=sq[:], axis=AX.X, op=ALU.add), [sq], [sv])
            s.op("dve", lambda e, sv=sv: e.tensor_scalar(out=sv[:], in0=sv[:], scalar1=1.0 / 128, scalar2=SUB_EPS, op0=ALU.mult,
                                                         op1=ALU.add), [sv], [sv])
            s.op("act", lambda e, sv=sv: e.activation(out=sv[:], in_=sv[:], func=AF.Sqrt), [sv], [sv])
            s.op("dve", lambda e, sv=sv: e.reciprocal(out=sv[:], in_=sv[:]), [sv], [sv])
            s.op("dve", lambda e, a=a, sv=sv: e.tensor_tensor(out=a[:], in0=a[:], in1=sv[:].unsqueeze(2).to_broadcast([128, 4, 128]),
                                                              op=ALU.mult), [a, sv], [a])
            s.op("dve", lambda e, a=a, oo=oo: e.tensor_tensor(out=oo[:, :, 128:256], in0=a[:],
                                                              in1=gsub[:].unsqueeze(1).to_broadcast([128, 4, 128]), op=ALU.mult),
                 [a, gsub], [oo])
            s.dma("sp", ovd[i], oo[:, :, 128:256], reads=[oo], writes=[ob])
        stream.push_fin(fin_a, fin_b)
    stream.flush()
    s.finish_outputs([ob])


_CACHE = {}


def _prog(key, fn):
    if key not in _CACHE:
        _CACHE[key] = fn()
    return _CACHE[key]


def kernel(**inputs):
    inputs = {k: np.asarray(v) for k, v in inputs.items()}
    x = np.ascontiguousarray(inputs["x"], dtype=np.float32).reshape(8, NT, D)
    bt = inputs["bias_table"]
    identb = np.eye(128, dtype=np.float32).astype(NPBF)
    cores = list(range(8))
    nc1 = _prog("l1", lambda: build_token_phase(False, False, 2840, False, False, gate_rows=(1280, 1304)))
    r1 = run_bass_kernel_spmd(nc1, [{"x": x[c], "ident": identb, "g_mix": inputs["norm_mix"][0:1],
                                     "w_in": inputs["ev_w_in"][0]} for c in cores], core_ids=cores).results
    projT = [np.concatenate([r1[b * 4 + q]["projT"] for q in range(4)], axis=1) for b in range(2)]
    gT = [np.concatenate([r1[b * 4 + q]["gT"] for q in range(4)], axis=1) for b in range(2)]
    nc2 = _prog("l2", lambda: build_l2(16, True, True))
    in2 = []
    for c in cores:
        b, g, half, hd = c // 4, (c % 4) // 2, c % 2, c % 4
        in2.append(l2_host_inputs(projT[b], gT[b], inputs, g, half, hd))
    r2 = run_bass_kernel_spmd(nc2, in2, core_ids=cores).results
    o0 = np.zeros((2, S, D), dtype=NPBF)
    for c in cores:
        b, g, half, hd = c // 4, (c % 4) // 2, c % 2, c % 4
        oc = r2[c]["o"]
        for k in range(2):
            h = g * 4 + 2 * half + k
            o0[b, :, h * 64:(h + 1) * 64] = oc[:, k * 64:(k + 1) * 64]
        o0[b, :, 512 + hd * 128:512 + (hd + 1) * 128] = oc[:, 128:256]
    o0 = o0.reshape(8, NT, D)
    nc3 = _prog("l3", lambda: build_token_phase(True, True, 3072, False, True))
    r3 = run_bass_kernel_spmd(nc3, [{"x": x[c], "ident": identb, "oT": np.ascontiguousarray(o0[c].T),
                                     "w_out": inputs["ev_w_out"][0], "g_mlp": inputs["norm_mlp"][0:1],
                                     "w1": inputs["mlp_w1"][0], "w2": inputs["mlp_w2"][0],
                                     "g_mix": inputs["norm_mix"][1:2], "w_in": inputs["od_w_in"][0]}
                                    for c in cores], core_ids=cores).results
    x1 = [r3[c]["x_o"] for c in cores]
    proj1T = [np.concatenate([r3[b * 4 + q]["projT"] for q in range(4)], axis=1) for b in range(2)]
    nc4 = _prog("l4", lambda: build_moba(16, 4))
    in4 = [moba_host_inputs(proj1T[c // 4], bt, [4 * (c % 4) + k for k in range(4)]) for c in cores]
    r4 = run_bass_kernel_spmd(nc4, in4, core_ids=cores).results
    o1 = np.zeros((2, S, D), dtype=NPBF)
    for c in cores:
        o1[c // 4, :, (c % 4) * 256:(c % 4 + 1) * 256] = r4[c]["o"]
    o1 = o1.reshape(8, NT, D)
    nc5 = _prog("l5", lambda: build_token_phase(True, True, 0, True, False))
    r5 = run_bass_kernel_spmd(nc5, [{"x": x1[c], "ident": identb, "oT": np.ascontiguousarray(o1[c].T),
                                     "w_out": inputs["od_w_out"][0], "g_mlp": inputs["norm_mlp"][1:2],
                                     "w1": inputs["mlp_w1"][1], "w2": inputs["mlp_w2"][1],
                                     "g_fin": inputs["norm_final"].reshape(1, D)} for c in cores], core_ids=cores).results
    out = np.stack([r5[c]["out"] for c in cores]).reshape(2, S, D).astype(np.float32)
    return out
```

```python
import numpy as np
import ml_dtypes
from contextlib import ExitStack
import concourse.bass as bass
import concourse.mybir as mybir
from concourse.bass_utils import run_bass_kernel_spmd

F32 = mybir.dt.float32
BF16 = mybir.dt.bfloat16
AF = mybir.ActivationFunctionType
ALU = mybir.AluOpType
AX = mybir.AxisListType
NPBF = ml_dtypes.bfloat16


class Buf:
    __slots__ = ("name", "t", "lw", "rd", "excl")

    def __init__(self, name, t, excl=False):
        self.name = name
        self.t = t
        self.excl = excl
        self.lw = None
        self.rd = {}

    def __getitem__(self, idx):
        return self.t[idx]


class Sched:
    ENGS = ("pe", "act", "dve", "pool", "sp")
    EPOCH = 12000
    NDMA = 24

    def __init__(self, nc, es):
        self.nc = nc
        self.es = es
        self.ops = {e: [] for e in self.ENGS}
        self.n = {e: 0 for e in self.ENGS}
        self.waited = {e: {} for e in self.ENGS}
        self.signal = {e: set() for e in self.ENGS}
        self.dma_uses = [0] * self.NDMA
        self.dma_i = 0
        self.nbuf = 0

    def sbuf(self, name, shape, dtype):
        t = self.es.enter_context(self.nc.sbuf_tensor("sb_" + name, list(shape), dtype))
        return t

    def psum(self, name, shape, dtype):
        t = self.es.enter_context(self.nc.psum_tensor("ps_" + name, list(shape), dtype))
        return t

    def buf(self, t, name=None, excl=False):
        self.nbuf += 1
        return Buf(name or f"b{self.nbuf}", t, excl)

    def pbuf(self, name, shape, dtype):
        return self.buf(self.psum(name, shape, dtype), name, excl=True)

    def _need(self, eng, tok, same_raw=False):
        if tok is None:
            return
        stream, v = tok
        if stream == eng:
            if not same_raw:
                return
            if self.n[eng] - v > 3:
                return
        if self.waited[eng].get(stream, -1) >= v:
            return
        self.waited[eng][stream] = v
        if isinstance(stream, str):
            self.signal[stream].add(v)
        self.ops[eng].append(("w", tok))

    def _deps(self, eng, reads, writes):
        for b in reads:
            self._need(eng, b.lw, same_raw=True)
            if b.excl:
                for tok in b.rd.values():
                    self._need(eng, tok, same_raw=False)
        for b in writes:
            self._need(eng, b.lw, same_raw=False)
            for tok in b.rd.values():
                self._need(eng, tok, same_raw=False)

    def _commit(self, tok, reads, writes):
        stream = tok[0]
        for b in reads:
            b.rd[stream] = tok
        for b in writes:
            b.lw = tok
            b.rd = {}

    def op(self, eng, fn, reads=(), writes=()):
        self._deps(eng, reads, writes)
        idx = self.n[eng]
        self.n[eng] += 1
        self.ops[eng].append(("i", fn, idx))
        self._commit((eng, idx), reads, writes)

    def dma(self, eng, out, in_, reads=(), writes=(), **kw):
        k = self.dma_i % self.NDMA
        self.dma_i += 1
        stream = ("dma", k)
        prev = self.dma_uses[k]
        self._deps(eng, reads, writes)
        if prev > 0:
            self._need(eng, (stream, prev * 16))
        self.dma_uses[k] = prev + 1
        val = (prev + 1) * 16
        self.ops[eng].append(("d", (out, in_, kw), k, val))
        self._commit((stream, val), reads, writes)

    def finish_outputs(self, bufs, eng="sp"):
        for b in bufs:
            self._need(eng, b.lw)

    def emit(self):
        nc = self.nc
        engobj = {"pe": nc.tensor, "act": nc.scalar, "dve": nc.vector,
                  "pool": nc.gpsimd, "sp": nc.sync}
        sigval = {}
        sems = {}
        for e in self.ENGS:
            cnt = 0
            for idx in sorted(self.signal[e]):
                ep = cnt // self.EPOCH
                sigval[(e, idx)] = ((e, ep), cnt % self.EPOCH + 1)
                cnt += 1
                if (e, ep) not in sems:
                    sems[(e, ep)] = self.es.enter_context(nc.semaphore(f"s_{e}_{ep}"))
        for k in range(self.NDMA):
            if self.dma_uses[k]:
                sems[("dma", k)] = self.es.enter_context(nc.semaphore(f"s_dma_{k}"))
        self.nsems = len(sems)
        block = self.es.enter_context(nc.Block())
        deco = {"pe": block.tensor, "act": block.scalar, "dve": block.vector,
                "pool": block.gpsimd, "sp": block.sync}

        def make(e):
            ops = self.ops[e]

            def body(eng):
                for o in ops:
                    if o[0] == "w":
                        stream, v = o[1]
                        if isinstance(stream, str):
                            sk, sv = sigval[(stream, v)]
                            eng.wait_ge(sems[sk], sv)
                        else:
                            eng.wait_ge(sems[stream], v)
                    elif o[0] == "i":
                        ins = o[1](eng)
                        sv = sigval.get((e, o[2]))
                        if sv is not None:
                            ins.then_inc(sems[sv[0]], 1)
                    else:
                        out, in_, kw = o[1]
                        eng.dma_start(out=out, in_=in_, **kw).then_inc(sems[("dma", o[2])], 16)
            return body

        for e in self.ENGS:
            if self.ops[e]:
                deco[e](make(e))


STAGE = 99

NT = 2048
D = 1024
EPS = 1e-6


class Ring:
    def __init__(self, items):
        self.items = list(items)
        self.i = 0

    def next(self):
        b = self.items[self.i % len(self.items)]
        self.i += 1
        return b


def build_token_phase(attn_in, mlp, proj_cols, final, x_out, gate_rows=None):
    nc = bass.Bass("TRN2", target_bir_lowering=False)
    dr = {}
    dr["x"] = nc.dram_tensor("x", [NT, D], F32, kind="ExternalInput").ap()
    dr["ident"] = nc.dram_tensor("ident", [128, 128], BF16, kind="ExternalInput").ap()
    if attn_in:
        dr["oT"] = nc.dram_tensor("oT", [D, NT], BF16, kind="ExternalInput").ap()
        dr["w_out"] = nc.dram_tensor("w_out", [D, D], F32, kind="ExternalInput").ap()
    if mlp:
        dr["g_mlp"] = nc.dram_tensor("g_mlp", [1, D], F32, kind="ExternalInput").ap()
        dr["w1"] = nc.dram_tensor("w1", [D, 4 * D], F32, kind="ExternalInput").ap()
        dr["w2"] = nc.dram_tensor("w2", [4 * D, D], F32, kind="ExternalInput").ap()
    if proj_cols:
        dr["g_mix"] = nc.dram_tensor("g_mix", [1, D], F32, kind="ExternalInput").ap()
        dr["w_in"] = nc.dram_tensor("w_in", [D, proj_cols], F32, kind="ExternalInput").ap()
        dr["projT"] = nc.dram_tensor("projT", [proj_cols, NT], BF16, kind="ExternalOutput").ap()
        if gate_rows:
            dr["gT"] = nc.dram_tensor("gT", [gate_rows[1] - gate_rows[0], NT], F32, kind="ExternalOutput").ap()
    if final:
        dr["g_fin"] = nc.dram_tensor("g_fin", [1, D], F32, kind="ExternalInput").ap()
        dr["out"] = nc.dram_tensor("out", [NT, D], F32, kind="ExternalOutput").ap()
    if x_out:
        dr["x_o"] = nc.dram_tensor("x_o", [NT, D], F32, kind="ExternalOutput").ap()

    es = ExitStack()
    with es:
        s = Sched(nc, es)
        token_phase_body(nc, s, dr, attn_in, mlp, proj_cols, final, x_out, gate_rows)
        s.emit()
    return nc


def token_phase_body(nc, s, dr, attn_in, mlp, proj_cols, final, x_out, gate_rows):
    NTT = NT // 128
    xs = s.sbuf("xs", [128, NTT, D], F32)
    xb = [s.buf(xs[:, t, :], f"x{t}") for t in range(NTT)]
    hnT_t = s.sbuf("hnT", [128, 8, NT], BF16)
    hnT = [s.buf(hnT_t[:, :, t * 128:(t + 1) * 128], f"hnT{t}") for t in range(NTT)]
    ident = s.buf(s.sbuf("ident", [128, 128], BF16), "ident")
    gbuf = s.buf(s.sbuf("gbuf", [128, D], F32), "gbuf")
    ss = s.buf(s.sbuf("ss", [128, NTT], F32), "ss")
    rstd = s.buf(s.sbuf("rstd", [128, NTT], F32), "rstd")
    junk = Ring([s.buf(s.sbuf(f"junk{i}", [128, D], BF16), f"junk{i}") for i in range(2)])
    hnb = Ring([s.buf(s.sbuf(f"hnb{i}", [128, D], BF16), f"hnb{i}") for i in range(2)])
    pT = Ring([s.pbuf(f"pT{i}", [128, 8, 128], BF16) for i in range(2)])
    pm = Ring([s.pbuf(f"pm{i}", [128, 512], F32) for i in range(5)])
    wa_t = [s.sbuf(f"wa{i}", [128, 8, 512], BF16) for i in range(2)]
    wa = Ring([s.buf(t, f"wa{i}") for i, t in enumerate(wa_t)])
    evac = Ring(["act", "dve"])

    s.dma("sp", ident[:], dr["ident"], writes=[ident])
    xv = dr["x"].rearrange("(t p) d -> t p d", p=128)
    for t in range(NTT):
        s.dma("sp", xb[t][:], xv[t], writes=[xb[t]])

    def rmsnorm_to_hnT(g_ap):
        s.dma("sp", gbuf[:], g_ap.to_broadcast([128, D]), writes=[gbuf])
        for t in range(NTT):
            j = junk.next()
            s.op("act", lambda e, t=t, j=j: e.activation(out=j[:], in_=xb[t][:], func=AF.Square,
                                                         accum_out=ss[:, t:t + 1]), [xb[t]], [j, ss])
        s.op("dve", lambda e: e.tensor_scalar(out=rstd[:], in0=ss[:], scalar1=1.0 / D, scalar2=EPS,
                                              op0=ALU.mult, op1=ALU.add), [ss], [rstd])
        s.op("act", lambda e: e.activation(out=rstd[:], in_=rstd[:], func=AF.Sqrt), [rstd], [rstd])
        s.op("dve", lambda e: e.reciprocal(out=rstd[:], in_=rstd[:]), [rstd], [rstd])

    def hn_transposes():
        if STAGE < 2: return
        for t in range(NTT):
            h = hnb.next()
            s.op("dve", lambda e, t=t, h=h: e.scalar_tensor_tensor(out=h[:], in0=xb[t][:], scalar=rstd[:, t:t + 1],
                                                                   in1=gbuf[:], op0=ALU.mult, op1=ALU.mult),
                 [xb[t], rstd, gbuf], [h])
            if STAGE < 3: continue
            p = pT.next()
            for k in range(8):
                s.op("pe", lambda e, k=k, h=h, p=p: e.transpose(out=p[:, k, :], in_=h[:, k * 128:(k + 1) * 128],
                                                                identity=ident[:]), [h, ident], [p])
            s.op("act", lambda e, t=t, p=p: e.copy(out=hnT[t][:], in_=p[:]), [p], [hnT[t]])

    if attn_in:
        oT = hnT
        ov = dr["oT"].rearrange("(k p) t -> p k t", p=128)
        for t in range(NTT):
            s.dma("sp", oT[t][:], ov[:, :, t * 128:(t + 1) * 128], writes=[oT[t]])
        wo_t = s.sbuf("wo", [128, 8, D], BF16)
        wo = s.buf(wo_t, "wo")
        s.dma("pool", wo[:], dr["w_out"].rearrange("(k p) n -> p k n", p=128), writes=[wo])
        for t in range(NTT):
            for half in range(2):
                p = pm.next()
                for k in range(8):
                    s.op("pe", lambda e, t=t, half=half, k=k, p=p: e.matmul(
                        p[:], lhsT=oT[t][:, k, :], rhs=wo[:, k, half * 512:(half + 1) * 512],
                        start=(k == 0), stop=(k == 7)), [oT[t], wo], [p])
                s.op("dve", lambda e, t=t, half=half, p=p: e.tensor_tensor(
                    out=xb[t][:, half * 512:(half + 1) * 512], in0=xb[t][:, half * 512:(half + 1) * 512],
                    in1=p[:], op=ALU.add), [p, xb[t]], [xb[t]])

    if mlp:
        rmsnorm_to_hnT(dr["g_mlp"])
        hn_transposes()
        wb_t = [s.sbuf(f"wb{i}", [128, 4, D], BF16) for i in range(2)]
        wb = Ring([s.buf(t, f"wb{i}") for i, t in enumerate(wb_t)])
        hT_t = s.sbuf("hT", [128, 4, NT], BF16)
        hT = [[s.buf(hT_t[:, f, c * 512:(c + 1) * 512], f"hT{f}_{c}") for c in range(4)] for f in range(4)]
        rl = Ring([s.buf(s.sbuf(f"rl{i}", [128, 512], F32), f"rl{i}") for i in range(3)])
        w1v = dr["w1"].rearrange("(k p) f -> p k f", p=128)
        w2v = dr["w2"].rearrange("(g f p) n -> g p f n", p=128, f=4)
        NG = 8
        for g in range(NG):
            a = wa.next()
            b = wb.next()
            s.dma("pool", a[:], w1v[:, :, g * 512:(g + 1) * 512], writes=[a])
            s.dma("pool", b[:], w2v[g], writes=[b])
            for c in range(4):
                for f in range(4):
                    p = pm.next()
                    for k in range(8):
                        s.op("pe", lambda e, k=k, f=f, c=c, p=p, a=a: e.matmul(
                            p[:], lhsT=a[:, k, f * 128:(f + 1) * 128], rhs=hnT_t[:, k, c * 512:(c + 1) * 512],
                            start=(k == 0), stop=(k == 7)), [a] + hnT[c * 4:(c + 1) * 4], [p])
                    r = rl.next()
                    s.op("act", lambda e, p=p, r=r: e.activation(out=r[:], in_=p[:], func=AF.Relu), [p], [r])
                    s.op("dve", lambda e, r=r, f=f, c=c: e.tensor_tensor(out=hT[f][c][:], in0=r[:], in1=r[:],
                                                                         op=ALU.mult), [r], [hT[f][c]])
            for t in range(NTT):
                c = t // 4
                for half in range(2):
                    p = pm.next()
                    for f in range(4):
                        s.op("pe", lambda e, t=t, f=f, half=half, p=p, b=b: e.matmul(
                            p[:], lhsT=hT_t[:, f, t * 128:(t + 1) * 128], rhs=b[:, f, half * 512:(half + 1) * 512],
                            start=(f == 0), stop=(f == 3)), [hT[f][c], b], [p])
                    s.op("dve", lambda e, t=t, half=half, p=p: e.tensor_tensor(
                        out=xb[t][:, half * 512:(half + 1) * 512], in0=xb[t][:, half * 512:(half + 1) * 512],
                        in1=p[:], op=ALU.add), [p, xb[t]], [xb[t]])

    if x_out:
        xo = s.buf(dr["x_o"], "x_o")
        xov = dr["x_o"].rearrange("(t p) d -> t p d", p=128)
        for t in range(NTT):
            s.dma("sp", xov[t], xb[t][:], reads=[xb[t]], writes=[xo])
        s.finish_outputs([xo])

    if proj_cols:
        rmsnorm_to_hnT(dr["g_mix"])
        hn_transposes()
        pj = s.buf(dr["projT"], "projT")
        outs = [pj]
        stage = Ring([s.buf(s.sbuf(f"stg{i}", [128, NT], BF16), f"stg{i}") for i in range(2)])
        if gate_rows:
            gt = s.buf(dr["gT"], "gT")
            outs.append(gt)
            gst = s.buf(s.sbuf("gst", [128, NT], F32), "gst")
        ncg = (proj_cols + 511) // 512
        if STAGE < 4: ncg = 0
        if STAGE == 4: ncg = 1
        if STAGE == 5: ncg = 5
        if STAGE == 6: ncg = 2
        if STAGE == 7: ncg = 3
        for cg in range(ncg):
            c0 = cg * 512
            cw = min(512, proj_cols - c0)
            a = wa.next()
            s.dma("pool", a[:, :, 0:cw], dr["w_in"].rearrange("(k p) n -> p k n", p=128)[:, :, c0:c0 + cw], writes=[a])
            for ct in range((cw + 127) // 128):
                m0 = ct * 128
                mw = min(128, cw - m0)
                st = stage.next()
                for c in range(4):
                    p = pm.next()
                    for k in range(8):
                        s.op("pe", lambda e, k=k, c=c, p=p, a=a, m0=m0, mw=mw: e.matmul(
                            p[0:mw, :], lhsT=a[:, k, m0:m0 + mw], rhs=hnT_t[:, k, c * 512:(c + 1) * 512],
                            start=(k == 0), stop=(k == 7)), [a] + hnT[c * 4:(c + 1) * 4], [p])
                    ev = evac.next()
                    if gate_rows and c0 + m0 == gate_rows[0]:
                        ev = "act"
                    if ev == "act":
                        s.op("act", lambda e, p=p, st=st, c=c, mw=mw: e.copy(out=st[0:mw, c * 512:(c + 1) * 512],
                                                                            in_=p[0:mw, :]), [p], [st])
                    else:
                        s.op("dve", lambda e, p=p, st=st, c=c, mw=mw: e.tensor_copy(out=st[0:mw, c * 512:(c + 1) * 512],
                                                                                   in_=p[0:mw, :]), [p], [st])
                    if gate_rows and c0 + m0 == gate_rows[0]:
                        ng = gate_rows[1] - gate_rows[0]
                        s.op("act", lambda e, p=p, c=c, ng=ng: e.copy(out=gst[0:ng, c * 512:(c + 1) * 512],
                                                                     in_=p[0:ng, :]), [p], [gst])
                s.dma("sp", dr["projT"][c0 + m0:c0 + m0 + mw, :], st[0:mw, :], reads=[st], writes=[pj])
                if gate_rows and c0 + m0 == gate_rows[0]:
                    ng = gate_rows[1] - gate_rows[0]
                    s.dma("sp", dr["gT"], gst[0:ng, :], reads=[gst], writes=[gt])
        s.finish_outputs(outs)

    if final:
        rmsnorm_to_hnT(dr["g_fin"])
        ob = s.buf(dr["out"], "out")
        ofin = Ring([s.buf(s.sbuf(f"ofin{i}", [128, D], F32), f"ofin{i}") for i in range(2)])
        outv = dr["out"].rearrange("(t p) d -> t p d", p=128)
        for t in range(NTT):
            o = ofin.next()
            s.op("dve", lambda e, t=t, o=o: e.scalar_tensor_tensor(out=o[:], in0=xb[t][:], scalar=rstd[:, t:t + 1],
                                                                   in1=gbuf[:], op0=ALU.mult, op1=ALU.mult),
                 [xb[t], rstd, gbuf], [o])
            s.dma("sp", outv[t], o[:], reads=[o], writes=[ob])
        s.finish_outputs([ob])


S = 8192
NEGM = -30000.0
NCH = 16


class TileStream:
    def __init__(self, s, la=2):
        self.s = s
        self.la = la
        self.pend = []
        self.n = 0

    def _drain(self):
        i = 0
        while i < len(self.pend):
            due, fn, arg = self.pend[i]
            if due <= self.n:
                self.pend.pop(i)
                fn(arg)
            else:
                i += 1

    def push(self, qk, pv):
        p = qk()
        self.pend.append((self.n + self.la + 1, pv, p))
        self.n += 1
        self._drain()

    def push_fin(self, fin_a, fin_b):
        due_a = self.n + self.la

        def run_a(_):
            x = fin_a()
            self.pend.append((due_a + 2, fin_b, x))
        self.pend.append((due_a, run_a, None))

    def flush(self):
        while self.pend:
            self.n += 1
            self._drain()


def t5_bucket_np(rel):
    n = np.maximum(rel, 0).astype(np.int32)
    nf = np.maximum(n, 1).astype(np.float32)
    large = 16 + (np.log(nf / np.float32(16)) / np.float32(np.log(128 / 16)) * np.float32(16)).astype(np.int32)
    large = np.minimum(large, 31)
    return np.where(n < 16, n, large)


def causal_strip(bias_table, h):
    p = np.arange(128)[:, None]
    y = np.arange(1024)[None, :]
    rel = y - p - 384
    v = bias_table[t5_bucket_np(rel), h].astype(np.float32)
    return np.where(rel >= 0, v, np.float32(NEGM)).astype(np.float32)


def moba_host_inputs(projT_b, bias_table, heads):
    qT = np.stack([projT_b[h * 64:(h + 1) * 64] for h in heads])
    ind = (np.arange(S)[None, :] // 256 == np.arange(32)[:, None]).astype(NPBF)
    zpad = np.zeros((32, S), NPBF)
    kA = np.stack([np.concatenate([projT_b[1024 + h * 64:1024 + (h + 1) * 64], ind, zpad], 0) for h in heads])
    ones = np.ones((S, 1), NPBF)
    vA = np.stack([np.concatenate([np.ascontiguousarray(projT_b[2048 + h * 64:2048 + (h + 1) * 64].T), ones], 1)
                   for h in heads])
    strip = np.stack([causal_strip(bias_table, h) for h in heads])
    cb = np.broadcast_to(bias_table[31, heads][None, :], (128, 4)).astype(np.float32).copy()
    c = np.arange(32)[:, None]
    n = np.arange(32)[None, :]
    past01 = (n < c).astype(np.float32)
    own01 = (n == c).astype(np.float32)
    pastneg = np.where(n < c, 0.0, NEGM).astype(np.float32)
    cm = np.stack([pastneg, past01, own01])
    cm = np.broadcast_to(cm[None], (128, 3, 32, 32)).copy()
    return {"qT": qT, "kA": kA, "vA": vA, "strip": strip, "cb": cb, "cm": cm,
            "identb": np.eye(128, dtype=np.float32).astype(NPBF), "identf": np.eye(128, dtype=np.float32)}


def build_moba(nch=NCH, nheads=4):
    nc = bass.Bass("TRN2", target_bir_lowering=False)
    dr = {}
    dr["qT"] = nc.dram_tensor("qT", [4, 64, S], BF16, kind="ExternalInput").ap()
    dr["kA"] = nc.dram_tensor("kA", [4, 128, S], BF16, kind="ExternalInput").ap()
    dr["vA"] = nc.dram_tensor("vA", [4, S, 65], BF16, kind="ExternalInput").ap()
    dr["strip"] = nc.dram_tensor("strip", [4, 128, 1024], F32, kind="ExternalInput").ap()
    dr["cb"] = nc.dram_tensor("cb", [128, 4], F32, kind="ExternalInput").ap()
    dr["cm"] = nc.dram_tensor("cm", [128, 3, 32, 32], F32, kind="ExternalInput").ap()
    dr["identb"] = nc.dram_tensor("identb", [128, 128], BF16, kind="ExternalInput").ap()
    dr["identf"] = nc.dram_tensor("identf", [128, 128], F32, kind="ExternalInput").ap()
    dr["o"] = nc.dram_tensor("o", [S, 256], BF16, kind="ExternalOutput").ap()
    es = ExitStack()
    with es:
        s = Sched(nc, es)
        moba_body(nc, s, dr, nch, nheads)
        s.emit()
    return nc


def moba_body(nc, s, dr, nch, nheads):
    H = nheads
    kA = [s.buf(s.sbuf(f"kA{h}", [128, S], BF16), f"kA{h}") for h in range(H)]
    vA = [s.buf(s.sbuf(f"vA{h}", [128, 64 * 65 + 64], BF16), f"vA{h}") for h in range(H)]
    strip = [s.buf(s.sbuf(f"strip{h}", [128, 1024], F32), f"strip{h}") for h in range(H)]
    cb = s.buf(s.sbuf("cb", [128, 4], F32), "cb")
    cm = s.buf(s.sbuf("cm", [128, 3, 32, 32], F32), "cm")
    identb = s.buf(s.sbuf("identb", [128, 128], BF16), "identb")
    identf = s.buf(s.sbuf("identf", [128, 128], F32), "identf")
    kmean = [s.buf(s.sbuf(f"kmean{h}", [64, 32], F32), f"kmean{h}") for h in range(H)]
    QA = Ring([s.buf(s.sbuf(f"QA{i}", [128, 512], BF16), f"QA{i}") for i in range(3)])
    qf = Ring([s.buf(s.sbuf(f"qf{i}", [64, 512], F32), f"qf{i}") for i in range(2)])
    gm = Ring([s.buf(s.sbuf(f"gm{i}", [128, 4, 32], F32), f"gm{i}") for i in range(2)])
    mx = Ring([s.buf(s.sbuf(f"mx{i}", [128, 4, 8], F32), f"mx{i}") for i in range(2)])
    sm = Ring([s.buf(s.sbuf(f"sm{i}", [128, 4, 32], F32), f"sm{i}") for i in range(2)])
    NM = Ring([s.buf(s.sbuf(f"NM{i}", [128, 4, 128], BF16), f"NM{i}") for i in range(2)])
    P = Ring([s.buf(s.sbuf(f"P{i}", [128, 512], BF16), f"P{i}") for i in range(4)])
    tmp = Ring([s.buf(s.sbuf(f"tmp{i}", [128, 512], F32), f"tmp{i}") for i in range(2)])
    Osb = Ring([s.buf(s.sbuf(f"Osb{i}", [65, 512], F32), f"Osb{i}") for i in range(2)])
    rs = Ring([s.buf(s.sbuf(f"rs{i}", [128, 4, 1], F32), f"rs{i}") for i in range(2)])
    ost = Ring([s.buf(s.sbuf(f"ost{i}", [128, 4, 256], BF16), f"ost{i}") for i in range(2)])
    pS = Ring([s.pbuf(f"pS{i}", [128, 512], F32) for i in range(3)])
    pO = Ring([s.pbuf(f"pO{i}", [128, 512], F32) for i in range(2)])
    pA = s.pbuf("pA", [128, 512], F32)
    pB = s.pbuf("pB", [128, 512], F32)
    ob = s.buf(dr["o"], "o")

    s.dma("sp", identb[:], dr["identb"], writes=[identb])
    s.dma("sp", identf[:], dr["identf"], writes=[identf])
    s.dma("sp", cb[:], dr["cb"], writes=[cb])
    s.dma("sp", cm[:], dr["cm"], writes=[cm])
    for h in range(H):
        s.dma("sp", kA[h][:], dr["kA"][h], writes=[kA[h]])
        s.op("pool", lambda e, h=h: e.memset(vA[h][:, 4160:4224], 0.0), [], [vA[h]])
        s.dma("sp", vA[h][:, 0:4160].rearrange("p (t c) -> p t c", c=65), dr["vA"][h].rearrange("(t p) c -> p t c", p=128), writes=[vA[h]])
        s.dma("sp", strip[h][:], dr["strip"][h], writes=[strip[h]])
    for nm in NM.items + QA.items:
        s.op("pool", lambda e, nm=nm: e.memset(nm[:], 0.0), [], [nm])
    for h in range(H):
        s.op("dve", lambda e, h=h: e.tensor_reduce(out=kmean[h][:], in_=kA[h][0:64, :].rearrange("p (n k) -> p n k", k=256),
                                                   axis=AX.X, op=ALU.add), [kA[h]], [kmean[h]])
        s.op("dve", lambda e, h=h: e.tensor_scalar(out=kmean[h][:], in0=kmean[h][:], scalar1=1.0 / 256, scalar2=None,
                                                   op0=ALU.mult), [kmean[h]], [kmean[h]])

    ov = dr["o"].rearrange("(c s p) d -> c p s d", p=128, s=4)
    pF = s.pbuf("pF", [128, 512], F32)
    units = [(i, h) for i in range(nch) for h in range(H)]
    state = {}

    def pre_a(u):
        i, h = units[u]
        qa = QA.next()
        s.dma("sp", qa[0:64, :], dr["qT"][h][:, i * 512:(i + 1) * 512], writes=[qa])
        q32 = qf.next()
        s.op("act", lambda e, qa=qa, q32=q32: e.copy(out=q32[:], in_=qa[0:64, :]), [qa], [q32])
        for sub in range(4):
            s.op("pe", lambda e, sub=sub, q32=q32, h=h: e.matmul(
                pA[:, sub * 32:(sub + 1) * 32], lhsT=q32[:, sub * 128:(sub + 1) * 128], rhs=kmean[h][:],
                start=True, stop=True), [q32, kmean[h]], [pA])
        g = gm.next()
        m8 = mx.next()
        sel = sm.next()
        nm = NM.next()
        for sub in range(4):
            c = (i * 4 + sub) // 2
            s.op("dve", lambda e, sub=sub, c=c, g=g: e.tensor_tensor(
                out=g[:, sub, :], in0=pA[:, sub * 32:(sub + 1) * 32], in1=cm[:, 0, c, :], op=ALU.add),
                [pA, cm], [g])
        for sub in range(4):
            s.op("dve", lambda e, sub=sub, g=g, m8=m8: e.max(out=m8[:, sub, :], in_=g[:, sub, :]), [g], [m8])
        for sub in range(4):
            c = (i * 4 + sub) // 2
            s.op("dve", lambda e, sub=sub, g=g, m8=m8, sel=sel: e.tensor_scalar(
                out=sel[:, sub, :], in0=g[:, sub, :], scalar1=m8[:, sub, 2:3], scalar2=None, op0=ALU.is_ge),
                [g, m8], [sel])
            s.op("dve", lambda e, sub=sub, c=c, sel=sel: e.tensor_tensor(
                out=sel[:, sub, :], in0=sel[:, sub, :], in1=cm[:, 1, c, :], op=ALU.mult), [sel, cm], [sel])
            s.op("dve", lambda e, sub=sub, c=c, sel=sel: e.tensor_tensor(
                out=sel[:, sub, :], in0=sel[:, sub, :], in1=cm[:, 2, c, :], op=ALU.add), [sel, cm], [sel])
        s.op("dve", lambda e, sel=sel, nm=nm: e.tensor_scalar(
            out=nm[:, :, 64:96], in0=sel[:], scalar1=1.0, scalar2=-NEGM, op0=ALU.subtract, op1=ALU.mult),
            [sel], [nm])
        state[u] = (qa, nm)

    def pre_b(u):
        qa, nm = state[u]
        for sub in range(4):
            s.op("pe", lambda e, sub=sub, nm=nm: e.matmul(
                pB[:, sub * 128:(sub + 1) * 128], lhsT=nm[:, sub, :], rhs=identb[:], start=True, stop=True),
                [nm, identb], [pB])
        s.op("act", lambda e, qa=qa: e.copy(out=qa[64:96, :], in_=pB[64:96, :]), [pB], [qa])

    stream = TileStream(s, la=2)
    osb_out = None
    pre_a(0)
    pre_b(0)
    for u, (i, h) in enumerate(units):
        if h == 0:
            osb_out = ost.next()
        qa, _ = state[u]
        po = pO.next()
        nkt = 4 * i + 4
        if u + 1 < len(units):
            pre_a(u + 1)
        for kt in range(nkt):
            j = kt - 4 * i

            def qk(kt=kt, j=j, qa=qa, h=h):
                ps = pS.next()
                s.op("pe", lambda e, kt=kt, ps=ps, qa=qa, h=h: e.matmul(
                    ps[:], lhsT=kA[h][:, kt * 128:(kt + 1) * 128], rhs=qa[:], start=True, stop=True),
                    [kA[h], qa], [ps])
                p = P.next()
                if j >= -1:
                    y0 = 384 - 128 * j
                    t = tmp.next()
                    s.op("dve", lambda e, ps=ps, t=t, h=h, y0=y0: e.scalar_tensor_tensor(
                        out=t[:], in0=ps[:], scalar=0.125, in1=strip[h][:, y0:y0 + 512], op0=ALU.mult, op1=ALU.add),
                        [ps, strip[h]], [t])
                    s.op("act", lambda e, t=t, p=p: e.activation(out=p[:], in_=t[:], func=AF.Exp), [t], [p])
                else:
                    s.op("act", lambda e, ps=ps, p=p, h=h: e.activation(out=p[:], in_=ps[:], func=AF.Exp,
                                                                        bias=cb[:, h:h + 1], scale=0.125),
                         [ps, cb], [p])
                return p

            def pv(p, kt=kt, po=po, h=h, nkt=nkt):
                s.op("pe", lambda e, kt=kt, p=p, po=po, h=h, nkt=nkt: e.matmul(
                    po[:], lhsT=vA[h][:, kt * 65:kt * 65 + 128], rhs=p[:], start=(kt == 0), stop=(kt == nkt - 1)),
                    [vA[h], p], [po])
            stream.push(qk, pv)
            if kt == min(7, nkt - 1) and u + 1 < len(units):
                pre_b(u + 1)

        def fin_a(po=po):
            osb = Osb.next()
            s.op("dve", lambda e, osb=osb, po=po: e.tensor_copy(out=osb[:], in_=po[0:65, :]), [po], [osb])
            return osb

        def fin_b(osb, h=h, osb_out=osb_out, i=i):
            for sub in range(4):
                s.op("pe", lambda e, sub=sub, osb=osb: e.transpose(
                    out=pF[:, sub * 65:(sub + 1) * 65], in_=osb[:, sub * 128:(sub + 1) * 128],
                    identity=identf[0:65, 0:65]), [osb, identf], [pF])
            r = rs.next()
            pav = pF[:, 0:260].rearrange("p (s c) -> p s c", c=65)
            s.op("dve", lambda e, r=r, pav=pav: e.reciprocal(out=r[:], in_=pav[:, :, 64:65]), [pF], [r])
            s.op("dve", lambda e, r=r, pav=pav, h=h, osb_out=osb_out: e.tensor_tensor(
                out=osb_out[:, :, h * 64:(h + 1) * 64], in0=pav[:, :, 0:64], in1=r[:].to_broadcast([128, 4, 64]),
                op=ALU.mult), [pF, r], [osb_out])
            if h == H - 1:
                s.dma("sp", ov[i], osb_out[:], reads=[osb_out], writes=[ob])
        stream.push_fin(fin_a, fin_b)
    stream.flush()
    s.finish_outputs([ob])


LAM_INIT0 = 0.8 - 0.6 * 1.0
SUB_EPS = 1e-6
GELU_C = 0.7978845608028654


def window_strip(bias_table, h):
    p = np.arange(128)[:, None]
    y = np.arange(1408)[None, :]
    rel = y - p - 384
    v = bias_table[t5_bucket_np(rel), h].astype(np.float32)
    return np.where((rel >= 0) & (rel < 512), v, np.float32(NEGM)).astype(np.float32)


def cmp_bias(bias_table, h):
    out = []
    p = np.arange(128)[:, None]
    x = np.arange(512)[None, :]
    for o in range(0, 2560, 512):
        rel = o + x - 16 * p - 31
        v = bias_table[t5_bucket_np(rel), h].astype(np.float32)
        out.append(np.where(rel >= 0, v, np.float32(NEGM)))
    return np.stack(out).astype(np.float32)


def l2_host_inputs(projT_b, gT_b, inputs, g, half, hd):
    bt = inputs["bias_table"]
    own = [2 * half, 2 * half + 1]
    oth = [r for r in range(4) if r not in own]
    order = own + oth
    gh = [g * 4 + r for r in order]
    qT4 = np.stack([projT_b[h * 64:(h + 1) * 64] for h in gh])
    kcvT = np.concatenate([projT_b[512 + g * 64:512 + (g + 1) * 64], projT_b[640 + g * 64:640 + (g + 1) * 64]], 0)
    ind = ((np.arange(S)[None, :] // 64) % 64 == np.arange(64)[:, None]).astype(NPBF)
    ksA = np.concatenate([projT_b[768 + g * 64:768 + (g + 1) * 64], ind], 0)
    ones = np.ones((S, 1), NPBF)
    vsA = np.concatenate([np.ascontiguousarray(projT_b[896 + g * 64:896 + (g + 1) * 64].T), ones], 1)
    kwT = np.concatenate([projT_b[1024 + g * 64:1024 + (g + 1) * 64], np.zeros((64, S), NPBF)], 0)
    vwA = np.concatenate([np.ascontiguousarray(projT_b[1152 + g * 64:1152 + (g + 1) * 64].T), ones], 1)
    gsel = np.stack([np.stack([gT_b[br * 8 + gh[k]] for br in range(3)], -1) for k in range(2)], 1).astype(np.float32)
    dqT = np.ascontiguousarray(projT_b[1304 + hd * 128:1304 + (hd + 1) * 128])
    dkT = np.ascontiguousarray(projT_b[1816 + hd * 128:1816 + (hd + 1) * 128])
    dvA = np.ascontiguousarray(projT_b[2328 + hd * 128:2328 + (hd + 1) * 128].T)
    cstrip = np.stack([causal_strip(bt, gh[0]), causal_strip(bt, gh[1]), causal_strip(bt, 8 + hd)])
    wstrip = np.stack([window_strip(bt, gh[0]), window_strip(bt, gh[1])])
    cbias = np.stack([cmp_bias(bt, h) for h in gh])
    cb = np.broadcast_to(bt[31, gh + [8 + hd]][None, :], (128, 5)).astype(np.float32).copy()
    w1kv = np.stack([inputs["ev_cmp_w1_k"][0], inputs["ev_cmp_w1_v"][0]])
    w2kv = np.stack([inputs["ev_cmp_w2_k"][0], inputs["ev_cmp_w2_v"][0]])
    posT = np.concatenate([inputs["ev_cmp_pos_k"][0].T, inputs["ev_cmp_pos_v"][0].T], 0)
    c = np.arange(512)
    ovl = np.zeros((512, 129), np.float32)
    for cc in range(511):
        for tk in range(16 * cc, 16 * cc + 32):
            ovl[cc, tk // 64] += 1.0 / 32
        ovl[cc, 128] = 1.0
    cur = np.arange(128)[:, None]
    n = np.arange(128)[None, :]
    elig = n <= cur
    forced = (n == 0) | (n == cur) | (n == cur - 1)
    A = elig.astype(np.float32)
    B = np.where(elig, np.where(forced, 1000.0, 0.0), -1.0).astype(np.float32)
    ABtab = np.stack([A, B], 1)
    lam = np.stack([inputs["ev_lam_q1"][0], inputs["ev_lam_k1"][0], inputs["ev_lam_q2"][0], inputs["ev_lam_k2"][0]])
    return {"qT4": qT4, "kcvT": kcvT, "ksA": ksA, "vsA": vsA, "kwT": kwT, "vwA": vwA, "gsel": gsel,
            "dqT": dqT, "dkT": dkT, "dvA": dvA, "cstrip": cstrip, "wstrip": wstrip, "cbias": cbias, "cb": cb,
            "w1kv": w1kv, "w2kv": w2kv, "posT": posT.astype(np.float32), "ovl": ovl, "ABtab": ABtab,
            "lam": lam.reshape(1, 256).astype(np.float32), "subln": inputs["ev_subln"][0].reshape(1, 128),
            "identb": np.eye(128, dtype=np.float32).astype(NPBF), "identf": np.eye(128, dtype=np.float32),
            "ones2": np.stack([np.concatenate([np.ones((128, 64)), np.zeros((128, 64))], 1),
                               np.concatenate([np.zeros((128, 64)), np.ones((128, 64))], 1)]).astype(NPBF)}


L2_SPECS = {
    "qT4": ([4, 64, S], BF16), "kcvT": ([128, S], BF16), "ksA": ([128, S], BF16), "vsA": ([S, 65], BF16),
    "kwT": ([128, S], BF16), "vwA": ([S, 65], BF16), "gsel": ([S, 2, 3], F32), "dqT": ([128, S], BF16),
    "dkT": ([128, S], BF16), "dvA": ([S, 128], BF16), "cstrip": ([3, 128, 1024], F32),
    "wstrip": ([2, 128, 1408], F32), "cbias": ([4, 5, 128, 512], F32), "cb": ([128, 5], F32),
    "w1kv": ([2, 2048, 128], F32), "w2kv": ([2, 128, 64], F32), "posT": ([128, 32], F32),
    "ovl": ([512, 129], F32), "ABtab": ([128, 2, 128], F32), "lam": ([1, 256], F32), "subln": ([1, 128], F32),
    "identb": ([128, 128], BF16), "identf": ([128, 128], F32), "ones2": ([2, 128, 128], BF16),
}


def build_l2(nch=16, do_nsa=True, do_diff=True):
    nc = bass.Bass("TRN2", target_bir_lowering=False)
    dr = {k: nc.dram_tensor(k, sh, dt, kind="ExternalInput").ap() for k, (sh, dt) in L2_SPECS.items()}
    dr["o"] = nc.dram_tensor("o", [S, 256], BF16, kind="ExternalOutput").ap()
    es = ExitStack()
    with es:
        s = Sched(nc, es)
        l2_body(nc, s, dr, nch, do_nsa, do_diff)
        s.emit()
    return nc


def l2_body(nc, s, dr, nch, do_nsa, do_diff):
    def sb(name, shape, dt):
        return s.buf(s.sbuf(name, shape, dt), name)

    def load(name, shape, dt, src=None, eng="sp"):
        b = sb(name, shape, dt)
        s.dma(eng, b[:], dr[name] if src is None else src, writes=[b])
        return b

    identb = load("identb", [128, 128], BF16)
    identf = load("identf", [128, 128], F32)
    cb = load("cb", [128, 5], F32)
    cstrip = [load(f"cstrip{k}", [128, 1024], F32, dr["cstrip"][k]) for k in range(3)]
    pS = Ring([s.pbuf(f"pS{i}", [128, 512], F32) for i in range(3)])
    pOa = s.pbuf("pOa", [128, 512], F32)
    pOb = s.pbuf("pOb", [128, 512], F32)
    pU0 = s.pbuf("pU0", [128, 512], F32)
    pU1 = s.pbuf("pU1", [128, 512], F32)
    pL = s.pbuf("pL", [128, 512], F32)
    P = Ring([sb(f"P{i}", [128, 512], BF16) for i in range(4)])
    tmp = Ring([sb(f"tmp{i}", [128, 512], F32) for i in range(3)])
    ost = Ring([sb(f"ost{i}", [128, 4, 256], BF16) for i in range(2)])
    ob = s.buf(dr["o"], "o")
    ov = dr["o"].rearrange("(c s p) d -> c p s d", p=128, s=4)

    def exp_tile(ps, j, strip_ap_fn, cb_ap):
        p = P.next()
        if strip_ap_fn is not None:
            t = tmp.next()
            sbuf_, ap = strip_ap_fn
            s.op("dve", lambda e, ps=ps, t=t, ap=ap: e.scalar_tensor_tensor(
                out=t[:], in0=ps[:], scalar=0.125, in1=ap, op0=ALU.mult, op1=ALU.add), [ps, sbuf_], [t])
            s.op("act", lambda e, t=t, p=p: e.activation(out=p[:], in_=t[:], func=AF.Exp), [t], [p])
        else:
            s.op("act", lambda e, ps=ps, p=p: e.activation(out=p[:], in_=ps[:], func=AF.Exp, bias=cb_ap, scale=0.125),
                 [ps, cb], [p])
        return p

    ksd = sb("ksd", [128, S], BF16)
    kcd = sb("kcd", [128, S], BF16)
    scr = sb("scr", [128, 8192], BF16)
    ob3 = Ring([sb(f"ob3_{i}", [128, 512], F32) for i in range(3)])
    if do_nsa:
        ksA = ksd
        s.dma("sp", ksd[:], dr["ksA"], writes=[ksd])
        vsA = sb("vsA", [128, 64 * 65 + 64], BF16)
        vwA = sb("vwA", [128, 64 * 65 + 64], BF16)
        for vv, nm_ in ((vsA, "vsA"), (vwA, "vwA")):
            s.op("pool", lambda e, vv=vv: e.memset(vv[:, 4160:4224], 0.0), [], [vv])
            s.dma("sp", vv[:, 0:4160].rearrange("p (t c) -> p t c", c=65), dr[nm_].rearrange("(t p) c -> p t c", p=128), writes=[vv])
        kwT = load("kwT", [128, S], BF16)
        wstrip = [load(f"wstrip{k}", [128, 1408], F32, dr["wstrip"][k]) for k in range(2)]
        cbias = [load(f"cbias{k}", [128, 5, 512], BF16, dr["cbias"][k].rearrange("o p x -> p o x"), eng="pool") for k in range(4)]
        gates = load("gates", [128, 64, 6], F32, dr["gsel"].rearrange("(t p) h b -> p t (h b)", p=128))
        s.op("act", lambda e: e.activation(out=gates[:], in_=gates[:], func=AF.Sigmoid), [gates], [gates])
        kcvT = kcd
        s.dma("sp", kcd[:], dr["kcvT"], writes=[kcd])
        w1kv = s.buf(scr.t[:, 0:4096].rearrange("p (j m) -> p j m", m=128), "w1kv_view")
        for m in range(2):
            s.dma("pool", w1kv[64 * m:64 * m + 64, :, :], dr["w1kv"][m].rearrange("(j d) m -> d j m", d=64), writes=[scr])
        w2kv = sb("w2kv", [128, 2, 64], BF16)
        s.dma("pool", w2kv[:], dr["w2kv"].rearrange("t k d -> k t d"), writes=[w2kv])
        posT = sb("posT", [128, 32], BF16)
        s.dma("pool", posT[:], dr["posT"], writes=[posT])
        R = sb("R", [128, 4, 193], BF16)
        s.dma("pool", R[:, :, 0:129], dr["ovl"].rearrange("(j p) n -> p j n", p=128), writes=[R])
        KcT = sb("KcT", [64, 512], BF16)
        gl = [P.items[0], P.items[1]]
        xs = tmp.items[0]
        x2 = tmp.items[1]
        for m in range(2):
            ps = pS.next()
            lo = 64 * m
            for j in range(32):
                s.op("pe", lambda e, j=j, lo=lo, ps=ps: e.matmul(
                    ps[:, 0:511], lhsT=w1kv[lo:lo + 64, j, :], rhs=kcvT[lo:lo + 64, j:j + 8161:16],
                    start=(j == 0), stop=False), [scr, kcvT], [ps])
            for j in range(32):
                s.op("pe", lambda e, j=j, lo=lo, ps=ps: e.matmul(
                    ps[:, 0:511], lhsT=w1kv[lo:lo + 64, j, :], rhs=posT[lo:lo + 64, j:j + 1].to_broadcast([64, 511]),
                    start=False, stop=(j == 31)), [scr, posT], [ps])
            s.op("pool", lambda e, m=m: e.memset(gl[m][:], 0.0), [], [gl[m]])
            s.op("act", lambda e, ps=ps: e.copy(out=xs[:, 0:511], in_=ps[:, 0:511]), [ps], [xs])
            s.op("dve", lambda e: e.tensor_tensor(out=x2[:, 0:511], in0=xs[:, 0:511], in1=xs[:, 0:511], op=ALU.mult), [xs], [x2])
            s.op("dve", lambda e: e.tensor_scalar(out=x2[:, 0:511], in0=x2[:, 0:511], scalar1=0.044715, scalar2=1.0,
                                                  op0=ALU.mult, op1=ALU.add), [x2], [x2])
            s.op("dve", lambda e: e.tensor_tensor(out=x2[:, 0:511], in0=x2[:, 0:511], in1=xs[:, 0:511], op=ALU.mult), [x2, xs], [x2])
            s.op("act", lambda e: e.activation(out=x2[:, 0:511], in_=x2[:, 0:511], func=AF.Sigmoid, scale=2.0 * GELU_C), [x2], [x2])
            s.op("dve", lambda e, m=m: e.tensor_tensor(out=gl[m][:, 0:511], in0=x2[:, 0:511], in1=xs[:, 0:511], op=ALU.mult),
                 [x2, xs], [gl[m]])
        ps = pS.next()
        s.op("pe", lambda e, ps=ps: e.matmul(ps[0:64, :], lhsT=w2kv[:, 0, :], rhs=gl[0][:], start=True, stop=True),
             [w2kv, gl[0]], [ps])
        s.op("act", lambda e, ps=ps: e.copy(out=KcT[:], in_=ps[0:64, :]), [ps], [KcT])
        ps = pS.next()
        for jt in range(4):
            s.op("pe", lambda e, jt=jt, ps=ps: e.matmul(ps[:, jt * 64:(jt + 1) * 64], lhsT=gl[1][:, jt * 128:(jt + 1) * 128],
                                                        rhs=w2kv[:, 1, :], start=True, stop=True), [gl[1], w2kv], [ps])
        s.op("act", lambda e, ps=ps: e.copy(out=R[:, :, 129:193], in_=ps[:, 0:256].rearrange("p (j d) -> p j d", d=64)),
             [ps], [R])
        QA = [[sb(f"QA{k}_{hf}_{i}", [128, 512], BF16) for i in range(2)] for k in range(2) for hf in range(2)]
        Qc = [Ring([sb(f"Qc{k}_{i}", [64, 512], BF16) for i in range(2)]) for k in range(2)]
        ABt = Ring([sb(f"ABt{i}", [128, 2, 128], F32) for i in range(3)])
        rsu = Ring([sb(f"rsu{i}", [128, 4], F32) for i in range(2)])
        imp = Ring([sb(f"imp{i}", [128, 128], F32) for i in range(2)])
        sc2 = Ring([sb(f"sc2{i}", [128, 128], F32) for i in range(2)])
        mx = Ring([sb(f"mx{i}", [128, 16], F32) for i in range(2)])
        NM = Ring([sb(f"NM{i}", [128, 192], BF16) for i in range(2)])
        for nm in NM.items + [q_ for grp in QA for q_ in grp]:
            s.op("pool", lambda e, nm=nm: e.memset(nm[:], 0.0), [], [nm])
        ocmp = Ring([sb(f"ocmp{i}", [128, 4, 2, 64], F32) for i in range(2)])
        Osb = ob3
        rs2 = Ring([sb(f"rs2{i}", [128, 4, 1], F32) for i in range(3)])
        acc = Ring([sb(f"acc{i}", [128, 4, 64], F32) for i in range(2)])
        t2b = Ring([sb(f"t2b{i}", [128, 4, 64], F32) for i in range(2)])
        wg = Ring([sb(f"wg{i}", [128, 4, 1], F32) for i in range(4)])
        Ecm = [[scr.t[:, (r * 4 + jt) * 512:(r * 4 + jt + 1) * 512] for jt in range(4)] for r in range(4)]

    if do_diff:
        ones2 = load("ones2", [128, 2, 128], BF16, dr["ones2"].rearrange("m p c -> p m c"))
        lamv = load("lamv", [128, 256], F32, dr["lam"].to_broadcast([128, 256]))
        gsub = load("gsub", [128, 128], F32, dr["subln"].to_broadcast([128, 128]))
        s.op("dve", lambda e: e.tensor_scalar(out=gsub[:], in0=gsub[:], scalar1=1.0 - LAM_INIT0, scalar2=None, op0=ALU.mult),
             [gsub], [gsub])
        lt = sb("lt", [128, 128], F32)
        l2s = sb("l2s", [128, 2], F32)
        nlam = sb("nlam", [128, 1], F32)
        lv = lamv[:].rearrange("p (a d) -> p a d", d=64)
        s.op("dve", lambda e: e.tensor_tensor(out=lt[:].rearrange("p (a d) -> p a d", d=64), in0=lv[:, 0:4:2, :],
                                              in1=lv[:, 1:4:2, :], op=ALU.mult), [lamv], [lt])
        s.op("dve", lambda e: e.tensor_reduce(out=l2s[:], in_=lt[:].rearrange("p (a d) -> p a d", d=64), axis=AX.X,
                                              op=ALU.add), [lt], [l2s])
        s.op("act", lambda e: e.activation(out=l2s[:], in_=l2s[:], func=AF.Exp), [l2s], [l2s])
        s.op("dve", lambda e: e.tensor_tensor(out=nlam[:], in0=l2s[:, 1:2], in1=l2s[:, 0:1], op=ALU.subtract), [l2s], [nlam])
        s.op("dve", lambda e: e.tensor_scalar(out=nlam[:], in0=nlam[:], scalar1=-LAM_INIT0, scalar2=None, op0=ALU.add),
             [nlam], [nlam])
        dq = Ring([[sb(f"dq{i}_{m}", [128, 512], BF16) for m in range(2)] for i in range(2)])
        for pair_ in dq.items:
            for b_ in pair_:
                s.op("pool", lambda e, b_=b_: e.memset(b_[:], 0.0), [], [b_])
        Od = ob3
        rd = Ring([sb(f"rd{i}", [128, 4, 2], F32) for i in range(2)])
        o0 = Ring([sb(f"o0{i}", [128, 4, 128], F32) for i in range(1)])
        av = Ring([sb(f"av{i}", [128, 4, 128], F32) for i in range(1)])
        sq = sb("sq", [128, 4, 128], F32)
        ssd = Ring([sb(f"ssd{i}", [128, 4], F32) for i in range(2)])

    ovn = dr["o"][:, 0:128].rearrange("(c s p) d -> c p s d", p=128, s=4)
    ovd = dr["o"][:, 128:256].rearrange("(c s p) d -> c p s d", p=128, s=4)
    pF = pU1
    st = {}

    def pre_gen(i):
        qa = [[QA[k * 2 + hf][i % 2] for hf in range(2)] for k in range(2)]
        nhf = 2 if i >= 8 else 1
        qsrc = []
        for k in range(2):
            for hf in range(nhf):
                s.dma("sp", qa[k][hf][0:64, :], dr["qT4"][k][:, i * 512:(i + 1) * 512], writes=[qa[k][hf]])
            qsrc.append((qa[k][0], qa[k][0][0:64, :]))
        for k in range(2):
            q = Qc[k].next()
            s.dma("sp", q[:], dr["qT4"][2 + k][:, i * 512:(i + 1) * 512], writes=[q])
            qsrc.append((q, q[:]))
        oc = ocmp.next()
        st[i] = (qa, oc)
        yield
        ncj = min(4, (512 * i + 511 - 31) // 16 // 128 + 1)
        for r in range(4):
            qb, qap = qsrc[r]
            for jt in range(ncj):
                ps = pS.next()
                s.op("pe", lambda e, ps=ps, jt=jt, qap=qap: e.matmul(
                    ps[:], lhsT=KcT[:, jt * 128:(jt + 1) * 128], rhs=qap, start=True, stop=True), [KcT, qb], [ps])
                o_ = 512 * i - 2048 * jt
                ec = Ecm[r][jt]
                if o_ <= 2048:
                    t = tmp.next()
                    s.op("dve", lambda e, ps=ps, t=t, r=r, o_=o_: e.scalar_tensor_tensor(
                        out=t[:], in0=ps[:], scalar=0.125, in1=cbias[r][:, o_ // 512, :], op0=ALU.mult, op1=ALU.add),
                        [ps, cbias[r]], [t])
                    s.op("act", lambda e, t=t, ec=ec: e.activation(out=ec, in_=t[:], func=AF.Exp), [t], [scr])
                else:
                    s.op("act", lambda e, ps=ps, ec=ec, r=r: e.activation(out=ec, in_=ps[:], func=AF.Exp,
                                                                        bias=cb[:, r:r + 1], scale=0.125), [ps, cb], [scr])
                yield
        for sub in range(4):
            tt = 4 * i + sub
            ab = ABt.next()
            for hh in range(2):
                s.dma("sp", ab[64 * hh:64 * hh + 64, :, :], dr["ABtab"][2 * tt + hh:2 * tt + hh + 1].to_broadcast([64, 2, 128]),
                      writes=[ab])
            ru = rsu.next()
            im = imp.next()
            for pair in range(2):
                for r in (2 * pair, 2 * pair + 1):
                    c0 = (r % 2) * 193
                    for jt in range(ncj):
                        s.op("pe", lambda e, c0=c0, r=r, jt=jt, sub=sub, ncj=ncj: e.matmul(
                            pU0[:, c0:c0 + 193], lhsT=Ecm[r][jt][:, sub * 128:(sub + 1) * 128], rhs=R[:, jt, :],
                            start=(jt == 0), stop=(jt == ncj - 1)), [scr, R], [pU0])
                yield
                s.op("dve", lambda e, pair=pair, ru=ru: e.tensor_scalar(
                    out=ru[:, 2 * pair:2 * pair + 2], in0=pU0[:, 0:386].rearrange("p (h c) -> p h c", c=193)[:, :, 128],
                    scalar1=1e-30, scalar2=None, op0=ALU.max), [pU0], [ru])
                s.op("dve", lambda e, pair=pair, ru=ru: e.reciprocal(out=ru[:, 2 * pair:2 * pair + 2],
                                                                     in_=ru[:, 2 * pair:2 * pair + 2]), [ru], [ru])
                for r in (2 * pair, 2 * pair + 1):
                    c0 = (r % 2) * 193
                    if r == 0:
                        s.op("dve", lambda e, im=im, ru=ru: e.tensor_scalar(out=im[:], in0=pU0[:, 0:128], scalar1=ru[:, 0:1],
                                                                             scalar2=None, op0=ALU.mult), [pU0, ru], [im])
                    else:
                        s.op("dve", lambda e, im=im, ru=ru, c0=c0, r=r: e.scalar_tensor_tensor(
                            out=im[:], in0=pU0[:, c0:c0 + 128], scalar=ru[:, r:r + 1], in1=im[:], op0=ALU.mult, op1=ALU.add),
                            [pU0, ru, im], [im])
                if pair == 0:
                    for k in range(2):
                        c0 = k * 193
                        s.op("dve", lambda e, k=k, c0=c0, oc=oc, ru=ru, sub=sub: e.tensor_scalar(
                            out=oc[:, sub, k, :], in0=pU0[:, c0 + 129:c0 + 193], scalar1=ru[:, k:k + 1], scalar2=None,
                            op0=ALU.mult), [pU0, ru], [oc])
                yield
            s.op("dve", lambda e, im=im, ab=ab: e.tensor_tensor(out=im[:], in0=im[:], in1=ab[:, 0, :], op=ALU.mult), [im, ab], [im])
            s.op("dve", lambda e, im=im, ab=ab: e.tensor_tensor(out=im[:], in0=im[:], in1=ab[:, 1, :], op=ALU.add), [im, ab], [im])
            m8 = mx.next()
            s2 = sc2.next()
            s.op("dve", lambda e, im=im, m8=m8: e.max(out=m8[:, 0:8], in_=im[:]), [im], [m8])
            s.op("dve", lambda e, im=im, m8=m8, s2=s2: e.match_replace(out=s2[:], in_to_replace=m8[:, 0:8], in_values=im[:],
                                                                       imm_value=-2.0), [im, m8], [s2])
            yield
            s.op("dve", lambda e, m8=m8, s2=s2: e.max(out=m8[:, 8:16], in_=s2[:]), [s2, m8], [m8])
            s.op("dve", lambda e, im=im, m8=m8, s2=s2: e.tensor_scalar(out=s2[:], in0=im[:], scalar1=m8[:, 15:16], scalar2=None,
                                                                       op0=ALU.is_ge), [im, m8], [s2])
            nm = NM.next()
            s.op("dve", lambda e, s2=s2, nm=nm: e.tensor_scalar(out=nm[:, 64:192], in0=s2[:], scalar1=1.0, scalar2=-NEGM,
                                                                op0=ALU.subtract, op1=ALU.mult), [s2], [nm])
            yield
            for hf in range(nhf):
                s.op("pe", lambda e, nm=nm, hf=hf: e.matmul(pL[:, hf * 128:(hf + 1) * 128], lhsT=nm[:, hf * 64:hf * 64 + 128],
                                                             rhs=identb[:], start=True, stop=True), [nm, identb], [pL])
            for hf in range(nhf):
                for k in range(2):
                    q = qa[k][hf]
                    s.op("act", lambda e, q=q, hf=hf, sub=sub: e.copy(out=q[64:128, sub * 128:(sub + 1) * 128],
                                                                      in_=pL[64:128, hf * 128:(hf + 1) * 128]), [pL], [q])
            yield

    def advance(gen, n):
        if gen is None:
            return None
        try:
            for _ in range(n):
                next(gen)
        except StopIteration:
            return None
        return gen

    stream = TileStream(s, la=2)
    if do_nsa:
        g0 = pre_gen(0)
        while g0 is not None:
            g0 = advance(g0, 1000)
    for i in range(nch if do_nsa else 0):
        oo = ost.next()
        nkt = 4 * i + 4
        qa, oc = st[i]
        gen = pre_gen(i + 1) if i + 1 < nch else None
        kw0 = max(0, 4 * i - 4)
        ntile = 2 * (nkt + (nkt - kw0))
        per = -(-48 // ntile)
        for k in range(2):
            for kt in range(nkt):
                j = kt - 4 * i
                hf = 0 if kt < 32 else 1
                q = qa[k][hf]

                def qk(kt=kt, j=j, q=q, k=k):
                    ps = pS.next()
                    s.op("pe", lambda e, ps=ps, kt=kt, q=q: e.matmul(ps[:], lhsT=ksA[:, kt * 128:(kt + 1) * 128], rhs=q[:],
                                                                      start=True, stop=True), [ksA, q], [ps])
                    if j >= -1:
                        y0 = 384 - 128 * j
                        return exp_tile(ps, j, (cstrip[k], cstrip[k][:, y0:y0 + 512]), None)
                    return exp_tile(ps, j, None, cb[:, k:k + 1])

                def pv(p, kt=kt, nkt=nkt):
                    s.op("pe", lambda e, p=p, kt=kt, nkt=nkt: e.matmul(pOa[:], lhsT=vsA[:, kt * 65:kt * 65 + 128], rhs=p[:],
                                                                        start=(kt == 0), stop=(kt == nkt - 1)), [vsA, p], [pOa])
                stream.push(qk, pv)
                gen = advance(gen, per)
            q = qa[k][0]
            for kt in range(kw0, nkt):
                j = kt - 4 * i

                def qk(kt=kt, j=j, q=q, k=k):
                    ps = pS.next()
                    s.op("pe", lambda e, ps=ps, kt=kt, q=q: e.matmul(ps[:], lhsT=kwT[:, kt * 128:(kt + 1) * 128], rhs=q[:],
                                                                      start=True, stop=True), [kwT, q], [ps])
                    y0 = 384 - 128 * j
                    return exp_tile(ps, j, (wstrip[k], wstrip[k][:, y0:y0 + 512]), None)

                def pv(p, kt=kt, kw0=kw0, nkt=nkt):
                    s.op("pe", lambda e, p=p, kt=kt, kw0=kw0, nkt=nkt: e.matmul(pOb[:], lhsT=vwA[:, kt * 65:kt * 65 + 128], rhs=p[:],
                                                                                 start=(kt == kw0), stop=(kt == nkt - 1)), [vwA, p], [pOb])
                stream.push(qk, pv)
                gen = advance(gen, per)

            def fin_a():
                o1 = Osb.next()
                o2 = Osb.next()
                s.op("dve", lambda e, o1=o1: e.tensor_copy(out=o1[0:65, :], in_=pOa[0:65, :]), [pOa], [o1])
                s.op("act", lambda e, o2=o2: e.copy(out=o2[0:65, :], in_=pOb[0:65, :]), [pOb], [o2])
                return (o1, o2)

            def fin_b(os_, k=k, i=i, oc=oc, oo=oo):
                a = acc.next()
                for bi, gi in ((0, 1), (1, 2)):
                    osb = os_[bi]
                    for sub in range(4):
                        s.op("pe", lambda e, sub=sub, osb=osb: e.transpose(
                            out=pF[:, sub * 65:(sub + 1) * 65], in_=osb[0:65, sub * 128:(sub + 1) * 128], identity=identf[0:65, 0:65]),
                            [osb, identf], [pF])
                    puv = pF[:, 0:260].rearrange("p (s c) -> p s c", c=65)
                    r2 = rs2.next()
                    w = wg.next()
                    s.op("dve", lambda e, r2=r2, puv=puv: e.reciprocal(out=r2[:], in_=puv[:, :, 64:65]), [pF], [r2])
                    s.op("dve", lambda e, r2=r2, w=w, k=k, gi=gi, i=i: e.tensor_tensor(
                        out=w[:], in0=r2[:], in1=gates[:, 4 * i:4 * i + 4, k * 3 + gi:k * 3 + gi + 1], op=ALU.mult), [r2, gates], [w])
                    if bi == 0:
                        s.op("dve", lambda e, a=a, puv=puv, w=w: e.tensor_tensor(out=a[:], in0=puv[:, :, 0:64],
                                                                                 in1=w[:].to_broadcast([128, 4, 64]), op=ALU.mult), [pF, w], [a])
                    else:
                        t2 = t2b.next()
                        s.op("dve", lambda e, t2=t2, puv=puv, w=w: e.tensor_tensor(out=t2[:], in0=puv[:, :, 0:64],
                                                                                   in1=w[:].to_broadcast([128, 4, 64]), op=ALU.mult), [pF, w], [t2])
                        s.op("pool", lambda e, a=a, t2=t2: e.tensor_tensor(out=a[:], in0=a[:], in1=t2[:], op=ALU.add), [a, t2], [a])
                t3 = t2b.next()
                s.op("pool", lambda e, t3=t3, oc=oc, k=k, i=i: e.tensor_tensor(
                    out=t3[:], in0=oc[:, :, k, :], in1=gates[:, 4 * i:4 * i + 4, k * 3:k * 3 + 1].to_broadcast([128, 4, 64]), op=ALU.mult),
                    [oc, gates], [t3])
                s.op("pool", lambda e, t3=t3, a=a, oo=oo, k=k: e.tensor_tensor(out=oo[:, :, k * 64:(k + 1) * 64], in0=a[:], in1=t3[:],
                                                                              op=ALU.add), [a, t3], [oo])
                if k == 1:
                    s.dma("sp", ovn[i], oo[:, :, 0:128], reads=[oo], writes=[ob])
            stream.push_fin(fin_a, fin_b)
        while gen is not None:
            gen = advance(gen, 1000)
    stream.flush()

    if do_diff:
        dkT = kcd
        s.dma("sp", kcd[:], dr["dkT"], writes=[kcd])
        dvA = s.buf(ksd.t[:, :].rearrange("p (t c) -> p t c", c=128), "dvA_view")
        s.dma("sp", dvA[:], dr["dvA"].rearrange("(t p) c -> p t c", p=128), writes=[ksd])
        pOm = [pOa, pOb]
        pUm = [pU0, pU1]
        dqs = {}

        def load_dq(i):
            dqc = dq.next()
            for m in range(2):
                s.dma("sp", dqc[m][64 * m:64 * m + 64, :], dr["dqT"][64 * m:64 * m + 64, i * 512:(i + 1) * 512], writes=[dqc[m]])
            dqs[i] = dqc
        load_dq(0)
    for i in range(nch if do_diff else 0):
        oo = ost.next()
        nkt = 4 * i + 4
        dqc = dqs[i]
        if i + 1 < nch:
            load_dq(i + 1)
        for m in range(2):
            lo = 64 * m
            for kt in range(nkt):
                j = kt - 4 * i

                def qk(kt=kt, j=j, dqm=dqc[m]):
                    ps = pS.next()
                    s.op("pe", lambda e, ps=ps, kt=kt, dqm=dqm: e.matmul(
                        ps[:], lhsT=dkT[:, kt * 128:(kt + 1) * 128], rhs=dqm[:], start=True, stop=True),
                        [dkT, dqm], [ps])
                    if j >= -1:
                        y0 = 384 - 128 * j
                        return exp_tile(ps, j, (cstrip[2], cstrip[2][:, y0:y0 + 512]), None)
                    return exp_tile(ps, j, None, cb[:, 4:5])

                def pv(p, kt=kt, m=m, nkt=nkt):
                    s.op("pe", lambda e, p=p, kt=kt, m=m, nkt=nkt: e.matmul(pOm[m][:], lhsT=dvA[:, kt, :], rhs=p[:],
                                                                             start=(kt == 0), stop=(kt == nkt - 1)), [ksd, p], [pOm[m]])
                    s.op("pe", lambda e, p=p, kt=kt, m=m, nkt=nkt: e.matmul(
                        pL[:], lhsT=ones2[:, m, :], rhs=p[:], start=(m == 0 and kt == 0), stop=(m == 1 and kt == nkt - 1)),
                        [ones2, p], [pL])
                stream.push(qk, pv)

        def fin_a():
            od = [Od.next(), Od.next(), Od.next()]
            s.op("dve", lambda e, od=od: e.tensor_copy(out=od[0][:], in_=pOa[:]), [pOa], [od[0]])
            s.op("act", lambda e, od=od: e.copy(out=od[1][:], in_=pOb[:]), [pOb], [od[1]])
            s.op("act", lambda e, od=od: e.copy(out=od[2][:], in_=pL[:]), [pL], [od[2]])
            return od

        def fin_b(od, i=i, oo=oo):
            for m in range(2):
                for sub in range(4):
                    s.op("pe", lambda e, m=m, sub=sub, od=od: e.transpose(
                        out=pUm[m][:, sub * 128:(sub + 1) * 128], in_=od[m][:, sub * 128:(sub + 1) * 128], identity=identf[:]),
                        [od[m], identf], [pUm[m]])
            pq = pS.next()
            for sub in range(4):
                s.op("pe", lambda e, sub=sub, pq=pq, od=od: e.transpose(out=pq[:, sub * 128:(sub + 1) * 128], in_=od[2][:, sub * 128:(sub + 1) * 128],
                                                                        identity=identf[:]), [od[2], identf], [pq])
            r = rd.next()
            s.op("dve", lambda e, r=r, pq=pq: e.reciprocal(out=r[:], in_=pq[:].rearrange("p (s m c) -> p s m c", m=2, c=64)[:, :, :, 0]), [pq], [r])
            s.op("dve", lambda e, r=r: e.tensor_scalar(out=r[:, :, 1:2], in0=r[:, :, 1:2], scalar1=nlam[:, 0:1], scalar2=None,
                                                        op0=ALU.mult), [r, nlam], [r])
            o0b = o0.next()
            a = av.next()
            pv0 = pU0[:].rearrange("p (s c) -> p s c", c=128)
            pv1 = pU1[:].rearrange("p (s c) -> p s c", c=128)
            s.op("dve", lambda e, o0b=o0b, r=r, pv0=pv0: e.tensor_tensor(out=o0b[:], in0=pv0, in1=r[:, :, 0:1].to_broadcast([128, 4, 128]),
                                                                         op=ALU.mult), [pU0, r], [o0b])
            s.op("dve", lambda e, a=a, r=r, pv1=pv1: e.tensor_tensor(out=a[:], in0=pv1, in1=r[:, :, 1:2].to_broadcast([128, 4, 128]),
                                                                     op=ALU.mult), [pU1, r], [a])
            s.op("pool", lambda e, a=a, o0b=o0b: e.tensor_tensor(out=a[:], in0=a[:], in1=o0b[:], op=ALU.add), [a, o0b], [a])
            s.op("pool", lambda e, a=a: e.tensor_tensor(out=sq[:], in0=a[:], in1=a[:], op=ALU.mult), [a], [sq])
            sv = ssd.next()
            s.op("dve", lambda e, sv=sv: e.tensor_reduce(out=sv[:], in_=sq[:], axis=AX.X, op=ALU.add), [sq], [sv])
            s.op("dve", lambda e, sv=sv: e.tensor_scalar(out=sv[:], in0=sv[:], scalar1=1.0 / 128, scalar2=SUB_EPS, op0=ALU.mult,
                                                         op1=ALU.add), [sv], [sv])
            s.op("act", lambda e, sv=sv: e.activation(out=sv[:], in_=sv[:], func=AF.Sqrt), [sv], [sv])
            s.op("dve", lambda e, sv=sv: e.reciprocal(out=sv[:], in_=sv[:]), [sv], [sv])
            s.op("dve", lambda e, a=a, sv=sv: e.tensor_tensor(out=a[:], in0=a[:], in1=sv[:].unsqueeze(2).to_broadcast([128, 4, 128]),
                                                              op=ALU.mult), [a, sv], [a])
            s.op("dve", lambda e, a=a, oo=oo: e.tensor_tensor(out=oo[:, :, 128:256], in0=a[:],
                                                              in1=gsub[:].unsqueeze(1).to_broadcast([128, 4, 128]), op=ALU.mult),
                 [a, gsub], [oo])
            s.dma("sp", ovd[i], oo[:, :, 128:256], reads=[oo], writes=[ob])
        stream.push_fin(fin_a, fin_b)
    stream.flush()
    s.finish_outputs([ob])


_CACHE = {}


def _prog(key, fn):
    if key not in _CACHE:
        _CACHE[key] = fn()
    return _CACHE[key]


def kernel(**inputs):
    inputs = {k: np.asarray(v) for k, v in inputs.items()}
    x = np.ascontiguousarray(inputs["x"], dtype=np.float32).reshape(8, NT, D)
    bt = inputs["bias_table"]
    identb = np.eye(128, dtype=np.float32).astype(NPBF)
    cores = list(range(8))
    nc1 = _prog("l1", lambda: build_token_phase(False, False, 2840, False, False, gate_rows=(1280, 1304)))
    r1 = run_bass_kernel_spmd(nc1, [{"x": x[c], "ident": identb, "g_mix": inputs["norm_mix"][0:1],
                                     "w_in": inputs["ev_w_in"][0]} for c in cores], core_ids=cores).results
    projT = [np.concatenate([r1[b * 4 + q]["projT"] for q in range(4)], axis=1) for b in range(2)]
    gT = [np.concatenate([r1[b * 4 + q]["gT"] for q in range(4)], axis=1) for b in range(2)]
    nc2 = _prog("l2", lambda: build_l2(16, True, True))
    in2 = []
    for c in cores:
        b, g, half, hd = c // 4, (c % 4) // 2, c % 2, c % 4
        in2.append(l2_host_inputs(projT[b], gT[b], inputs, g, half, hd))
    r2 = run_bass_kernel_spmd(nc2, in2, core_ids=cores).results
    o0 = np.zeros((2, S, D), dtype=NPBF)
    for c in cores:
        b, g, half, hd = c // 4, (c % 4) // 2, c % 2, c % 4
        oc = r2[c]["o"]
        for k in range(2):
            h = g * 4 + 2 * half + k
            o0[b, :, h * 64:(h + 1) * 64] = oc[:, k * 64:(k + 1) * 64]
        o0[b, :, 512 + hd * 128:512 + (hd + 1) * 128] = oc[:, 128:256]
    o0 = o0.reshape(8, NT, D)
    nc3 = _prog("l3", lambda: build_token_phase(True, True, 3072, False, True))
    r3 = run_bass_kernel_spmd(nc3, [{"x": x[c], "ident": identb, "oT": np.ascontiguousarray(o0[c].T),
                                     "w_out": inputs["ev_w_out"][0], "g_mlp": inputs["norm_mlp"][0:1],
                                     "w1": inputs["mlp_w1"][0], "w2": inputs["mlp_w2"][0],
                                     "g_mix": inputs["norm_mix"][1:2], "w_in": inputs["od_w_in"][0]}
                                    for c in cores], core_ids=cores).results
    x1 = [r3[c]["x_o"] for c in cores]
    proj1T = [np.concatenate([r3[b * 4 + q]["projT"] for q in range(4)], axis=1) for b in range(2)]
    nc4 = _prog("l4", lambda: build_moba(16, 4))
    in4 = [moba_host_inputs(proj1T[c // 4], bt, [4 * (c % 4) + k for k in range(4)]) for c in cores]
    r4 = run_bass_kernel_spmd(nc4, in4, core_ids=cores).results
    o1 = np.zeros((2, S, D), dtype=NPBF)
    for c in cores:
        o1[c // 4, :, (c % 4) * 256:(c % 4 + 1) * 256] = r4[c]["o"]
    o1 = o1.reshape(8, NT, D)
    nc5 = _prog("l5", lambda: build_token_phase(True, True, 0, True, False))
    r5 = run_bass_kernel_spmd(nc5, [{"x": x1[c], "ident": identb, "oT": np.ascontiguousarray(o1[c].T),
                                     "w_out": inputs["od_w_out"][0], "g_mlp": inputs["norm_mlp"][1:2],
                                     "w1": inputs["mlp_w1"][1], "w2": inputs["mlp_w2"][1],
                                     "g_fin": inputs["norm_final"].reshape(1, D)} for c in cores], core_ids=cores).results
    out = np.stack([r5[c]["out"] for c in cores]).reshape(2, S, D).astype(np.float32)
    return out
```

```python
import numpy as np
import ml_dtypes
from contextlib import ExitStack
import concourse.bass as bass
import concourse.mybir as mybir
from concourse.bass_utils import run_bass_kernel_spmd

F32 = mybir.dt.float32
BF16 = mybir.dt.bfloat16
AF = mybir.ActivationFunctionType
ALU = mybir.AluOpType
AX = mybir.AxisListType
NPBF = ml_dtypes.bfloat16


class Buf:
    __slots__ = ("name", "t", "lw", "rd", "excl")

    def __init__(self, name, t, excl=False):
        self.name = name
        self.t = t
        self.excl = excl
        self.lw = None
        self.rd = {}

    def __getitem__(self, idx):
        return self.t[idx]


class Sched:
    ENGS = ("pe", "act", "dve", "pool", "sp")
    EPOCH = 12000
    NDMA = 24

    def __init__(self, nc, es):
        self.nc = nc
        self.es = es
        self.ops = {e: [] for e in self.ENGS}
        self.n = {e: 0 for e in self.ENGS}
        self.waited = {e: {} for e in self.ENGS}
        self.signal = {e: set() for e in self.ENGS}
        self.dma_uses = [0] * self.NDMA
        self.dma_i = 0
        self.nbuf = 0

    def sbuf(self, name, shape, dtype):
        t = self.es.enter_context(self.nc.sbuf_tensor("sb_" + name, list(shape), dtype))
        return t

    def psum(self, name, shape, dtype):
        t = self.es.enter_context(self.nc.psum_tensor("ps_" + name, list(shape), dtype))
        return t

    def buf(self, t, name=None, excl=False):
        self.nbuf += 1
        return Buf(name or f"b{self.nbuf}", t, excl)

    def pbuf(self, name, shape, dtype):
        return self.buf(self.psum(name, shape, dtype), name, excl=True)

    def _need(self, eng, tok, same_raw=False):
        if tok is None:
            return
        stream, v = tok
        if stream == eng:
            if not same_raw:
                return
            if self.n[eng] - v > 3:
                return
        if self.waited[eng].get(stream, -1) >= v:
            return
        self.waited[eng][stream] = v
        if isinstance(stream, str):
            self.signal[stream].add(v)
        self.ops[eng].append(("w", tok))

    def _deps(self, eng, reads, writes):
        for b in reads:
            self._need(eng, b.lw, same_raw=True)
            if b.excl:
                for tok in b.rd.values():
                    self._need(eng, tok, same_raw=False)
        for b in writes:
            self._need(eng, b.lw, same_raw=False)
            for tok in b.rd.values():
                self._need(eng, tok, same_raw=False)

    def _commit(self, tok, reads, writes):
        stream = tok[0]
        for b in reads:
            b.rd[stream] = tok
        for b in writes:
            b.lw = tok
            b.rd = {}

    def op(self, eng, fn, reads=(), writes=()):
        self._deps(eng, reads, writes)
        idx = self.n[eng]
        self.n[eng] += 1
        self.ops[eng].append(("i", fn, idx))
        self._commit((eng, idx), reads, writes)

    def dma(self, eng, out, in_, reads=(), writes=(), **kw):
        k = self.dma_i % self.NDMA
        self.dma_i += 1
        stream = ("dma", k)
        prev = self.dma_uses[k]
        self._deps(eng, reads, writes)
        if prev > 0:
            self._need(eng, (stream, prev * 16))
        self.dma_uses[k] = prev + 1
        val = (prev + 1) * 16
        self.ops[eng].append(("d", (out, in_, kw), k, val))
        self._commit((stream, val), reads, writes)

    def finish_outputs(self, bufs, eng="sp"):
        for b in bufs:
            self._need(eng, b.lw)

    def emit(self):
        nc = self.nc
        engobj = {"pe": nc.tensor, "act": nc.scalar, "dve": nc.vector,
                  "pool": nc.gpsimd, "sp": nc.sync}
        sigval = {}
        sems = {}
        for e in self.ENGS:
            cnt = 0
            for idx in sorted(self.signal[e]):
                ep = cnt // self.EPOCH
                sigval[(e, idx)] = ((e, ep), cnt % self.EPOCH + 1)
                cnt += 1
                if (e, ep) not in sems:
                    sems[(e, ep)] = self.es.enter_context(nc.semaphore(f"s_{e}_{ep}"))
        for k in range(self.NDMA):
            if self.dma_uses[k]:
                sems[("dma", k)] = self.es.enter_context(nc.semaphore(f"s_dma_{k}"))
        self.nsems = len(sems)
        block = self.es.enter_context(nc.Block())
        deco = {"pe": block.tensor, "act": block.scalar, "dve": block.vector,
                "pool": block.gpsimd, "sp": block.sync}

        def make(e):
            ops = self.ops[e]

            def body(eng):
                for o in ops:
                    if o[0] == "w":
                        stream, v = o[1]
                        if isinstance(stream, str):
                            sk, sv = sigval[(stream, v)]
                            eng.wait_ge(sems[sk], sv)
                        else:
                            eng.wait_ge(sems[stream], v)
                    elif o[0] == "i":
                        ins = o[1](eng)
                        sv = sigval.get((e, o[2]))
                        if sv is not None:
                            ins.then_inc(sems[sv[0]], 1)
                    else:
                        out, in_, kw = o[1]
                        eng.dma_start(out=out, in_=in_, **kw).then_inc(sems[("dma", o[2])], 16)
            return body

        for e in self.ENGS:
            if self.ops[e]:
                deco[e](make(e))


STAGE = 99

NT = 2048
D = 1024
EPS = 1e-6


class Ring:
    def __init__(self, items):
        self.items = list(items)
        self.i = 0

    def next(self):
        b = self.items[self.i % len(self.items)]
        self.i += 1
        return b


def build_token_phase(attn_in, mlp, proj_cols, final, x_out, gate_rows=None):
    nc = bass.Bass("TRN2", target_bir_lowering=False)
    dr = {}
    dr["x"] = nc.dram_tensor("x", [NT, D], F32, kind="ExternalInput").ap()
    dr["ident"] = nc.dram_tensor("ident", [128, 128], BF16, kind="ExternalInput").ap()
    if attn_in:
        dr["oT"] = nc.dram_tensor("oT", [D, NT], BF16, kind="ExternalInput").ap()
        dr["w_out"] = nc.dram_tensor("w_out", [D, D], F32, kind="ExternalInput").ap()
    if mlp:
        dr["g_mlp"] = nc.dram_tensor("g_mlp", [1, D], F32, kind="ExternalInput").ap()
        dr["w1"] = nc.dram_tensor("w1", [D, 4 * D], F32, kind="ExternalInput").ap()
        dr["w2"] = nc.dram_tensor("w2", [4 * D, D], F32, kind="ExternalInput").ap()
    if proj_cols:
        dr["g_mix"] = nc.dram_tensor("g_mix", [1, D], F32, kind="ExternalInput").ap()
        dr["w_in"] = nc.dram_tensor("w_in", [D, proj_cols], F32, kind="ExternalInput").ap()
        dr["projT"] = nc.dram_tensor("projT", [proj_cols, NT], BF16, kind="ExternalOutput").ap()
        if gate_rows:
            dr["gT"] = nc.dram_tensor("gT", [gate_rows[1] - gate_rows[0], NT], F32, kind="ExternalOutput").ap()
    if final:
        dr["g_fin"] = nc.dram_tensor("g_fin", [1, D], F32, kind="ExternalInput").ap()
        dr["out"] = nc.dram_tensor("out", [NT, D], F32, kind="ExternalOutput").ap()
    if x_out:
        dr["x_o"] = nc.dram_tensor("x_o", [NT, D], F32, kind="ExternalOutput").ap()

    es = ExitStack()
    with es:
        s = Sched(nc, es)
        token_phase_body(nc, s, dr, attn_in, mlp, proj_cols, final, x_out, gate_rows)
        s.emit()
    return nc


def token_phase_body(nc, s, dr, attn_in, mlp, proj_cols, final, x_out, gate_rows):
    NTT = NT // 128
    xs = s.sbuf("xs", [128, NTT, D], F32)
    xb = [s.buf(xs[:, t, :], f"x{t}") for t in range(NTT)]
    hnT_t = s.sbuf("hnT", [128, 8, NT], BF16)
    hnT = [s.buf(hnT_t[:, :, t * 128:(t + 1) * 128], f"hnT{t}") for t in range(NTT)]
    ident = s.buf(s.sbuf("ident", [128, 128], BF16), "ident")
    gbuf = s.buf(s.sbuf("gbuf", [128, D], F32), "gbuf")
    ss = s.buf(s.sbuf("ss", [128, NTT], F32), "ss")
    rstd = s.buf(s.sbuf("rstd", [128, NTT], F32), "rstd")
    junk = Ring([s.buf(s.sbuf(f"junk{i}", [128, D], BF16), f"junk{i}") for i in range(2)])
    hnb = Ring([s.buf(s.sbuf(f"hnb{i}", [128, D], BF16), f"hnb{i}") for i in range(2)])
    pT = Ring([s.pbuf(f"pT{i}", [128, 8, 128], BF16) for i in range(2)])
    pm = Ring([s.pbuf(f"pm{i}", [128, 512], F32) for i in range(5)])
    wa_t = [s.sbuf(f"wa{i}", [128, 8, 512], BF16) for i in range(2)]
    wa = Ring([s.buf(t, f"wa{i}") for i, t in enumerate(wa_t)])
    evac = Ring(["act", "dve"])

    s.dma("sp", ident[:], dr["ident"], writes=[ident])
    xv = dr["x"].rearrange("(t p) d -> t p d", p=128)
    for t in range(NTT):
        s.dma("sp", xb[t][:], xv[t], writes=[xb[t]])

    def rmsnorm_to_hnT(g_ap):
        s.dma("sp", gbuf[:], g_ap.to_broadcast([128, D]), writes=[gbuf])
        for t in range(NTT):
            j = junk.next()
            s.op("act", lambda e, t=t, j=j: e.activation(out=j[:], in_=xb[t][:], func=AF.Square,
                                                         accum_out=ss[:, t:t + 1]), [xb[t]], [j, ss])
        s.op("dve", lambda e: e.tensor_scalar(out=rstd[:], in0=ss[:], scalar1=1.0 / D, scalar2=EPS,
                                              op0=ALU.mult, op1=ALU.add), [ss], [rstd])
        s.op("act", lambda e: e.activation(out=rstd[:], in_=rstd[:], func=AF.Sqrt), [rstd], [rstd])
        s.op("dve", lambda e: e.reciprocal(out=rstd[:], in_=rstd[:]), [rstd], [rstd])

    def hn_transposes():
        if STAGE < 2: return
        for t in range(NTT):
            h = hnb.next()
            s.op("dve", lambda e, t=t, h=h: e.scalar_tensor_tensor(out=h[:], in0=xb[t][:], scalar=rstd[:, t:t + 1],
                                                                   in1=gbuf[:], op0=ALU.mult, op1=ALU.mult),
                 [xb[t], rstd, gbuf], [h])
            if STAGE < 3: continue
            p = pT.next()
            for k in range(8):
                s.op("pe", lambda e, k=k, h=h, p=p: e.transpose(out=p[:, k, :], in_=h[:, k * 128:(k + 1) * 128],
                                                                identity=ident[:]), [h, ident], [p])
            s.op("act", lambda e, t=t, p=p: e.copy(out=hnT[t][:], in_=p[:]), [p], [hnT[t]])

    if attn_in:
        oT = hnT
        ov = dr["oT"].rearrange("(k p) t -> p k t", p=128)
        for t in range(NTT):
            s.dma("sp", oT[t][:], ov[:, :, t * 128:(t + 1) * 128], writes=[oT[t]])
        wo_t = s.sbuf("wo", [128, 8, D], BF16)
        wo = s.buf(wo_t, "wo")
        s.dma("pool", wo[:], dr["w_out"].rearrange("(k p) n -> p k n", p=128), writes=[wo])
        for t in range(NTT):
            for half in range(2):
                p = pm.next()
                for k in range(8):
                    s.op("pe", lambda e, t=t, half=half, k=k, p=p: e.matmul(
                        p[:], lhsT=oT[t][:, k, :], rhs=wo[:, k, half * 512:(half + 1) * 512],
                        start=(k == 0), stop=(k == 7)), [oT[t], wo], [p])
                s.op("dve", lambda e, t=t, half=half, p=p: e.tensor_tensor(
                    out=xb[t][:, half * 512:(half + 1) * 512], in0=xb[t][:, half * 512:(half + 1) * 512],
                    in1=p[:], op=ALU.add), [p, xb[t]], [xb[t]])

    if mlp:
        rmsnorm_to_hnT(dr["g_mlp"])
        hn_transposes()
        wb_t = [s.sbuf(f"wb{i}", [128, 4, D], BF16) for i in range(2)]
        wb = Ring([s.buf(t, f"wb{i}") for i, t in enumerate(wb_t)])
        hT_t = s.sbuf("hT", [128, 4, NT], BF16)
        hT = [[s.buf(hT_t[:, f, c * 512:(c + 1) * 512], f"hT{f}_{c}") for c in range(4)] for f in range(4)]
        rl = Ring([s.buf(s.sbuf(f"rl{i}", [128, 512], F32), f"rl{i}") for i in range(3)])
        w1v = dr["w1"].rearrange("(k p) f -> p k f", p=128)
        w2v = dr["w2"].rearrange("(g f p) n -> g p f n", p=128, f=4)
        NG = 8
        for g in range(NG):
            a = wa.next()
            b = wb.next()
            s.dma("pool", a[:], w1v[:, :, g * 512:(g + 1) * 512], writes=[a])
            s.dma("pool", b[:], w2v[g], writes=[b])
            for c in range(4):
                for f in range(4):
                    p = pm.next()
                    for k in range(8):
                        s.op("pe", lambda e, k=k, f=f, c=c, p=p, a=a: e.matmul(
                            p[:], lhsT=a[:, k, f * 128:(f + 1) * 128], rhs=hnT_t[:, k, c * 512:(c + 1) * 512],
                            start=(k == 0), stop=(k == 7)), [a] + hnT[c * 4:(c + 1) * 4], [p])
                    r = rl.next()
                    s.op("act", lambda e, p=p, r=r: e.activation(out=r[:], in_=p[:], func=AF.Relu), [p], [r])
                    s.op("dve", lambda e, r=r, f=f, c=c: e.tensor_tensor(out=hT[f][c][:], in0=r[:], in1=r[:],
                                                                         op=ALU.mult), [r], [hT[f][c]])
            for t in range(NTT):
                c = t // 4
                for half in range(2):
                    p = pm.next()
                    for f in range(4):
                        s.op("pe", lambda e, t=t, f=f, half=half, p=p, b=b: e.matmul(
                            p[:], lhsT=hT_t[:, f, t * 128:(t + 1) * 128], rhs=b[:, f, half * 512:(half + 1) * 512],
                            start=(f == 0), stop=(f == 3)), [hT[f][c], b], [p])
                    s.op("dve", lambda e, t=t, half=half, p=p: e.tensor_tensor(
                        out=xb[t][:, half * 512:(half + 1) * 512], in0=xb[t][:, half * 512:(half + 1) * 512],
                        in1=p[:], op=ALU.add), [p, xb[t]], [xb[t]])

    if x_out:
        xo = s.buf(dr["x_o"], "x_o")
        xov = dr["x_o"].rearrange("(t p) d -> t p d", p=128)
        for t in range(NTT):
            s.dma("sp", xov[t], xb[t][:], reads=[xb[t]], writes=[xo])
        s.finish_outputs([xo])

    if proj_cols:
        rmsnorm_to_hnT(dr["g_mix"])
        hn_transposes()
        pj = s.buf(dr["projT"], "projT")
        outs = [pj]
        stage = Ring([s.buf(s.sbuf(f"stg{i}", [128, NT], BF16), f"stg{i}") for i in range(2)])
        if gate_rows:
            gt = s.buf(dr["gT"], "gT")
            outs.append(gt)
            gst = s.buf(s.sbuf("gst", [128, NT], F32), "gst")
        ncg = (proj_cols + 511) // 512
        if STAGE < 4: ncg = 0
        if STAGE == 4: ncg = 1
        if STAGE == 5: ncg = 5
        if STAGE == 6: ncg = 2
        if STAGE == 7: ncg = 3
        for cg in range(ncg):
            c0 = cg * 512
            cw = min(512, proj_cols - c0)
            a = wa.next()
            s.dma("pool", a[:, :, 0:cw], dr["w_in"].rearrange("(k p) n -> p k n", p=128)[:, :, c0:c0 + cw], writes=[a])
            for ct in range((cw + 127) // 128):
                m0 = ct * 128
                mw = min(128, cw - m0)
                st = stage.next()
                for c in range(4):
                    p = pm.next()
                    for k in range(8):
                        s.op("pe", lambda e, k=k, c=c, p=p, a=a, m0=m0, mw=mw: e.matmul(
                            p[0:mw, :], lhsT=a[:, k, m0:m0 + mw], rhs=hnT_t[:, k, c * 512:(c + 1) * 512],
                            start=(k == 0), stop=(k == 7)), [a] + hnT[c * 4:(c + 1) * 4], [p])
                    ev = evac.next()
                    if gate_rows and c0 + m0 == gate_rows[0]:
                        ev = "act"
                    if ev == "act":
                        s.op("act", lambda e, p=p, st=st, c=c, mw=mw: e.copy(out=st[0:mw, c * 512:(c + 1) * 512],
                                                                            in_=p[0:mw, :]), [p], [st])
                    else:
                        s.op("dve", lambda e, p=p, st=st, c=c, mw=mw: e.tensor_copy(out=st[0:mw, c * 512:(c + 1) * 512],
                                                                                   in_=p[0:mw, :]), [p], [st])
                    if gate_rows and c0 + m0 == gate_rows[0]:
                        ng = gate_rows[1] - gate_rows[0]
                        s.op("act", lambda e, p=p, c=c, ng=ng: e.copy(out=gst[0:ng, c * 512:(c + 1) * 512],
                                                                     in_=p[0:ng, :]), [p], [gst])
                s.dma("sp", dr["projT"][c0 + m0:c0 + m0 + mw, :], st[0:mw, :], reads=[st], writes=[pj])
                if gate_rows and c0 + m0 == gate_rows[0]:
                    ng = gate_rows[1] - gate_rows[0]
                    s.dma("sp", dr["gT"], gst[0:ng, :], reads=[gst], writes=[gt])
        s.finish_outputs(outs)

    if final:
        rmsnorm_to_hnT(dr["g_fin"])
        ob = s.buf(dr["out"], "out")
        ofin = Ring([s.buf(s.sbuf(f"ofin{i}", [128, D], F32), f"ofin{i}") for i in range(2)])
        outv = dr["out"].rearrange("(t p) d -> t p d", p=128)
        for t in range(NTT):
            o = ofin.next()
            s.op("dve", lambda e, t=t, o=o: e.scalar_tensor_tensor(out=o[:], in0=xb[t][:], scalar=rstd[:, t:t + 1],
                                                                   in1=gbuf[:], op0=ALU.mult, op1=ALU.mult),
                 [xb[t], rstd, gbuf], [o])
            s.dma("sp", outv[t], o[:], reads=[o], writes=[ob])
        s.finish_outputs([ob])


S = 8192
NEGM = -30000.0
NCH = 16


class TileStream:
    def __init__(self, s, la=2):
        self.s = s
        self.la = la
        self.pend = []
        self.n = 0

    def _drain(self):
        i = 0
        while i < len(self.pend):
            due, fn, arg = self.pend[i]
            if due <= self.n:
                self.pend.pop(i)
                fn(arg)
            else:
                i += 1

    def push(self, qk, pv):
        p = qk()
        self.pend.append((self.n + self.la + 1, pv, p))
        self.n += 1
        self._drain()

    def push_fin(self, fin_a, fin_b):
        due_a = self.n + self.la

        def run_a(_):
            x = fin_a()
            self.pend.append((due_a + 2, fin_b, x))
        self.pend.append((due_a, run_a, None))

    def flush(self):
        while self.pend:
            self.n += 1
            self._drain()


def t5_bucket_np(rel):
    n = np.maximum(rel, 0).astype(np.int32)
    nf = np.maximum(n, 1).astype(np.float32)
    large = 16 + (np.log(nf / np.float32(16)) / np.float32(np.log(128 / 16)) * np.float32(16)).astype(np.int32)
    large = np.minimum(large, 31)
    return np.where(n < 16, n, large)


def causal_strip(bias_table, h):
    p = np.arange(128)[:, None]
    y = np.arange(1024)[None, :]
    rel = y - p - 384
    v = bias_table[t5_bucket_np(rel), h].astype(np.float32)
    return np.where(rel >= 0, v, np.float32(NEGM)).astype(np.float32)


def moba_host_inputs(projT_b, bias_table, heads):
    qT = np.stack([projT_b[h * 64:(h + 1) * 64] for h in heads])
    ind = (np.arange(S)[None, :] // 256 == np.arange(32)[:, None]).astype(NPBF)
    zpad = np.zeros((32, S), NPBF)
    kA = np.stack([np.concatenate([projT_b[1024 + h * 64:1024 + (h + 1) * 64], ind, zpad], 0) for h in heads])
    ones = np.ones((S, 1), NPBF)
    vA = np.stack([np.concatenate([np.ascontiguousarray(projT_b[2048 + h * 64:2048 + (h + 1) * 64].T), ones], 1)
                   for h in heads])
    strip = np.stack([causal_strip(bias_table, h) for h in heads])
    cb = np.broadcast_to(bias_table[31, heads][None, :], (128, 4)).astype(np.float32).copy()
    c = np.arange(32)[:, None]
    n = np.arange(32)[None, :]
    past01 = (n < c).astype(np.float32)
    own01 = (n == c).astype(np.float32)
    pastneg = np.where(n < c, 0.0, NEGM).astype(np.float32)
    cm = np.stack([pastneg, past01, own01])
    cm = np.broadcast_to(cm[None], (128, 3, 32, 32)).copy()
    return {"qT": qT, "kA": kA, "vA": vA, "strip": strip, "cb": cb, "cm": cm,
            "identb": np.eye(128, dtype=np.float32).astype(NPBF), "identf": np.eye(128, dtype=np.float32)}


def build_moba(nch=NCH, nheads=4):
    nc = bass.Bass("TRN2", target_bir_lowering=False)
    dr = {}
    dr["qT"] = nc.dram_tensor("qT", [4, 64, S], BF16, kind="ExternalInput").ap()
    dr["kA"] = nc.dram_tensor("kA", [4, 128, S], BF16, kind="ExternalInput").ap()
    dr["vA"] = nc.dram_tensor("vA", [4, S, 65], BF16, kind="ExternalInput").ap()
    dr["strip"] = nc.dram_tensor("strip", [4, 128, 1024], F32, kind="ExternalInput").ap()
    dr["cb"] = nc.dram_tensor("cb", [128, 4], F32, kind="ExternalInput").ap()
    dr["cm"] = nc.dram_tensor("cm", [128, 3, 32, 32], F32, kind="ExternalInput").ap()
    dr["identb"] = nc.dram_tensor("identb", [128, 128], BF16, kind="ExternalInput").ap()
    dr["identf"] = nc.dram_tensor("identf", [128, 128], F32, kind="ExternalInput").ap()
    dr["o"] = nc.dram_tensor("o", [S, 256], BF16, kind="ExternalOutput").ap()
    es = ExitStack()
    with es:
        s = Sched(nc, es)
        moba_body(nc, s, dr, nch, nheads)
        s.emit()
    return nc


def moba_body(nc, s, dr, nch, nheads):
    H = nheads
    kA = [s.buf(s.sbuf(f"kA{h}", [128, S], BF16), f"kA{h}") for h in range(H)]
    vA = [s.buf(s.sbuf(f"vA{h}", [128, 64 * 65 + 64], BF16), f"vA{h}") for h in range(H)]
    strip = [s.buf(s.sbuf(f"strip{h}", [128, 1024], F32), f"strip{h}") for h in range(H)]
    cb = s.buf(s.sbuf("cb", [128, 4], F32), "cb")
    cm = s.buf(s.sbuf("cm", [128, 3, 32, 32], F32), "cm")
    identb = s.buf(s.sbuf("identb", [128, 128], BF16), "identb")
    identf = s.buf(s.sbuf("identf", [128, 128], F32), "identf")
    kmean = [s.buf(s.sbuf(f"kmean{h}", [64, 32], F32), f"kmean{h}") for h in range(H)]
    QA = Ring([s.buf(s.sbuf(f"QA{i}", [128, 512], BF16), f"QA{i}") for i in range(4)])
    qf = Ring([s.buf(s.sbuf(f"qf{i}", [64, 512], F32), f"qf{i}") for i in range(2)])
    gm = Ring([s.buf(s.sbuf(f"gm{i}", [128, 4, 32], F32), f"gm{i}") for i in range(2)])
    mx = Ring([s.buf(s.sbuf(f"mx{i}", [128, 4, 8], F32), f"mx{i}") for i in range(2)])
    sm = Ring([s.buf(s.sbuf(f"sm{i}", [128, 4, 32], F32), f"sm{i}") for i in range(2)])
    NM = Ring([s.buf(s.sbuf(f"NM{i}", [128, 4, 128], BF16), f"NM{i}") for i in range(2)])
    P = Ring([s.buf(s.sbuf(f"P{i}", [128, 512], BF16), f"P{i}") for i in range(4)])
    tmp = Ring([s.buf(s.sbuf(f"tmp{i}", [128, 512], F32), f"tmp{i}") for i in range(2)])
    Osb = Ring([s.buf(s.sbuf(f"Osb{i}", [65, 512], F32), f"Osb{i}") for i in range(2)])
    rs = Ring([s.buf(s.sbuf(f"rs{i}", [128, 4, 1], F32), f"rs{i}") for i in range(2)])
    ost = Ring([s.buf(s.sbuf(f"ost{i}", [128, 4, 256], BF16), f"ost{i}") for i in range(2)])
    pS = Ring([s.pbuf(f"pS{i}", [128, 512], F32) for i in range(3)])
    pO = Ring([s.pbuf(f"pO{i}", [128, 512], F32) for i in range(2)])
    pA = s.pbuf("pA", [128, 512], F32)
    pB = s.pbuf("pB", [128, 512], F32)
    ob = s.buf(dr["o"], "o")

    s.dma("sp", identb[:], dr["identb"], writes=[identb])
    s.dma("sp", identf[:], dr["identf"], writes=[identf])
    s.dma("sp", cb[:], dr["cb"], writes=[cb])
    s.dma("sp", cm[:], dr["cm"], writes=[cm])
    for nm in NM.items + QA.items:
        s.op("pool", lambda e, nm=nm: e.memset(nm[:], 0.0), [], [nm])

    ov = dr["o"].rearrange("(c s p) d -> c p s d", p=128, s=4)
    pF = s.pbuf("pF", [128, 512], F32)
    units = [(i, h) for i in range(nch) for h in range(H)]
    state = {}

    qas = {}

    def pre_dma(u):
        i, h = units[u]
        qa = QA.next()
        s.dma("sp", qa[0:64, :], dr["qT"][h][:, i * 512:(i + 1) * 512], writes=[qa])
        qas[u] = qa

    def pre_a(u):
        i, h = units[u]
        qa = qas[u]
        q32 = qf.next()
        s.op("act", lambda e, qa=qa, q32=q32: e.copy(out=q32[:], in_=qa[0:64, :]), [qa], [q32])
        for sub in range(4):
            s.op("pe", lambda e, sub=sub, q32=q32, h=h: e.matmul(
                pA[:, sub * 32:(sub + 1) * 32], lhsT=q32[:, sub * 128:(sub + 1) * 128], rhs=kmean[h][:],
                start=True, stop=True), [q32, kmean[h]], [pA])
        g = gm.next()
        m8 = mx.next()
        sel = sm.next()
        nm = NM.next()
        for sub in range(4):
            c = (i * 4 + sub) // 2
            s.op("dve", lambda e, sub=sub, c=c, g=g: e.tensor_tensor(
                out=g[:, sub, :], in0=pA[:, sub * 32:(sub + 1) * 32], in1=cm[:, 0, c, :], op=ALU.add),
                [pA, cm], [g])
        for sub in range(4):
            s.op("dve", lambda e, sub=sub, g=g, m8=m8: e.max(out=m8[:, sub, :], in_=g[:, sub, :]), [g], [m8])
        for sub in range(4):
            c = (i * 4 + sub) // 2
            s.op("dve", lambda e, sub=sub, g=g, m8=m8, sel=sel: e.tensor_scalar(
                out=sel[:, sub, :], in0=g[:, sub, :], scalar1=m8[:, sub, 2:3], scalar2=None, op0=ALU.is_ge),
                [g, m8], [sel])
            s.op("dve", lambda e, sub=sub, c=c, sel=sel: e.tensor_tensor(
                out=sel[:, sub, :], in0=sel[:, sub, :], in1=cm[:, 1, c, :], op=ALU.mult), [sel, cm], [sel])
            s.op("dve", lambda e, sub=sub, c=c, sel=sel: e.tensor_tensor(
                out=sel[:, sub, :], in0=sel[:, sub, :], in1=cm[:, 2, c, :], op=ALU.add), [sel, cm], [sel])
        s.op("dve", lambda e, sel=sel, nm=nm: e.tensor_scalar(
            out=nm[:, :, 64:96], in0=sel[:], scalar1=1.0, scalar2=-NEGM, op0=ALU.subtract, op1=ALU.mult),
            [sel], [nm])
        state[u] = (qa, nm)

    def pre_b(u):
        qa, nm = state[u]
        for sub in range(4):
            s.op("pe", lambda e, sub=sub, nm=nm: e.matmul(
                pB[:, sub * 128:(sub + 1) * 128], lhsT=nm[:, sub, :], rhs=identb[:], start=True, stop=True),
                [nm, identb], [pB])
        s.op("act", lambda e, qa=qa: e.copy(out=qa[64:96, :], in_=pB[64:96, :]), [pB], [qa])

    stream = TileStream(s, la=2)
    osb_out = None
    pre_dma(0)
    if len(units) > 1:
        pre_dma(1)
    for h in range(H):
        s.dma("sp", kA[h][:], dr["kA"][h], writes=[kA[h]])
        s.op("pool", lambda e, h=h: e.memset(vA[h][:, 4160:4224], 0.0), [], [vA[h]])
        s.dma("sp", vA[h][:, 0:4160].rearrange("p (t c) -> p t c", c=65), dr["vA"][h].rearrange("(t p) c -> p t c", p=128), writes=[vA[h]])
        s.dma("sp", strip[h][:], dr["strip"][h], writes=[strip[h]])
    for h in range(H):
        s.op("dve", lambda e, h=h: e.tensor_reduce(out=kmean[h][:], in_=kA[h][0:64, :].rearrange("p (n k) -> p n k", k=256),
                                                   axis=AX.X, op=ALU.add), [kA[h]], [kmean[h]])
        s.op("dve", lambda e, h=h: e.tensor_scalar(out=kmean[h][:], in0=kmean[h][:], scalar1=1.0 / 256, scalar2=None,
                                                   op0=ALU.mult), [kmean[h]], [kmean[h]])
    pre_a(0)
    pre_b(0)
    for u, (i, h) in enumerate(units):
        if u + 2 < len(units):
            pre_dma(u + 2)
        if h == 0:
            osb_out = ost.next()
        qa, _ = state[u]
        po = pO.next()
        nkt = 4 * i + 4
        if u + 1 < len(units):
            pre_a(u + 1)
        for kt in range(nkt):
            j = kt - 4 * i

            def qk(kt=kt, j=j, qa=qa, h=h):
                ps = pS.next()
                s.op("pe", lambda e, kt=kt, ps=ps, qa=qa, h=h: e.matmul(
                    ps[:], lhsT=kA[h][:, kt * 128:(kt + 1) * 128], rhs=qa[:], start=True, stop=True),
                    [kA[h], qa], [ps])
                p = P.next()
                if j >= -1:
                    y0 = 384 - 128 * j
                    t = tmp.next()
                    s.op("dve", lambda e, ps=ps, t=t, h=h, y0=y0: e.scalar_tensor_tensor(
                        out=t[:], in0=ps[:], scalar=0.125, in1=strip[h][:, y0:y0 + 512], op0=ALU.mult, op1=ALU.add),
                        [ps, strip[h]], [t])
                    s.op("act", lambda e, t=t, p=p: e.activation(out=p[:], in_=t[:], func=AF.Exp), [t], [p])
                else:
                    s.op("act", lambda e, ps=ps, p=p, h=h: e.activation(out=p[:], in_=ps[:], func=AF.Exp,
                                                                        bias=cb[:, h:h + 1], scale=0.125),
                         [ps, cb], [p])
                return p

            def pv(p, kt=kt, po=po, h=h, nkt=nkt):
                s.op("pe", lambda e, kt=kt, p=p, po=po, h=h, nkt=nkt: e.matmul(
                    po[:], lhsT=vA[h][:, kt * 65:kt * 65 + 128], rhs=p[:], start=(kt == 0), stop=(kt == nkt - 1)),
                    [vA[h], p], [po])
            stream.push(qk, pv)
            if kt == min(13, nkt - 1) and u + 1 < len(units):
                pre_b(u + 1)

        def fin_a(po=po):
            osb = Osb.next()
            s.op("dve", lambda e, osb=osb, po=po: e.tensor_copy(out=osb[:], in_=po[0:65, :]), [po], [osb])
            return osb

        def fin_b(osb, h=h, osb_out=osb_out, i=i):
            for sub in range(4):
                s.op("pe", lambda e, sub=sub, osb=osb: e.transpose(
                    out=pF[:, sub * 65:(sub + 1) * 65], in_=osb[:, sub * 128:(sub + 1) * 128],
                    identity=identf[0:65, 0:65]), [osb, identf], [pF])
            r = rs.next()
            pav = pF[:, 0:260].rearrange("p (s c) -> p s c", c=65)
            s.op("dve", lambda e, r=r, pav=pav: e.reciprocal(out=r[:], in_=pav[:, :, 64:65]), [pF], [r])
            s.op("dve", lambda e, r=r, pav=pav, h=h, osb_out=osb_out: e.tensor_tensor(
                out=osb_out[:, :, h * 64:(h + 1) * 64], in0=pav[:, :, 0:64], in1=r[:].to_broadcast([128, 4, 64]),
                op=ALU.mult), [pF, r], [osb_out])
            if h == H - 1:
                s.dma("sp", ov[i], osb_out[:], reads=[osb_out], writes=[ob])
        stream.push_fin(fin_a, fin_b)
    stream.flush()
    s.finish_outputs([ob])


LAM_INIT0 = 0.8 - 0.6 * 1.0
SUB_EPS = 1e-6
GELU_C = 0.7978845608028654


def window_strip(bias_table, h):
    p = np.arange(128)[:, None]
    y = np.arange(1408)[None, :]
    rel = y - p - 384
    v = bias_table[t5_bucket_np(rel), h].astype(np.float32)
    return np.where((rel >= 0) & (rel < 512), v, np.float32(NEGM)).astype(np.float32)


def cmp_bias(bias_table, h):
    out = []
    p = np.arange(128)[:, None]
    x = np.arange(512)[None, :]
    for o in range(0, 2560, 512):
        rel = o + x - 16 * p - 31
        v = bias_table[t5_bucket_np(rel), h].astype(np.float32)
        out.append(np.where(rel >= 0, v, np.float32(NEGM)))
    return np.stack(out).astype(np.float32)


def l2_host_inputs(projT_b, gT_b, inputs, g, half, hd):
    bt = inputs["bias_table"]
    own = [2 * half, 2 * half + 1]
    oth = [r for r in range(4) if r not in own]
    order = own + oth
    gh = [g * 4 + r for r in order]
    qT4 = np.stack([projT_b[h * 64:(h + 1) * 64] for h in gh])
    kcvT = np.concatenate([projT_b[512 + g * 64:512 + (g + 1) * 64], projT_b[640 + g * 64:640 + (g + 1) * 64]], 0)
    ind = ((np.arange(S)[None, :] // 64) % 64 == np.arange(64)[:, None]).astype(NPBF)
    ksA = np.concatenate([projT_b[768 + g * 64:768 + (g + 1) * 64], ind], 0)
    ones = np.ones((S, 1), NPBF)
    vsA = np.concatenate([np.ascontiguousarray(projT_b[896 + g * 64:896 + (g + 1) * 64].T), ones], 1)
    kwT = np.concatenate([projT_b[1024 + g * 64:1024 + (g + 1) * 64], np.zeros((64, S), NPBF)], 0)
    vwA = np.concatenate([np.ascontiguousarray(projT_b[1152 + g * 64:1152 + (g + 1) * 64].T), ones], 1)
    gsel = np.stack([np.stack([gT_b[br * 8 + gh[k]] for br in range(3)], -1) for k in range(2)], 1).astype(np.float32)
    dqT = np.ascontiguousarray(projT_b[1304 + hd * 128:1304 + (hd + 1) * 128])
    dkT = np.ascontiguousarray(projT_b[1816 + hd * 128:1816 + (hd + 1) * 128])
    dvA = np.ascontiguousarray(projT_b[2328 + hd * 128:2328 + (hd + 1) * 128].T)
    cstrip = np.stack([causal_strip(bt, gh[0]), causal_strip(bt, gh[1]), causal_strip(bt, 8 + hd)])
    wstrip = np.stack([window_strip(bt, gh[0]), window_strip(bt, gh[1])])
    cbias = np.stack([cmp_bias(bt, h) for h in gh])
    cb = np.broadcast_to(bt[31, gh + [8 + hd]][None, :], (128, 5)).astype(np.float32).copy()
    w1kv = np.stack([inputs["ev_cmp_w1_k"][0], inputs["ev_cmp_w1_v"][0]])
    w2kv = np.stack([inputs["ev_cmp_w2_k"][0], inputs["ev_cmp_w2_v"][0]])
    posT = np.concatenate([inputs["ev_cmp_pos_k"][0].T, inputs["ev_cmp_pos_v"][0].T], 0)
    c = np.arange(512)
    ovl = np.zeros((512, 129), np.float32)
    for cc in range(511):
        for tk in range(16 * cc, 16 * cc + 32):
            ovl[cc, tk // 64] += 1.0 / 32
        ovl[cc, 128] = 1.0
    cur = np.arange(128)[:, None]
    n = np.arange(128)[None, :]
    elig = n <= cur
    forced = (n == 0) | (n == cur) | (n == cur - 1)
    A = elig.astype(np.float32)
    B = np.where(elig, np.where(forced, 1000.0, 0.0), -1.0).astype(np.float32)
    ABtab = np.stack([A, B], 1)
    lam = np.stack([inputs["ev_lam_q1"][0], inputs["ev_lam_k1"][0], inputs["ev_lam_q2"][0], inputs["ev_lam_k2"][0]])
    return {"qT4": qT4, "kcvT": kcvT, "ksA": ksA, "vsA": vsA, "kwT": kwT, "vwA": vwA, "gsel": gsel,
            "dqT": dqT, "dkT": dkT, "dvA": dvA, "cstrip": cstrip, "wstrip": wstrip, "cbias": cbias, "cb": cb,
            "w1kv": w1kv, "w2kv": w2kv, "posT": posT.astype(np.float32), "ovl": ovl, "ABtab": ABtab,
            "lam": lam.reshape(1, 256).astype(np.float32), "subln": inputs["ev_subln"][0].reshape(1, 128),
            "identb": np.eye(128, dtype=np.float32).astype(NPBF), "identf": np.eye(128, dtype=np.float32),
            "ones2": np.stack([np.concatenate([np.ones((128, 64)), np.zeros((128, 64))], 1),
                               np.concatenate([np.zeros((128, 64)), np.ones((128, 64))], 1)]).astype(NPBF)}


L2_SPECS = {
    "qT4": ([4, 64, S], BF16), "kcvT": ([128, S], BF16), "ksA": ([128, S], BF16), "vsA": ([S, 65], BF16),
    "kwT": ([128, S], BF16), "vwA": ([S, 65], BF16), "gsel": ([S, 2, 3], F32), "dqT": ([128, S], BF16),
    "dkT": ([128, S], BF16), "dvA": ([S, 128], BF16), "cstrip": ([3, 128, 1024], F32),
    "wstrip": ([2, 128, 1408], F32), "cbias": ([4, 5, 128, 512], F32), "cb": ([128, 5], F32),
    "w1kv": ([2, 2048, 128], F32), "w2kv": ([2, 128, 64], F32), "posT": ([128, 32], F32),
    "ovl": ([512, 129], F32), "ABtab": ([128, 2, 128], F32), "lam": ([1, 256], F32), "subln": ([1, 128], F32),
    "identb": ([128, 128], BF16), "identf": ([128, 128], F32), "ones2": ([2, 128, 128], BF16),
}


def build_l2(nch=16, do_nsa=True, do_diff=True):
    nc = bass.Bass("TRN2", target_bir_lowering=False)
    dr = {k: nc.dram_tensor(k, sh, dt, kind="ExternalInput").ap() for k, (sh, dt) in L2_SPECS.items()}
    dr["o"] = nc.dram_tensor("o", [S, 256], BF16, kind="ExternalOutput").ap()
    es = ExitStack()
    with es:
        s = Sched(nc, es)
        l2_body(nc, s, dr, nch, do_nsa, do_diff)
        s.emit()
    return nc


def l2_body(nc, s, dr, nch, do_nsa, do_diff):
    def sb(name, shape, dt):
        return s.buf(s.sbuf(name, shape, dt), name)

    def load(name, shape, dt, src=None, eng="sp"):
        b = sb(name, shape, dt)
        s.dma(eng, b[:], dr[name] if src is None else src, writes=[b])
        return b

    identb = load("identb", [128, 128], BF16)
    identf = load("identf", [128, 128], F32)
    cb = load("cb", [128, 5], F32)
    cstrip = [load(f"cstrip{k}", [128, 1024], F32, dr["cstrip"][k]) for k in range(3)]
    pS = Ring([s.pbuf(f"pS{i}", [128, 512], F32) for i in range(3)])
    pOa = s.pbuf("pOa", [128, 512], F32)
    pOb = s.pbuf("pOb", [128, 512], F32)
    pU0 = s.pbuf("pU0", [128, 512], F32)
    pU1 = s.pbuf("pU1", [128, 512], F32)
    pL = s.pbuf("pL", [128, 512], F32)
    P = Ring([sb(f"P{i}", [128, 512], BF16) for i in range(4)])
    tmp = Ring([sb(f"tmp{i}", [128, 512], F32) for i in range(3)])
    ost = Ring([sb(f"ost{i}", [128, 4, 256], BF16) for i in range(2)])
    ob = s.buf(dr["o"], "o")
    ov = dr["o"].rearrange("(c s p) d -> c p s d", p=128, s=4)

    def exp_tile(ps, j, strip_ap_fn, cb_ap):
        p = P.next()
        if strip_ap_fn is not None:
            t = tmp.next()
            sbuf_, ap = strip_ap_fn
            s.op("dve", lambda e, ps=ps, t=t, ap=ap: e.scalar_tensor_tensor(
                out=t[:], in0=ps[:], scalar=0.125, in1=ap, op0=ALU.mult, op1=ALU.add), [ps, sbuf_], [t])
            s.op("act", lambda e, t=t, p=p: e.activation(out=p[:], in_=t[:], func=AF.Exp), [t], [p])
        else:
            s.op("act", lambda e, ps=ps, p=p: e.activation(out=p[:], in_=ps[:], func=AF.Exp, bias=cb_ap, scale=0.125),
                 [ps, cb], [p])
        return p

    ksd = sb("ksd", [128, S], BF16)
    kcd = sb("kcd", [128, S], BF16)
    scr = sb("scr", [128, 8192], BF16)
    ob3 = Ring([sb(f"ob3_{i}", [128, 512], F32) for i in range(3)])
    if do_nsa:
        s.dma("sp", kcd[:], dr["kcvT"], writes=[kcd])
        ksA = ksd
        s.dma("sp", ksd[:], dr["ksA"], writes=[ksd])
        vsA = sb("vsA", [128, 64 * 65 + 64], BF16)
        vwA = sb("vwA", [128, 64 * 65 + 64], BF16)
        for vv, nm_ in ((vsA, "vsA"), (vwA, "vwA")):
            s.op("pool", lambda e, vv=vv: e.memset(vv[:, 4160:4224], 0.0), [], [vv])
            s.dma("sp", vv[:, 0:4160].rearrange("p (t c) -> p t c", c=65), dr[nm_].rearrange("(t p) c -> p t c", p=128), writes=[vv])
        kwT = load("kwT", [128, S], BF16)
        wstrip = [load(f"wstrip{k}", [128, 1408], F32, dr["wstrip"][k]) for k in range(2)]
        gates = load("gates", [128, 64, 6], F32, dr["gsel"].rearrange("(t p) h b -> p t (h b)", p=128))
        s.op("act", lambda e: e.activation(out=gates[:], in_=gates[:], func=AF.Sigmoid), [gates], [gates])
        kcvT = kcd
        w1kv = s.buf(scr.t[:, 0:4096].rearrange("p (j m) -> p j m", m=128), "w1kv_view")
        for m in range(2):
            s.dma("pool", w1kv[64 * m:64 * m + 64, :, :], dr["w1kv"][m].rearrange("(j d) m -> d j m", d=64), writes=[scr])
        w2kv = sb("w2kv", [128, 2, 64], BF16)
        s.dma("pool", w2kv[:], dr["w2kv"].rearrange("t k d -> k t d"), writes=[w2kv])
        posT = sb("posT", [128, 32], BF16)
        s.dma("pool", posT[:], dr["posT"], writes=[posT])
        R = sb("R", [128, 4, 193], BF16)
        s.dma("pool", R[:, :, 0:129], dr["ovl"].rearrange("(j p) n -> p j n", p=128), writes=[R])
        cbias = [load(f"cbias{k}", [128, 5, 512], BF16, dr["cbias"][k].rearrange("o p x -> p o x"), eng="pool") for k in range(4)]
        KcT = sb("KcT", [64, 512], BF16)
        gl = [P.items[0], P.items[1]]
        xs = tmp.items[0]
        x2 = tmp.items[1]
        for m in range(2):
            ps = pS.next()
            lo = 64 * m
            for j in range(32):
                s.op("pe", lambda e, j=j, lo=lo, ps=ps: e.matmul(
                    ps[:, 0:511], lhsT=w1kv[lo:lo + 64, j, :], rhs=kcvT[lo:lo + 64, j:j + 8161:16],
                    start=(j == 0), stop=False), [scr, kcvT], [ps])
            for j in range(32):
                s.op("pe", lambda e, j=j, lo=lo, ps=ps: e.matmul(
                    ps[:, 0:511], lhsT=w1kv[lo:lo + 64, j, :], rhs=posT[lo:lo + 64, j:j + 1].to_broadcast([64, 511]),
                    start=False, stop=(j == 31)), [scr, posT], [ps])
            s.op("pool", lambda e, m=m: e.memset(gl[m][:], 0.0), [], [gl[m]])
            s.op("act", lambda e, ps=ps: e.copy(out=xs[:, 0:511], in_=ps[:, 0:511]), [ps], [xs])
            s.op("dve", lambda e: e.tensor_tensor(out=x2[:, 0:511], in0=xs[:, 0:511], in1=xs[:, 0:511], op=ALU.mult), [xs], [x2])
            s.op("dve", lambda e: e.tensor_scalar(out=x2[:, 0:511], in0=x2[:, 0:511], scalar1=0.044715, scalar2=1.0,
                                                  op0=ALU.mult, op1=ALU.add), [x2], [x2])
            s.op("dve", lambda e: e.tensor_tensor(out=x2[:, 0:511], in0=x2[:, 0:511], in1=xs[:, 0:511], op=ALU.mult), [x2, xs], [x2])
            s.op("act", lambda e: e.activation(out=x2[:, 0:511], in_=x2[:, 0:511], func=AF.Sigmoid, scale=2.0 * GELU_C), [x2], [x2])
            s.op("dve", lambda e, m=m: e.tensor_tensor(out=gl[m][:, 0:511], in0=x2[:, 0:511], in1=xs[:, 0:511], op=ALU.mult),
                 [x2, xs], [gl[m]])
        ps = pS.next()
        s.op("pe", lambda e, ps=ps: e.matmul(ps[0:64, :], lhsT=w2kv[:, 0, :], rhs=gl[0][:], start=True, stop=True),
             [w2kv, gl[0]], [ps])
        s.op("act", lambda e, ps=ps: e.copy(out=KcT[:], in_=ps[0:64, :]), [ps], [KcT])
        ps = pS.next()
        for jt in range(4):
            s.op("pe", lambda e, jt=jt, ps=ps: e.matmul(ps[:, jt * 64:(jt + 1) * 64], lhsT=gl[1][:, jt * 128:(jt + 1) * 128],
                                                        rhs=w2kv[:, 1, :], start=True, stop=True), [gl[1], w2kv], [ps])
        s.op("act", lambda e, ps=ps: e.copy(out=R[:, :, 129:193], in_=ps[:, 0:256].rearrange("p (j d) -> p j d", d=64)),
             [ps], [R])
        QA = [[sb(f"QA{k}_{hf}_{i}", [128, 512], BF16) for i in range(2)] for k in range(2) for hf in range(2)]
        Qc = [Ring([sb(f"Qc{k}_{i}", [64, 512], BF16) for i in range(2)]) for k in range(2)]
        ABt = Ring([sb(f"ABt{i}", [128, 2, 128], F32) for i in range(3)])
        rsu = Ring([sb(f"rsu{i}", [128, 4], F32) for i in range(2)])
        imp = Ring([sb(f"imp{i}", [128, 128], F32) for i in range(2)])
        sc2 = Ring([sb(f"sc2{i}", [128, 128], F32) for i in range(2)])
        mx = Ring([sb(f"mx{i}", [128, 16], F32) for i in range(2)])
        NM = Ring([sb(f"NM{i}", [128, 192], BF16) for i in range(2)])
        for nm in NM.items + [q_ for grp in QA for q_ in grp]:
            s.op("pool", lambda e, nm=nm: e.memset(nm[:], 0.0), [], [nm])
        ocmp = Ring([sb(f"ocmp{i}", [128, 4, 2, 64], F32) for i in range(2)])
        Osb = ob3
        rs2 = Ring([sb(f"rs2{i}", [128, 4, 1], F32) for i in range(3)])
        acc = Ring([sb(f"acc{i}", [128, 4, 64], F32) for i in range(2)])
        t2b = Ring([sb(f"t2b{i}", [128, 4, 64], F32) for i in range(2)])
        wg = Ring([sb(f"wg{i}", [128, 4, 1], F32) for i in range(4)])
        Ecm = [[scr.t[:, (r * 4 + jt) * 512:(r * 4 + jt + 1) * 512] for jt in range(4)] for r in range(4)]

    if do_diff:
        ones2 = load("ones2", [128, 2, 128], BF16, dr["ones2"].rearrange("m p c -> p m c"))
        lamv = load("lamv", [128, 256], F32, dr["lam"].to_broadcast([128, 256]))
        gsub = load("gsub", [128, 128], F32, dr["subln"].to_broadcast([128, 128]))
        s.op("dve", lambda e: e.tensor_scalar(out=gsub[:], in0=gsub[:], scalar1=1.0 - LAM_INIT0, scalar2=None, op0=ALU.mult),
             [gsub], [gsub])
        lt = sb("lt", [128, 128], F32)
        l2s = sb("l2s", [128, 2], F32)
        nlam = sb("nlam", [128, 1], F32)
        lv = lamv[:].rearrange("p (a d) -> p a d", d=64)
        s.op("dve", lambda e: e.tensor_tensor(out=lt[:].rearrange("p (a d) -> p a d", d=64), in0=lv[:, 0:4:2, :],
                                              in1=lv[:, 1:4:2, :], op=ALU.mult), [lamv], [lt])
        s.op("dve", lambda e: e.tensor_reduce(out=l2s[:], in_=lt[:].rearrange("p (a d) -> p a d", d=64), axis=AX.X,
                                              op=ALU.add), [lt], [l2s])
        s.op("act", lambda e: e.activation(out=l2s[:], in_=l2s[:], func=AF.Exp), [l2s], [l2s])
        s.op("dve", lambda e: e.tensor_tensor(out=nlam[:], in0=l2s[:, 1:2], in1=l2s[:, 0:1], op=ALU.subtract), [l2s], [nlam])
        s.op("dve", lambda e: e.tensor_scalar(out=nlam[:], in0=nlam[:], scalar1=-LAM_INIT0, scalar2=None, op0=ALU.add),
             [nlam], [nlam])
        dq = Ring([[sb(f"dq{i}_{m}", [128, 512], BF16) for m in range(2)] for i in range(2)])
        for pair_ in dq.items:
            for b_ in pair_:
                s.op("pool", lambda e, b_=b_: e.memset(b_[:], 0.0), [], [b_])
        Od = ob3
        rd = Ring([sb(f"rd{i}", [128, 4, 2], F32) for i in range(2)])
        o0 = Ring([sb(f"o0{i}", [128, 4, 128], F32) for i in range(1)])
        av = Ring([sb(f"av{i}", [128, 4, 128], F32) for i in range(1)])
        sq = sb("sq", [128, 4, 128], F32)
        ssd = Ring([sb(f"ssd{i}", [128, 4], F32) for i in range(2)])

    ovn = dr["o"][:, 0:128].rearrange("(c s p) d -> c p s d", p=128, s=4)
    ovd = dr["o"][:, 128:256].rearrange("(c s p) d -> c p s d", p=128, s=4)
    pF = pU1
    st = {}

    def pre_gen(i):
        qa = [[QA[k * 2 + hf][i % 2] for hf in range(2)] for k in range(2)]
        nhf = 2 if i >= 8 else 1
        qsrc = []
        for k in range(2):
            for hf in range(nhf):
                s.dma("sp", qa[k][hf][0:64, :], dr["qT4"][k][:, i * 512:(i + 1) * 512], writes=[qa[k][hf]])
            qsrc.append((qa[k][0], qa[k][0][0:64, :]))
        for k in range(2):
            q = Qc[k].next()
            s.dma("sp", q[:], dr["qT4"][2 + k][:, i * 512:(i + 1) * 512], writes=[q])
            qsrc.append((q, q[:]))
        oc = ocmp.next()
        st[i] = (qa, oc)
        yield
        ncj = min(4, (512 * i + 511 - 31) // 16 // 128 + 1)
        for r in range(4):
            qb, qap = qsrc[r]
            for jt in range(ncj):
                ps = pS.next()
                s.op("pe", lambda e, ps=ps, jt=jt, qap=qap: e.matmul(
                    ps[:], lhsT=KcT[:, jt * 128:(jt + 1) * 128], rhs=qap, start=True, stop=True), [KcT, qb], [ps])
                o_ = 512 * i - 2048 * jt
                ec = Ecm[r][jt]
                if o_ <= 2048:
                    t = tmp.next()
                    s.op("dve", lambda e, ps=ps, t=t, r=r, o_=o_: e.scalar_tensor_tensor(
                        out=t[:], in0=ps[:], scalar=0.125, in1=cbias[r][:, o_ // 512, :], op0=ALU.mult, op1=ALU.add),
                        [ps, cbias[r]], [t])
                    s.op("act", lambda e, t=t, ec=ec: e.activation(out=ec, in_=t[:], func=AF.Exp), [t], [scr])
                else:
                    s.op("act", lambda e, ps=ps, ec=ec, r=r: e.activation(out=ec, in_=ps[:], func=AF.Exp,
                                                                        bias=cb[:, r:r + 1], scale=0.125), [ps, cb], [scr])
                yield
        for sub in range(4):
            tt = 4 * i + sub
            ab = ABt.next()
            for hh in range(2):
                s.dma("sp", ab[64 * hh:64 * hh + 64, :, :], dr["ABtab"][2 * tt + hh:2 * tt + hh + 1].to_broadcast([64, 2, 128]),
                      writes=[ab])
            ru = rsu.next()
            im = imp.next()
            for pair in range(2):
                for r in (2 * pair, 2 * pair + 1):
                    c0 = (r % 2) * 193
                    for jt in range(ncj):
                        s.op("pe", lambda e, c0=c0, r=r, jt=jt, sub=sub, ncj=ncj: e.matmul(
                            pU0[:, c0:c0 + 193], lhsT=Ecm[r][jt][:, sub * 128:(sub + 1) * 128], rhs=R[:, jt, :],
                            start=(jt == 0), stop=(jt == ncj - 1)), [scr, R], [pU0])
                yield
                s.op("dve", lambda e, pair=pair, ru=ru: e.tensor_scalar(
                    out=ru[:, 2 * pair:2 * pair + 2], in0=pU0[:, 0:386].rearrange("p (h c) -> p h c", c=193)[:, :, 128],
                    scalar1=1e-30, scalar2=None, op0=ALU.max), [pU0], [ru])
                s.op("dve", lambda e, pair=pair, ru=ru: e.reciprocal(out=ru[:, 2 * pair:2 * pair + 2],
                                                                     in_=ru[:, 2 * pair:2 * pair + 2]), [ru], [ru])
                for r in (2 * pair, 2 * pair + 1):
                    c0 = (r % 2) * 193
                    if r == 0:
                        s.op("dve", lambda e, im=im, ru=ru: e.tensor_scalar(out=im[:], in0=pU0[:, 0:128], scalar1=ru[:, 0:1],
                                                                             scalar2=None, op0=ALU.mult), [pU0, ru], [im])
                    else:
                        s.op("dve", lambda e, im=im, ru=ru, c0=c0, r=r: e.scalar_tensor_tensor(
                            out=im[:], in0=pU0[:, c0:c0 + 128], scalar=ru[:, r:r + 1], in1=im[:], op0=ALU.mult, op1=ALU.add),
                            [pU0, ru, im], [im])
                if pair == 0:
                    for k in range(2):
                        c0 = k * 193
                        s.op("dve", lambda e, k=k, c0=c0, oc=oc, ru=ru, sub=sub: e.tensor_scalar(
                            out=oc[:, sub, k, :], in0=pU0[:, c0 + 129:c0 + 193], scalar1=ru[:, k:k + 1], scalar2=None,
                            op0=ALU.mult), [pU0, ru], [oc])
                yield
            s.op("dve", lambda e, im=im, ab=ab: e.tensor_tensor(out=im[:], in0=im[:], in1=ab[:, 0, :], op=ALU.mult), [im, ab], [im])
            s.op("dve", lambda e, im=im, ab=ab: e.tensor_tensor(out=im[:], in0=im[:], in1=ab[:, 1, :], op=ALU.add), [im, ab], [im])
            m8 = mx.next()
            s2 = sc2.next()
            s.op("dve", lambda e, im=im, m8=m8: e.max(out=m8[:, 0:8], in_=im[:]), [im], [m8])
            s.op("dve", lambda e, im=im, m8=m8, s2=s2: e.match_replace(out=s2[:], in_to_replace=m8[:, 0:8], in_values=im[:],
                                                                       imm_value=-2.0), [im, m8], [s2])
            yield
            s.op("dve", lambda e, m8=m8, s2=s2: e.max(out=m8[:, 8:16], in_=s2[:]), [s2, m8], [m8])
            s.op("dve", lambda e, im=im, m8=m8, s2=s2: e.tensor_scalar(out=s2[:], in0=im[:], scalar1=m8[:, 15:16], scalar2=None,
                                                                       op0=ALU.is_ge), [im, m8], [s2])
            nm = NM.next()
            s.op("dve", lambda e, s2=s2, nm=nm: e.tensor_scalar(out=nm[:, 64:192], in0=s2[:], scalar1=1.0, scalar2=-NEGM,
                                                                op0=ALU.subtract, op1=ALU.mult), [s2], [nm])
            yield
            for hf in range(nhf):
                s.op("pe", lambda e, nm=nm, hf=hf: e.matmul(pL[:, hf * 128:(hf + 1) * 128], lhsT=nm[:, hf * 64:hf * 64 + 128],
                                                             rhs=identb[:], start=True, stop=True), [nm, identb], [pL])
            for hf in range(nhf):
                for k in range(2):
                    q = qa[k][hf]
                    s.op("act", lambda e, q=q, hf=hf, sub=sub: e.copy(out=q[64:128, sub * 128:(sub + 1) * 128],
                                                                      in_=pL[64:128, hf * 128:(hf + 1) * 128]), [pL], [q])
            yield

    def advance(gen, n):
        if gen is None:
            return None
        try:
            for _ in range(n):
                next(gen)
        except StopIteration:
            return None
        return gen

    stream = TileStream(s, la=2)
    if do_nsa:
        g0 = pre_gen(0)
        while g0 is not None:
            g0 = advance(g0, 1000)
    for i in range(nch if do_nsa else 0):
        oo = ost.next()
        nkt = 4 * i + 4
        qa, oc = st[i]
        gen = pre_gen(i + 1) if i + 1 < nch else None
        kw0 = max(0, 4 * i - 4)
        ntile = 2 * (nkt + (nkt - kw0))
        per = -(-48 // ntile)
        for k in range(2):
            for kt in range(nkt):
                j = kt - 4 * i
                hf = 0 if kt < 32 else 1
                q = qa[k][hf]

                def qk(kt=kt, j=j, q=q, k=k):
                    ps = pS.next()
                    s.op("pe", lambda e, ps=ps, kt=kt, q=q: e.matmul(ps[:], lhsT=ksA[:, kt * 128:(kt + 1) * 128], rhs=q[:],
                                                                      start=True, stop=True), [ksA, q], [ps])
                    if j >= -1:
                        y0 = 384 - 128 * j
                        return exp_tile(ps, j, (cstrip[k], cstrip[k][:, y0:y0 + 512]), None)
                    return exp_tile(ps, j, None, cb[:, k:k + 1])

                def pv(p, kt=kt, nkt=nkt):
                    s.op("pe", lambda e, p=p, kt=kt, nkt=nkt: e.matmul(pOa[:], lhsT=vsA[:, kt * 65:kt * 65 + 128], rhs=p[:],
                                                                        start=(kt == 0), stop=(kt == nkt - 1)), [vsA, p], [pOa])
                stream.push(qk, pv)
                gen = advance(gen, per)
            q = qa[k][0]
            for kt in range(kw0, nkt):
                j = kt - 4 * i

                def qk(kt=kt, j=j, q=q, k=k):
                    ps = pS.next()
                    s.op("pe", lambda e, ps=ps, kt=kt, q=q: e.matmul(ps[:], lhsT=kwT[:, kt * 128:(kt + 1) * 128], rhs=q[:],
                                                                      start=True, stop=True), [kwT, q], [ps])
                    y0 = 384 - 128 * j
                    return exp_tile(ps, j, (wstrip[k], wstrip[k][:, y0:y0 + 512]), None)

                def pv(p, kt=kt, kw0=kw0, nkt=nkt):
                    s.op("pe", lambda e, p=p, kt=kt, kw0=kw0, nkt=nkt: e.matmul(pOb[:], lhsT=vwA[:, kt * 65:kt * 65 + 128], rhs=p[:],
                                                                                 start=(kt == kw0), stop=(kt == nkt - 1)), [vwA, p], [pOb])
                stream.push(qk, pv)
                gen = advance(gen, per)

            def fin_a():
                o1 = Osb.next()
                o2 = Osb.next()
                s.op("dve", lambda e, o1=o1: e.tensor_copy(out=o1[0:65, :], in_=pOa[0:65, :]), [pOa], [o1])
                s.op("act", lambda e, o2=o2: e.copy(out=o2[0:65, :], in_=pOb[0:65, :]), [pOb], [o2])
                return (o1, o2)

            def fin_b(os_, k=k, i=i, oc=oc, oo=oo):
                a = acc.next()
                for bi, gi in ((0, 1), (1, 2)):
                    osb = os_[bi]
                    for sub in range(4):
                        s.op("pe", lambda e, sub=sub, osb=osb: e.transpose(
                            out=pF[:, sub * 65:(sub + 1) * 65], in_=osb[0:65, sub * 128:(sub + 1) * 128], identity=identf[0:65, 0:65]),
                            [osb, identf], [pF])
                    puv = pF[:, 0:260].rearrange("p (s c) -> p s c", c=65)
                    r2 = rs2.next()
                    w = wg.next()
                    s.op("dve", lambda e, r2=r2, puv=puv: e.reciprocal(out=r2[:], in_=puv[:, :, 64:65]), [pF], [r2])
                    s.op("dve", lambda e, r2=r2, w=w, k=k, gi=gi, i=i: e.tensor_tensor(
                        out=w[:], in0=r2[:], in1=gates[:, 4 * i:4 * i + 4, k * 3 + gi:k * 3 + gi + 1], op=ALU.mult), [r2, gates], [w])
                    if bi == 0:
                        s.op("dve", lambda e, a=a, puv=puv, w=w: e.tensor_tensor(out=a[:], in0=puv[:, :, 0:64],
                                                                                 in1=w[:].to_broadcast([128, 4, 64]), op=ALU.mult), [pF, w], [a])
                    else:
                        t2 = t2b.next()
                        s.op("dve", lambda e, t2=t2, puv=puv, w=w: e.tensor_tensor(out=t2[:], in0=puv[:, :, 0:64],
                                                                                   in1=w[:].to_broadcast([128, 4, 64]), op=ALU.mult), [pF, w], [t2])
                        s.op("pool", lambda e, a=a, t2=t2: e.tensor_tensor(out=a[:], in0=a[:], in1=t2[:], op=ALU.add), [a, t2], [a])
                t3 = t2b.next()
                s.op("pool", lambda e, t3=t3, oc=oc, k=k, i=i: e.tensor_tensor(
                    out=t3[:], in0=oc[:, :, k, :], in1=gates[:, 4 * i:4 * i + 4, k * 3:k * 3 + 1].to_broadcast([128, 4, 64]), op=ALU.mult),
                    [oc, gates], [t3])
                s.op("pool", lambda e, t3=t3, a=a, oo=oo, k=k: e.tensor_tensor(out=oo[:, :, k * 64:(k + 1) * 64], in0=a[:], in1=t3[:],
                                                                              op=ALU.add), [a, t3], [oo])
                if k == 1:
                    s.dma("sp", ovn[i], oo[:, :, 0:128], reads=[oo], writes=[ob])
            stream.push_fin(fin_a, fin_b)
        while gen is not None:
            gen = advance(gen, 1000)
    stream.flush()

    if do_diff:
        dkT = kcd
        s.dma("sp", kcd[:], dr["dkT"], writes=[kcd])
        dvA = s.buf(ksd.t[:, :].rearrange("p (t c) -> p t c", c=128), "dvA_view")
        s.dma("sp", dvA[:], dr["dvA"].rearrange("(t p) c -> p t c", p=128), writes=[ksd])
        pOm = [pOa, pOb]
        pUm = [pU0, pU1]
        dqs = {}

        def load_dq(i):
            dqc = dq.next()
            for m in range(2):
                s.dma("sp", dqc[m][64 * m:64 * m + 64, :], dr["dqT"][64 * m:64 * m + 64, i * 512:(i + 1) * 512], writes=[dqc[m]])
            dqs[i] = dqc
        load_dq(0)
    for i in range(nch if do_diff else 0):
        oo = ost.next()
        nkt = 4 * i + 4
        dqc = dqs[i]
        if i + 1 < nch:
            load_dq(i + 1)
        for m in range(2):
            lo = 64 * m
            for kt in range(nkt):
                j = kt - 4 * i

                def qk(kt=kt, j=j, dqm=dqc[m]):
                    ps = pS.next()
                    s.op("pe", lambda e, ps=ps, kt=kt, dqm=dqm: e.matmul(
                        ps[:], lhsT=dkT[:, kt * 128:(kt + 1) * 128], rhs=dqm[:], start=True, stop=True),
                        [dkT, dqm], [ps])
                    if j >= -1:
                        y0 = 384 - 128 * j
                        return exp_tile(ps, j, (cstrip[2], cstrip[2][:, y0:y0 + 512]), None)
                    return exp_tile(ps, j, None, cb[:, 4:5])

                def pv(p, kt=kt, m=m, nkt=nkt):
                    s.op("pe", lambda e, p=p, kt=kt, m=m, nkt=nkt: e.matmul(pOm[m][:], lhsT=dvA[:, kt, :], rhs=p[:],
                                                                             start=(kt == 0), stop=(kt == nkt - 1)), [ksd, p], [pOm[m]])
                    s.op("pe", lambda e, p=p, kt=kt, m=m, nkt=nkt: e.matmul(
                        pL[:], lhsT=ones2[:, m, :], rhs=p[:], start=(m == 0 and kt == 0), stop=(m == 1 and kt == nkt - 1)),
                        [ones2, p], [pL])
                stream.push(qk, pv)

        def fin_a():
            od = [Od.next(), Od.next(), Od.next()]
            s.op("dve", lambda e, od=od: e.tensor_copy(out=od[0][:], in_=pOa[:]), [pOa], [od[0]])
            s.op("act", lambda e, od=od: e.copy(out=od[1][:], in_=pOb[:]), [pOb], [od[1]])
            s.op("act", lambda e, od=od: e.copy(out=od[2][:], in_=pL[:]), [pL], [od[2]])
            return od

        def fin_b(od, i=i, oo=oo):
            for m in range(2):
                for sub in range(4):
                    s.op("pe", lambda e, m=m, sub=sub, od=od: e.transpose(
                        out=pUm[m][:, sub * 128:(sub + 1) * 128], in_=od[m][:, sub * 128:(sub + 1) * 128], identity=identf[:]),
                        [od[m], identf], [pUm[m]])
            pq = pS.next()
            for sub in range(4):
                s.op("pe", lambda e, sub=sub, pq=pq, od=od: e.transpose(out=pq[:, sub * 128:(sub + 1) * 128], in_=od[2][:, sub * 128:(sub + 1) * 128],
                                                                        identity=identf[:]), [od[2], identf], [pq])
            r = rd.next()
            s.op("dve", lambda e, r=r, pq=pq: e.reciprocal(out=r[:], in_=pq[:].rearrange("p (s m c) -> p s m c", m=2, c=64)[:, :, :, 0]), [pq], [r])
            s.op("dve", lambda e, r=r: e.tensor_scalar(out=r[:, :, 1:2], in0=r[:, :, 1:2], scalar1=nlam[:, 0:1], scalar2=None,
                                                        op0=ALU.mult), [r, nlam], [r])
            o0b = o0.next()
            a = av.next()
            pv0 = pU0[:].rearrange("p (s c) -> p s c", c=128)
            pv1 = pU1[:].rearrange("p (s c) -> p s c", c=128)
            s.op("dve", lambda e, o0b=o0b, r=r, pv0=pv0: e.tensor_tensor(out=o0b[:], in0=pv0, in1=r[:, :, 0:1].to_broadcast([128, 4, 128]),
                                                                         op=ALU.mult), [pU0, r], [o0b])
            s.op("dve", lambda e, a=a, r=r, pv1=pv1: e.tensor_tensor(out=a[:], in0=pv1, in1=r[:, :, 1:2].to_broadcast([128, 4, 128]),
                                                                     op=ALU.mult), [pU1, r], [a])
            s.op("pool", lambda e, a=a, o0b=o0b: e.tensor_tensor(out=a[:], in0=a[:], in1=o0b[:], op=ALU.add), [a, o0b], [a])
            s.op("pool", lambda e, a=a: e.tensor_tensor(out=sq[:], in0=a[:], in1=a[:], op=ALU.mult), [a], [sq])
            sv = ssd.next()
            s.op("dve", lambda e, sv=sv: e.tensor_reduce(out=sv[:], in_=sq[:], axis=AX.X, op=ALU.add), [sq], [sv])
            s.op("dve", lambda e, sv=sv: e.tensor_scalar(out=sv[:], in0=sv[:], scalar1=1.0 / 128, scalar2=SUB_EPS, op0=ALU.mult,
                                                         op1=ALU.add), [sv], [sv])
            s.op("act", lambda e, sv=sv: e.activation(out=sv[:], in_=sv[:], func=AF.Sqrt), [sv], [sv])
            s.op("dve", lambda e, sv=sv: e.reciprocal(out=sv[:], in_=sv[:]), [sv], [sv])
            s.op("dve", lambda e, a=a, sv=sv: e.tensor_tensor(out=a[:], in0=a[:], in1=sv[:].unsqueeze(2).to_broadcast([128, 4, 128]),
                                                              op=ALU.mult), [a, sv], [a])
            s.op("dve", lambda e, a=a, oo=oo: e.tensor_tensor(out=oo[:, :, 128:256], in0=a[:],
                                                              in1=gsub[:].unsqueeze(1).to_broadcast([128, 4, 128]), op=ALU.mult),
                 [a, gsub], [oo])
            s.dma("sp", ovd[i], oo[:, :, 128:256], reads=[oo], writes=[ob])
        stream.push_fin(fin_a, fin_b)
    stream.flush()
    s.finish_outputs([ob])


_CACHE = {}


def _prog(key, fn):
    if key not in _CACHE:
        _CACHE[key] = fn()
    return _CACHE[key]


def kernel(**inputs):
    inputs = {k: np.asarray(v) for k, v in inputs.items()}
    x = np.ascontiguousarray(inputs["x"], dtype=np.float32).reshape(8, NT, D)
    bt = inputs["bias_table"]
    identb = np.eye(128, dtype=np.float32).astype(NPBF)
    cores = list(range(8))
    nc1 = _prog("l1", lambda: build_token_phase(False, False, 2840, False, False, gate_rows=(1280, 1304)))
    r1 = run_bass_kernel_spmd(nc1, [{"x": x[c], "ident": identb, "g_mix": inputs["norm_mix"][0:1],
                                     "w_in": inputs["ev_w_in"][0]} for c in cores], core_ids=cores).results
    projT = [np.concatenate([r1[b * 4 + q]["projT"] for q in range(4)], axis=1) for b in range(2)]
    gT = [np.concatenate([r1[b * 4 + q]["gT"] for q in range(4)], axis=1) for b in range(2)]
    nc2 = _prog("l2", lambda: build_l2(16, True, True))
    in2 = []
    for c in cores:
        b, g, half, hd = c // 4, (c % 4) // 2, c % 2, c % 4
        in2.append(l2_host_inputs(projT[b], gT[b], inputs, g, half, hd))
    r2 = run_bass_kernel_spmd(nc2, in2, core_ids=cores).results
    o0 = np.zeros((2, S, D), dtype=NPBF)
    for c in cores:
        b, g, half, hd = c // 4, (c % 4) // 2, c % 2, c % 4
        oc = r2[c]["o"]
        for k in range(2):
            h = g * 4 + 2 * half + k
            o0[b, :, h * 64:(h + 1) * 64] = oc[:, k * 64:(k + 1) * 64]
        o0[b, :, 512 + hd * 128:512 + (hd + 1) * 128] = oc[:, 128:256]
    o0 = o0.reshape(8, NT, D)
    nc3 = _prog("l3", lambda: build_token_phase(True, True, 3072, False, True))
    r3 = run_bass_kernel_spmd(nc3, [{"x": x[c], "ident": identb, "oT": np.ascontiguousarray(o0[c].T),
                                     "w_out": inputs["ev_w_out"][0], "g_mlp": inputs["norm_mlp"][0:1],
                                     "w1": inputs["mlp_w1"][0], "w2": inputs["mlp_w2"][0],
                                     "g_mix": inputs["norm_mix"][1:2], "w_in": inputs["od_w_in"][0]}
                                    for c in cores], core_ids=cores).results
    x1 = [r3[c]["x_o"] for c in cores]
    proj1T = [np.concatenate([r3[b * 4 + q]["projT"] for q in range(4)], axis=1) for b in range(2)]
    nc4 = _prog("l4", lambda: build_moba(16, 4))
    in4 = [moba_host_inputs(proj1T[c // 4], bt, [4 * (c % 4) + k for k in range(4)]) for c in cores]
    r4 = run_bass_kernel_spmd(nc4, in4, core_ids=cores).results
    o1 = np.zeros((2, S, D), dtype=NPBF)
    for c in cores:
        o1[c // 4, :, (c % 4) * 256:(c % 4 + 1) * 256] = r4[c]["o"]
    o1 = o1.reshape(8, NT, D)
    nc5 = _prog("l5", lambda: build_token_phase(True, True, 0, True, False))
    r5 = run_bass_kernel_spmd(nc5, [{"x": x1[c], "ident": identb, "oT": np.ascontiguousarray(o1[c].T),
                                     "w_out": inputs["od_w_out"][0], "g_mlp": inputs["norm_mlp"][1:2],
                                     "w1": inputs["mlp_w1"][1], "w2": inputs["mlp_w2"][1],
                                     "g_fin": inputs["norm_final"].reshape(1, D)} for c in cores], core_ids=cores).results
    out = np.stack([r5[c]["out"] for c in cores]).reshape(2, S, D).astype(np.float32)
    return out
```
